# Optimizing a Trainium2 kernel written in Bass

```python
import math
import jax, jax.numpy as jnp
from jax import lax
import numpy as np

D_MODEL = 1024
BATCH = 16
SEQ = 2048
DEPTH = 2

MEM_LEN = 256
D_MIX = D_MODEL
D_SSM = (3 * D_MIX) // 8
D_POOL = D_MIX // 4
D_CONV = D_MIX - D_SSM - D_POOL
SSM_GROUP = 16
N_SSM_GROUPS = D_SSM // SSM_GROUP
SSM_STATE = 64
POOL_WINDOWS = (2, 4, 8, 16)
N_POOL_GROUPS = len(POOL_WINDOWS)
POOL_GROUP = D_POOL // N_POOL_GROUPS
CONV_WIDTH = 31
D_IN = D_SSM + D_POOL + 2 * D_CONV
D_FF = 2816
N_XHEADS = 4
XHEAD_DIM = D_MODEL // N_XHEADS
EPS = 1e-6
DT_MIN = 1e-3
DT_MAX = 1e-1

kernel_name = "hybrid_s5_pool_conv_macaron_xattn"


def rmsnorm(x, g):
    xf = x.astype(jnp.float32)
    y = xf * lax.rsqrt(jnp.mean(xf * xf, axis=-1, keepdims=True) + EPS)
    return (y * g.astype(jnp.float32)).astype(x.dtype)


def swiglu_ffn(h, w_gate, w_up, w_down):
    return (jax.nn.silu(h @ w_gate) * (h @ w_up)) @ w_down


def _complex_linear_combine(c1, c2):
    ar1, ai1, br1, bi1 = c1
    ar2, ai2, br2, bi2 = c2
    ar = ar2 * ar1 - ai2 * ai1
    ai = ar2 * ai1 + ai2 * ar1
    br = ar2 * br1 - ai2 * bi1 + br2
    bi = ar2 * bi1 + ai2 * br1 + bi2
    return (ar, ai, br, bi)


def s5_mixer(u, lam_re, lam_im, log_dt, b_re, b_im, c_re, c_im, d, w_glu):
    bsz, s, _ = u.shape
    f32 = jnp.float32
    uf = u.astype(f32)
    ug = uf.reshape(bsz, s, N_SSM_GROUPS, SSM_GROUP)
    lr = lam_re.astype(f32)
    li = lam_im.astype(f32)
    dt = jnp.exp(log_dt.astype(f32))[:, None]
    mag = jnp.exp(lr * dt)
    ar = mag * jnp.cos(li * dt)
    ai = mag * jnp.sin(li * dt)
    den = lr * lr + li * li
    zr = ((ar - 1.0) * lr + ai * li) / den
    zi = (ai * lr - (ar - 1.0) * li) / den
    br_ = b_re.astype(f32)
    bi_ = b_im.astype(f32)
    bbar_r = zr[..., None] * br_ - zi[..., None] * bi_
    bbar_i = zr[..., None] * bi_ + zi[..., None] * br_
    bu_r = jnp.einsum('gpk,bsgk->bsgp', bbar_r, ug)
    bu_i = jnp.einsum('gpk,bsgk->bsgp', bbar_i, ug)
    a_r = jnp.broadcast_to(ar[None, None], (1, s, N_SSM_GROUPS, SSM_STATE))
    a_i = jnp.broadcast_to(ai[None, None], (1, s, N_SSM_GROUPS, SSM_STATE))
    _, _, x_r, x_i = lax.associative_scan(_complex_linear_combine, (a_r, a_i, bu_r, bu_i), axis=1)
    y = (jnp.einsum('gkp,bsgp->bsgk', c_re.astype(f32), x_r)
         - jnp.einsum('gkp,bsgp->bsgk', c_im.astype(f32), x_i))
    y = y.reshape(bsz, s, D_SSM) + d.astype(f32) * uf
    y = jax.nn.gelu(y)
    out = y * jax.nn.sigmoid(y @ w_glu.astype(f32))
    return out.astype(u.dtype)


def pool_mixer(u, w_pool, pool_scale):
    bsz, s, _ = u.shape
    uf = u.astype(jnp.float32)
    cs = jnp.cumsum(uf, axis=1)
    pos = jnp.arange(1, s + 1, dtype=jnp.float32)[None, :, None]
    outs = []
    for gi, w in enumerate(POOL_WINDOWS):
        c = cs[..., gi * POOL_GROUP:(gi + 1) * POOL_GROUP]
        prev = jnp.pad(c[:, :-w], ((0, 0), (w, 0), (0, 0)))
        mean = (c - prev) / jnp.minimum(pos, float(w))
        outs.append(mean - uf[..., gi * POOL_GROUP:(gi + 1) * POOL_GROUP])
    p = jnp.stack(outs, axis=2)
    p = jnp.einsum('bsgc,gcd->bsgd', p, w_pool.astype(jnp.float32)).reshape(bsz, s, D_POOL)
    return (p * pool_scale.astype(jnp.float32)).astype(u.dtype)


def conv_module(v, g, conv_w, conv_b, ln_g, ln_b):
    h = v * jax.nn.sigmoid(g)
    h = lax.conv_general_dilated(
        h, conv_w[:, None, :].astype(h.dtype), window_strides=(1,),
        padding=[(CONV_WIDTH - 1, 0)], dimension_numbers=('NWC', 'WIO', 'NWC'),
        feature_group_count=D_CONV) + conv_b
    hf = h.astype(jnp.float32)
    mu = jnp.mean(hf, axis=-1, keepdims=True)
    var = jnp.mean(jnp.square(hf - mu), axis=-1, keepdims=True)
    hf = (hf - mu) * lax.rsqrt(var + EPS) * ln_g.astype(jnp.float32) + ln_b.astype(jnp.float32)
    return jax.nn.silu(hf).astype(v.dtype)


def cross_attention(h, m, wq, wk, wv, wo):
    bsz, s, _ = h.shape
    mlen = m.shape[1]
    q = (h @ wq).reshape(bsz, s, N_XHEADS, XHEAD_DIM)
    k = (m @ wk).reshape(bsz, mlen, N_XHEADS, XHEAD_DIM)
    v = (m @ wv).reshape(bsz, mlen, N_XHEADS, XHEAD_DIM)
    scores = jnp.einsum('bshd,bmhd->bhsm', q, k).astype(jnp.float32) * (XHEAD_DIM ** -0.5)
    probs = jax.nn.softmax(scores, axis=-1).astype(v.dtype)
    o = jnp.einsum('bhsm,bmhd->bshd', probs, v).reshape(bsz, s, D_MODEL)
    return o @ wo


def setup_inputs(seed: int = 0) -> dict:
    key = jax.random.key(seed)
    ks = iter(jax.random.split(key, 48))
    L = DEPTH

    def nrm(shape, scale):
        return jax.random.normal(next(ks), shape, jnp.float32) * scale

    def gain(shape):
        return 1.0 + 0.05 * jax.random.normal(next(ks), shape, jnp.float32)

    inp = {}
    inp["x"] = nrm((BATCH, SEQ, D_MODEL), 1.0)
    inp["mem"] = nrm((BATCH, MEM_LEN, D_MODEL), 1.0)
    inp["ffn1_norm"] = gain((L, D_MODEL))
    inp["ffn1_w_gate"] = nrm((L, D_MODEL, D_FF), D_MODEL ** -0.5)
    inp["ffn1_w_up"] = nrm((L, D_MODEL, D_FF), D_MODEL ** -0.5)
    inp["ffn1_w_down"] = nrm((L, D_FF, D_MODEL), D_FF ** -0.5)
    inp["mix_norm"] = gain((L, D_MODEL))
    inp["w_in"] = nrm((L, D_MODEL, D_IN), D_MODEL ** -0.5)
    inp["w_out"] = nrm((L, D_MIX, D_MODEL), D_MIX ** -0.5)
    inp["ssm_lambda_re"] = -0.5 + 0.01 * jax.random.normal(next(ks), (L, N_SSM_GROUPS, SSM_STATE), jnp.float32)
    inp["ssm_lambda_im"] = jnp.broadcast_to(
        jnp.pi * jnp.arange(SSM_STATE, dtype=jnp.float32), (L, N_SSM_GROUPS, SSM_STATE))
    inp["ssm_log_dt"] = jax.random.uniform(next(ks), (L, N_SSM_GROUPS), jnp.float32,
                                           math.log(DT_MIN), math.log(DT_MAX))
    bscale = (2.0 * SSM_GROUP) ** -0.5
    cscale = (2.0 * SSM_STATE) ** -0.5
    inp["ssm_b_re"] = nrm((L, N_SSM_GROUPS, SSM_STATE, SSM_GROUP), bscale)
    inp["ssm_b_im"] = nrm((L, N_SSM_GROUPS, SSM_STATE, SSM_GROUP), bscale)
    inp["ssm_c_re"] = nrm((L, N_SSM_GROUPS, SSM_GROUP, SSM_STATE), cscale)
    inp["ssm_c_im"] = nrm((L, N_SSM_GROUPS, SSM_GROUP, SSM_STATE), cscale)
    inp["ssm_d"] = nrm((L, D_SSM), 1.0)
    inp["ssm_w_glu"] = nrm((L, D_SSM, D_SSM), D_SSM ** -0.5)
    inp["pool_w"] = nrm((L, N_POOL_GROUPS, POOL_GROUP, POOL_GROUP), POOL_GROUP ** -0.5)
    inp["pool_scale"] = gain((L, D_POOL))
    inp["conv_w"] = nrm((L, CONV_WIDTH, D_CONV), CONV_WIDTH ** -0.5)
    inp["conv_b"] = nrm((L, D_CONV), 0.02)
    inp["conv_ln_g"] = gain((L, D_CONV))
    inp["conv_ln_b"] = nrm((L, D_CONV), 0.02)
    inp["xattn_norm"] = gain((L, D_MODEL))
    inp["mem_norm"] = gain((L, D_MODEL))
    inp["xattn_wq"] = nrm((L, D_MODEL, D_MODEL), D_MODEL ** -0.5)
    inp["xattn_wk"] = nrm((L, D_MODEL, D_MODEL), D_MODEL ** -0.5)
    inp["xattn_wv"] = nrm((L, D_MODEL, D_MODEL), D_MODEL ** -0.5)
    inp["xattn_wo"] = nrm((L, D_MODEL, D_MODEL), D_MODEL ** -0.5)
    inp["ffn2_norm"] = gain((L, D_MODEL))
    inp["ffn2_w_gate"] = nrm((L, D_MODEL, D_FF), D_MODEL ** -0.5)
    inp["ffn2_w_up"] = nrm((L, D_MODEL, D_FF), D_MODEL ** -0.5)
    inp["ffn2_w_down"] = nrm((L, D_FF, D_MODEL), D_FF ** -0.5)
    inp["final_norm"] = gain((D_MODEL,))
    return inp


def reference(x, mem, ffn1_norm, ffn1_w_gate, ffn1_w_up, ffn1_w_down, mix_norm, w_in, w_out,
              ssm_lambda_re, ssm_lambda_im, ssm_log_dt, ssm_b_re, ssm_b_im, ssm_c_re, ssm_c_im,
              ssm_d, ssm_w_glu, pool_w, pool_scale, conv_w, conv_b, conv_ln_g, conv_ln_b,
              xattn_norm, mem_norm, xattn_wq, xattn_wk, xattn_wv, xattn_wo,
              ffn2_norm, ffn2_w_gate, ffn2_w_up, ffn2_w_down, final_norm):
    split_pts = [D_SSM, D_SSM + D_POOL, D_SSM + D_POOL + D_CONV]
    for l in range(DEPTH):
        h = rmsnorm(x, ffn1_norm[l])
        x = x + 0.5 * swiglu_ffn(h, ffn1_w_gate[l], ffn1_w_up[l], ffn1_w_down[l])
        h = rmsnorm(x, mix_norm[l])
        z = h @ w_in[l]
        u_ssm, u_pool, v_conv, g_conv = jnp.split(z, split_pts, axis=-1)
        y_ssm = s5_mixer(u_ssm, ssm_lambda_re[l], ssm_lambda_im[l], ssm_log_dt[l],
                         ssm_b_re[l], ssm_b_im[l], ssm_c_re[l], ssm_c_im[l],
                         ssm_d[l], ssm_w_glu[l])
        y_pool = pool_mixer(u_pool, pool_w[l], pool_scale[l])
        y_conv = conv_module(v_conv, g_conv, conv_w[l], conv_b[l], conv_ln_g[l], conv_ln_b[l])
        y = jnp.concatenate([y_ssm, y_pool, y_conv], axis=-1)
        x = x + y @ w_out[l]
        h = rmsnorm(x, xattn_norm[l])
        m = rmsnorm(mem, mem_norm[l])
        x = x + cross_attention(h, m, xattn_wq[l], xattn_wk[l], xattn_wv[l], xattn_wo[l])
        h = rmsnorm(x, ffn2_norm[l])
        x = x + 0.5 * swiglu_ffn(h, ffn2_w_gate[l], ffn2_w_up[l], ffn2_w_down[l])
    return rmsnorm(x, final_norm)
```

```python
import numpy as np
import concourse.bass as bass
import concourse.mybir as mybir
from concourse.bass_utils import run_bass_kernel_spmd

F32 = mybir.dt.float32
BF16 = mybir.dt.bfloat16
AF = mybir.ActivationFunctionType
ALU = mybir.AluOpType

DEPTH = 2
D = 1024
S = 2048
FF = 2816
MEM = 256
D_SSM, D_POOL, D_CONV = 384, 256, 384
D_IN = 1408
CONVW = 31
EPS = 1e-6
KT = 8
TT = 512
NTT = S // TT
FR = 128
NGP = 12
MAGIC = 12582912.0
ENGS = ['pe', 'act', 'dve', 'pool', 'sp']

PARAM_NAMES = ["ffn1_norm", "ffn1_w_gate", "ffn1_w_up", "ffn1_w_down", "mix_norm", "w_in", "w_out",
               "ssm_lambda_re", "ssm_lambda_im", "ssm_log_dt", "ssm_b_re", "ssm_b_im", "ssm_c_re",
               "ssm_c_im", "ssm_d", "ssm_w_glu", "pool_w", "pool_scale", "conv_w", "conv_b",
               "conv_ln_g", "conv_ln_b", "xattn_norm", "mem_norm", "xattn_wq", "xattn_wk",
               "xattn_wv", "xattn_wo", "ffn2_norm", "ffn2_w_gate", "ffn2_w_up", "ffn2_w_down",
               "final_norm"]


class Op:
    __slots__ = ('id', 'eng', 'fn', 'deps', 'pos', 'chan', 'chan_val', 'sig', 'sig_idx', 'waits', 'bsnap')


class _Rec:
    def __getattr__(self, name):
        return lambda *a, **k: (name, a, k)


e_ = _Rec()


class Prog:
    def __init__(self):
        self.ops = []
        self.last_write = {}
        self.readers = {}
        self.pos = {e: 0 for e in ENGS}
        self.chan_cnt = {}
        self.chan_kind = {}
        self.last_op = {e: None for e in ENGS}

    def add(self, eng, fn, reads=(), writes=(), chan=None, kind='serial', extra_deps=()):
        o = Op()
        o.id = len(self.ops)
        o.eng = eng
        o.fn = fn
        o.pos = self.pos[eng]
        self.pos[eng] += 1
        o.chan = chan
        o.sig = False
        o.sig_idx = 0
        o.chan_val = 0
        if chan is not None:
            self.chan_kind.setdefault(chan, kind)
            self.chan_cnt[chan] = self.chan_cnt.get(chan, 0) + 16
            o.chan_val = self.chan_cnt[chan]
        deps = set(extra_deps)
        for r in reads:
            w = self.last_write.get(r)
            if w is not None:
                deps.add(w)
        for w in writes:
            lw = self.last_write.get(w)
            if lw is not None:
                deps.add(lw)
            rd = self.readers.get(w)
            if rd:
                deps.update(rd.values())
        for w in writes:
            self.last_write[w] = o.id
            self.readers[w] = {}
        for r in reads:
            self.readers.setdefault(r, {})[eng] = o.id
        deps.discard(o.id)
        o.deps = deps
        o.bsnap = None
        for d in deps:
            c = self.ops[d].chan
            if c is not None and self.chan_kind[c] == 'batch':
                if o.bsnap is None:
                    o.bsnap = {}
                o.bsnap[c] = self.chan_cnt[c] - (16 if c == chan else 0)
        self.ops.append(o)
        if fn is not None:
            self.last_op[eng] = o.id
        return o.id

    def chan_sync(self, chan):
        o_id = self.add('sp', None)
        self.ops[o_id].bsnap = {('force', chan): self.chan_cnt.get(chan, 0)}

    def fence(self):
        last = [v for v in self.last_op.values() if v is not None]
        for e in ENGS:
            self.add(e, None, extra_deps=last)

    def finalize(self):
        ops = self.ops
        for o in ops:
            w = {}
            for d in o.deps:
                do = ops[d]
                if do.chan is not None:
                    key = ('c', do.chan)
                    val = do.chan_val if self.chan_kind[do.chan] == 'serial' else o.bsnap[do.chan]
                    w[key] = max(w.get(key, 0), val)
                    continue
                if do.eng == o.eng:
                    if o.eng in ('pe', 'sp'):
                        continue
                key = ('e', do.eng)
                prev = w.get(key)
                if prev is None or ops[prev].pos < do.pos:
                    w[key] = d
            if o.bsnap:
                for bk, bv in o.bsnap.items():
                    if isinstance(bk, tuple) and bk[0] == 'force' and bv > 0:
                        w[('c', bk[1])] = max(w.get(('c', bk[1]), 0), bv)
            o.waits = w
            for key, v in w.items():
                if key[0] == 'e':
                    ops[v].sig = True
        cnt = {e: 0 for e in ENGS}
        for o in ops:
            if o.sig:
                cnt[o.eng] += 1
                o.sig_idx = cnt[o.eng]

    def emit(self, nc, engines, esem, csem):
        ops = self.ops
        for e in ENGS:
            lst = [o for o in ops if o.eng == e]

            def body(eng, lst=lst, e=e):
                waited = {}
                for o in lst:
                    for key, v in o.waits.items():
                        if key[0] == 'c':
                            sem, val = csem[key[1]], v
                        else:
                            sem, val = esem[key[1]], ops[v].sig_idx
                        if waited.get(key, 0) >= val:
                            continue
                        waited[key] = val
                        eng.wait_ge(sem, val)
                    if o.fn is None:
                        assert not o.sig
                        continue
                    name, a, k = o.fn
                    inst = getattr(eng, name)(*a, **k)
                    if o.chan is not None:
                        inst.then_inc(csem[o.chan], 16)
                    elif o.sig:
                        inst.then_inc(esem[e], 1)
            engines[e](body)


def build_nc(n_seq=2, n_layers=DEPTH, stages=('ffn1', 'mix', 'xattn', 'ffn2')):
    nc = bass.Bass("TRN2", target_bir_lowering=False)
    dr = {}
    dr['x'] = nc.dram_tensor("x", [n_seq, S, D], F32, kind="ExternalInput").ap()
    dr['mem'] = nc.dram_tensor("mem", [n_seq, MEM, D], F32, kind="ExternalInput").ap()
    shapes = dict(
        ffn1_norm=[DEPTH, D], ffn1_w_gate=[DEPTH, D, FF], ffn1_w_up=[DEPTH, D, FF], ffn1_w_down=[DEPTH, FF, D],
        mix_norm=[DEPTH, D], w_in=[DEPTH, D, D_IN], w_out=[DEPTH, D, D],
        ssm_lambda_re=[DEPTH, 24, 64], ssm_lambda_im=[DEPTH, 24, 64], ssm_log_dt=[DEPTH, 24],
        ssm_b_re=[DEPTH, 24, 64, 16], ssm_b_im=[DEPTH, 24, 64, 16], ssm_c_re=[DEPTH, 24, 16, 64],
        ssm_c_im=[DEPTH, 24, 16, 64], ssm_d=[DEPTH, 384], ssm_w_glu=[DEPTH, 384, 384],
        pool_w=[DEPTH, 4, 64, 64], pool_scale=[DEPTH, 256], conv_w=[DEPTH, 31, 384], conv_b=[DEPTH, 384],
        conv_ln_g=[DEPTH, 384], conv_ln_b=[DEPTH, 384], xattn_norm=[DEPTH, D], mem_norm=[DEPTH, D],
        xattn_wq=[DEPTH, D, D], xattn_wk=[DEPTH, D, D], xattn_wv=[DEPTH, D, D], xattn_wo=[DEPTH, D, D],
        ffn2_norm=[DEPTH, D], ffn2_w_gate=[DEPTH, D, FF], ffn2_w_up=[DEPTH, D, FF], ffn2_w_down=[DEPTH, FF, D],
        final_norm=[D])
    for n in PARAM_NAMES:
        dr[n] = nc.dram_tensor(n, shapes[n], F32, kind="ExternalInput").ap()
    dr['consts'] = nc.dram_tensor("consts", [128, 512], F32, kind="ExternalInput").ap()
    out = nc.dram_tensor("out", [n_seq, S, D], F32, kind="ExternalOutput").ap()

    P = Prog()
    NW = 53000
    import contextlib
    with contextlib.ExitStack() as es:
        arena = es.enter_context(nc.sbuf_tensor("arena", [128, NW], F32))
        banks = [es.enter_context(nc.psum_tensor("ps%d" % i, [128, 512], F32)) for i in range(8)]
        esem = {e: es.enter_context(nc.semaphore("sem_" + e)) for e in ENGS}
        chans = ['st0', 'st1', 'st2', 'st3', 'par', 'cst', 'out', 'lp0', 'lp1', 'lp2', 'lp3']
        csem = {c: es.enter_context(nc.semaphore("ch_" + c)) for c in chans}

        cur = [0]

        def carve(nelem, dt=F32):
            words = nelem if dt == F32 else (nelem + 1) // 2
            words = (words + 7) // 8 * 8
            off = cur[0]
            cur[0] += words
            assert cur[0] <= NW, "arena overflow %d" % cur[0]
            a = arena[:, off:off + words]
            if dt != F32:
                a = a.bitcast(dt)[:, 0:nelem]
            else:
                a = a[:, 0:nelem]
            return a

        xres = carve(KT * S).rearrange("p (k t) -> p k t", k=KT)
        h_off = cur[0]
        hT = carve(KT * S, BF16).rearrange("p (k t) -> p k t", k=KT)
        h_end = cur[0]
        stage = [carve(1024) for _ in range(4)]
        cst = carve(512)
        ident = cst[:, 0:128]
        ones_f = cst[:, 128:256]
        iota1 = cst[:, 256:384]
        m0 = cst[:, 384:385]
        m1 = cst[:, 385:386]
        epsc = cst[:, 386:387]
        ident_b = carve(128, BF16)
        ones_b = carve(128, BF16)
        pvec = carve(128)
        D_off = cur[0]

        psn = [0]

        ps_pools = {'all': list(range(8)), 'A_by': [0, 1], 'A_b': [2, 3, 4, 5], 'B': [6, 7]}
        ps_cnt = {k: 0 for k in ps_pools}
        ps_cur = ['all']

        def psum():
            pl = ps_pools[ps_cur[0]]
            b = pl[ps_cnt[ps_cur[0]] % len(pl)]
            ps_cnt[ps_cur[0]] += 1
            psn[0] += 1
            return b

        def PS(b):
            return ('ps', b)

        stn = [0]

        def stage_slot():
            s_ = stn[0] % 4
            stn[0] += 1
            return s_

        P.add('sp', e_.dma_start(out=cst, in_=dr['consts']), writes=['cst'], chan='cst', kind='batch')
        P.add('dve', e_.tensor_copy(out=ident_b, in_=ident), reads=['cst'], writes=['identb'])
        P.add('dve', e_.tensor_copy(out=ones_b, in_=ones_f), reads=['cst'], writes=['onesb'])

        prow = {}
        rows = []
        r = 0
        for n in ["ffn1_norm", "mix_norm", "xattn_norm", "mem_norm", "ffn2_norm"]:
            prow[n] = r
            rows.append((n, r, dr[n].rearrange("l (k p) -> (l k) p", p=128), DEPTH * 8))
            r += DEPTH * 8
        prow["final_norm"] = r
        rows.append(("final_norm", r, dr["final_norm"].rearrange("(k p) -> k p", p=128), 8))
        r += 8
        for n, w in [("ssm_d", 3), ("pool_scale", 2), ("conv_b", 3), ("conv_ln_g", 3), ("conv_ln_b", 3)]:
            prow[n] = r
            rows.append((n, r, dr[n].rearrange("l (k p) -> (l k) p", p=128), DEPTH * w))
            r += DEPTH * w
        NPROW = r
        assert NPROW <= 128
        st_par = stage[3]
        for (n, r0, src, cnt) in rows:
            P.add('sp', e_.dma_start(out=st_par[r0:r0 + cnt, 0:128], in_=src),
                  writes=[('st', 3)], chan='par', kind='batch')
        bpar = psum()
        P.add('pe', e_.transpose(out=banks[bpar][:, 0:NPROW], in_=st_par[0:NPROW, 0:128],
                                          identity=ident[0:NPROW, 0:NPROW]),
              reads=[('st', 3), 'cst'], writes=[PS(bpar)])
        P.add('dve', e_.tensor_copy(out=pvec[:, 0:NPROW], in_=banks[bpar][:, 0:NPROW]),
              reads=[PS(bpar)], writes=['pvec'])

        def pcol(name, l, k, width):
            c = prow[name] + l * width + k
            return pvec[:, c:c + 1]

        def load_piece(src_ap, dst_view_fn, cast_out, cast_keys_w, cast_keys_r=()):
            s_ = stage_slot()
            dst = dst_view_fn(stage[s_])
            P.add('sp', e_.dma_start(out=dst, in_=src_ap), writes=[('st', s_)], chan='st%d' % s_)
            if s_ % 2 == 0:
                P.add('pool', e_.tensor_copy(out=cast_out, in_=dst), reads=[('st', s_)] + list(cast_keys_r),
                      writes=list(cast_keys_w))
            else:
                P.add('act', e_.copy(out=cast_out, in_=dst), reads=[('st', s_)] + list(cast_keys_r),
                      writes=list(cast_keys_w))

        def rmsnorm_tile(tt, gname, l, out_fn, out_keys, sqbuf, src=None, src_keys=None, ncols=TT, gw=8):
            if src is None:
                src = lambda k: xres[:, k, tt * TT:(tt + 1) * TT]
                src_keys = lambda k: ('x', k, tt)
            b = psum()
            for k in range(KT):
                sq = sqbuf[k % len(sqbuf)]
                P.add('act', e_.activation(out=sq[:, 0:ncols], in_=src(k), func=AF.Square),
                      reads=[src_keys(k)], writes=[('sq', id(sqbuf), k % len(sqbuf))])
                P.add('pe', e_.matmul(banks[b][:, 0:ncols], lhsT=ones_b, rhs=sq[:, 0:ncols],
                                                           start=(k == 0), stop=(k == KT - 1)),
                      reads=[('sq', id(sqbuf), k % len(sqbuf)), 'onesb'], writes=[PS(b)])
            P.add('act', e_.activation(out=banks[b][:, 0:ncols], in_=banks[b][:, 0:ncols], func=AF.Sqrt,
                                                bias=epsc, scale=1.0 / D),
                  reads=[PS(b), 'cst'], writes=[PS(b)])
            P.add('dve', e_.reciprocal(out=banks[b][:, 0:ncols], in_=banks[b][:, 0:ncols]),
                  reads=[PS(b)], writes=[PS(b)])
            for k in range(KT):
                P.add('dve', e_.scalar_tensor_tensor(out=out_fn(k), in0=src(k), scalar=pcol(gname, l, k, gw),
                                                                   in1=banks[b][:, 0:ncols], op0=ALU.mult, op1=ALU.mult),
                      reads=[src_keys(k), PS(b), 'pvec'], writes=[out_keys(k)])

        def residual_add(b, k, tt, scale):
            xs = xres[:, k, tt * TT:(tt + 1) * TT]
            P.add('dve', e_.scalar_tensor_tensor(out=xs, in0=banks[b][:, :], scalar=scale, in1=xs,
                                                          op0=ALU.mult, op1=ALU.add),
                  reads=[PS(b), ('x', k, tt)], writes=[('x', k, tt)])

        def proj_g(Wv, n_out_tiles, wp, rhs_fn, rhs_keys, ncols, handler, tag):
            def load(o):
                buf = o % 2
                load_piece(Wv[:, :, o * 128:(o + 1) * 128],
                           lambda st: st.rearrange("p (k n) -> p k n", k=KT),
                           wp[buf], [('wp', tag, buf)])
            load(0)
            for o in range(n_out_tiles):
                if o + 1 < n_out_tiles:
                    load(o + 1)
                buf = o % 2
                b = psum()
                for k in range(KT):
                    P.add('pe', e_.matmul(banks[b][:, 0:ncols], lhsT=wp[buf][:, k, :],
                                          rhs=rhs_fn(k), start=(k == 0), stop=(k == KT - 1)),
                          reads=[('wp', tag, buf), rhs_keys(k)], writes=[PS(b)])
                handler(o, b)
                yield

        def proj(Wv, n_out_tiles, wp, rhs_fn, rhs_keys, ncols, handler, tag):
            for _ in proj_g(Wv, n_out_tiles, wp, rhs_fn, rhs_keys, ncols, handler, tag):
                pass

        def proj_res(Wb, wkey, n_out_tiles, rhs_fn, rhs_keys, ncols, handler):
            for o in range(n_out_tiles):
                b = psum()
                for k in range(KT):
                    P.add('pe', e_.matmul(banks[b][:, 0:ncols], lhsT=Wb[:, k, o * 128:(o + 1) * 128],
                                          rhs=rhs_fn(k), start=(k == 0), stop=(k == KT - 1)),
                          reads=[(wkey, k), rhs_keys(k)], writes=[PS(b)])
                handler(o, b)

        def ffn(l, which):
            gname = which + "_norm"
            wg = dr[which + "_w_gate"][l].rearrange("(k p) f -> p k f", p=128)
            wu = dr[which + "_w_up"][l].rearrange("(k p) f -> p k f", p=128)
            wd = dr[which + "_w_down"][l].rearrange("(f p) d -> p f d", p=128)
            cur[0] = D_off
            Wg = [carve(KT * 256, BF16).rearrange("p (k n) -> p k n", k=KT) for _ in range(2)]
            Wu = [carve(KT * 256, BF16).rearrange("p (k n) -> p k n", k=KT) for _ in range(2)]
            Wd = [carve(2 * D, BF16).rearrange("p (f n) -> p f n", f=2) for _ in range(2)]
            a_bf = [[carve(TT, BF16) for _ in range(2)] for _ in range(2)]
            sg = [carve(TT) for _ in range(2)]
            sqb = [carve(TT, BF16) for _ in range(4)]
            for tt in range(NTT):
                rmsnorm_tile(tt, gname, l, lambda k, tt=tt: hT[:, k, tt * TT:(tt + 1) * TT],
                             lambda k, tt=tt: ('h', k, tt), sqb)
            NCH = FF // 256

            def load_chunk(c):
                buf = c % 2
                for half in range(2):
                    for (W, Wb, nm) in ((wg, Wg, 'wg'), (wu, Wu, 'wu')):
                        load_piece(W[:, half * 4:(half + 1) * 4, c * 256:(c + 1) * 256],
                                   lambda st: st.rearrange("p (k n) -> p k n", k=4),
                                   Wb[buf][:, half * 4:(half + 1) * 4, :], [(nm, buf, half)])
                for fl in range(2):
                    load_piece(wd[:, c * 2 + fl, :], lambda st: st, Wd[buf][:, fl, :], [('wd', buf, fl)])

            pend = [None]
            un = [0]

            def down(c, tt, ab):
                buf = c % 2
                for dm in range(KT):
                    b = psum()
                    for fl in range(2):
                        P.add('pe', e_.matmul(banks[b][:, :], lhsT=Wd[buf][:, fl, dm * 128:(dm + 1) * 128],
                                                                          rhs=a_bf[ab][fl], start=(fl == 0), stop=(fl == 1)),
                              reads=[('wd', buf, fl), ('a', ab, fl)], writes=[PS(b)])
                    residual_add(b, dm, tt, 0.5)

            load_chunk(0)
            for c in range(NCH):
                buf = c % 2
                for tt in range(NTT):
                    ab = un[0] % 2
                    un[0] += 1
                    for fl in range(2):
                        bg = psum()
                        bu = psum()
                        for (Wb, nm, b) in ((Wg, 'wg', bg), (Wu, 'wu', bu)):
                            for k in range(KT):
                                P.add('pe', e_.matmul(
                                    banks[b][:, :], lhsT=Wb[buf][:, k, fl * 128:(fl + 1) * 128],
                                    rhs=hT[:, k, tt * TT:(tt + 1) * TT], start=(k == 0), stop=(k == KT - 1)),
                                    reads=[(nm, buf, k // 4), ('h', k, tt)], writes=[PS(b)])
                        P.add('act', e_.activation(out=sg[fl], in_=banks[bg][:, :], func=AF.Silu),
                              reads=[PS(bg)], writes=[('sg', fl)])
                        P.add('dve', e_.tensor_tensor(out=a_bf[ab][fl], in0=banks[bu][:, :],
                                                                                 in1=sg[fl], op=ALU.mult),
                              reads=[PS(bu), ('sg', fl)], writes=[('a', ab, fl)])
                    if pend[0] is not None:
                        down(*pend[0])
                    pend[0] = (c, tt, ab)
                    if tt == 0 and c + 1 < NCH:
                        load_chunk(c + 1)
            down(*pend[0])
            P.fence()

        def xattn(l, sq_i):
            cur[0] = D_off
            sqb = [carve(TT, BF16) for _ in range(2)]
            Kt = carve(KT * MEM, BF16).rearrange("p (o m) -> p o m", o=KT)
            Vb = carve(2 * D, BF16).rearrange("p (t n) -> p t n", t=2)
            rD = carve(TT)
            Wq_b = carve(KT * D, BF16).rearrange("p (k n) -> p k n", k=KT)
            Wo_b = carve(KT * D, BF16).rearrange("p (k n) -> p k n", k=KT)
            x_off = cur[0]
            wp = [carve(KT * 128, BF16).rearrange("p (k n) -> p k n", k=KT) for _ in range(2)]
            memT = carve(KT * MEM).rearrange("p (k m) -> p k m", k=KT)
            mT = carve(KT * MEM, BF16).rearrange("p (k m) -> p k m", k=KT)
            wvp = [carve(D, BF16) for _ in range(2)]
            x_end = cur[0]
            cur[0] = x_off
            qT = carve(KT * TT, BF16).rearrange("p (k t) -> p k t", k=KT)
            eT = carve(8 * TT, BF16).rearrange("p (j t) -> p j t", j=8)
            oT = carve(KT * TT, BF16).rearrange("p (k t) -> p k t", k=KT)
            cur[0] = max(cur[0], x_end)
            wq = dr["xattn_wq"][l].rearrange("(k p) n -> p k n", p=128)
            wo = dr["xattn_wo"][l].rearrange("(k p) n -> p k n", p=128)
            for mt in range(2):
                s_ = stage_slot()
                P.add('sp', e_.dma_start(out=stage[s_], in_=dr['mem'][sq_i, mt * 128:(mt + 1) * 128, :]),
                      writes=[('st', s_)], chan='st%d' % s_)
                for kh in range(2):
                    b = psum()
                    for kk in range(4):
                        k = kh * 4 + kk
                        P.add('pe', e_.transpose(out=banks[b][:, kk * 128:(kk + 1) * 128],
                                                                                in_=stage[s_][:, k * 128:(k + 1) * 128], identity=ident),
                              reads=[('st', s_), 'cst'], writes=[PS(b)])
                    P.add('act', e_.copy(out=memT[:, kh * 4:(kh + 1) * 4, mt * 128:(mt + 1) * 128],
                                                                     in_=banks[b][:, :].rearrange("p (k m) -> p k m", k=4)),
                          reads=[PS(b)], writes=[('memT', kh)])
            rmsnorm_tile(0, "mem_norm", l, lambda k: mT[:, k, :], lambda k: ('mT', k), sqb,
                         src=lambda k: memT[:, k, :], src_keys=lambda k: ('memT', k // 4), ncols=MEM)
            wk = dr["xattn_wk"][l].rearrange("(k p) n -> p k n", p=128)

            def k_handler(o, b):
                P.add('act', e_.activation(out=Kt[:, o, :], in_=banks[b][:, 0:MEM], func=AF.Copy, scale=0.0625),
                      reads=[PS(b)], writes=[('Kt', o)])
            proj(wk, KT, wp, lambda k: mT[:, k, :], lambda k: ('mT', k), MEM, k_handler, 'x')
            wv = dr["xattn_wv"][l].rearrange("(k p) n -> p k n", p=128)
            vb = [psum() for _ in range(4)]
            for k in range(KT):
                buf = k % 2
                load_piece(wv[:, k, :], lambda st: st, wvp[buf], [('wvp', buf)])
                for mt in range(2):
                    for ch in range(2):
                        b = vb[mt * 2 + ch]
                        P.add('pe', e_.matmul(
                            banks[b][:, :], lhsT=mT[:, k, mt * 128:(mt + 1) * 128], rhs=wvp[buf][:, ch * 512:(ch + 1) * 512],
                            start=(k == 0), stop=(k == KT - 1)),
                            reads=[('mT', k), ('wvp', buf)], writes=[PS(b)])
            for mt in range(2):
                for ch in range(2):
                    b = vb[mt * 2 + ch]
                    P.add('act', e_.copy(out=Vb[:, mt, ch * 512:(ch + 1) * 512], in_=banks[b][:, :]),
                          reads=[PS(b)], writes=[('V', mt)])
            for k in range(KT):
                load_piece(wq[:, k, :], lambda st: st, Wq_b[:, k, :], [('wq', k)])
            for k in range(KT):
                load_piece(wo[:, k, :], lambda st: st, Wo_b[:, k, :], [('wo', k)])
            P.fence()
            for tt in range(NTT):
                rmsnorm_tile(tt, "xattn_norm", l, lambda k: hT[:, k, 0:TT], lambda k: ('h', k, 0), sqb)

                def q_handler(o, b):
                    P.add('act', e_.copy(out=qT[:, o, :], in_=banks[b][:, :]), reads=[PS(b)], writes=[('qT', o)])
                proj_res(Wq_b, 'wq', KT, lambda k: hT[:, k, 0:TT], lambda k: ('h', k, 0), TT, q_handler)
                for h in range(4):
                    for mt in range(2):
                        b = psum()
                        for half in range(2):
                            P.add('pe', e_.matmul(
                                banks[b][:, :], lhsT=Kt[:, h * 2 + half, mt * 128:(mt + 1) * 128], rhs=qT[:, h * 2 + half, :],
                                start=(half == 0), stop=(half == 1)),
                                reads=[('Kt', h * 2 + half), ('qT', h * 2 + half)], writes=[PS(b)])
                        P.add('act', e_.activation(out=eT[:, h * 2 + mt, :], in_=banks[b][:, :], func=AF.Exp),
                              reads=[PS(b)], writes=[('eT', h * 2 + mt)])
                    bd = psum()
                    for mt in range(2):
                        P.add('pe', e_.matmul(banks[bd][:, :], lhsT=ones_b, rhs=eT[:, h * 2 + mt, :],
                                                                          start=(mt == 0), stop=(mt == 1)),
                              reads=[('eT', h * 2 + mt), 'onesb'], writes=[PS(bd)])
                    P.add('dve', e_.reciprocal(out=rD, in_=banks[bd][:, :]), reads=[PS(bd)], writes=['rD'])
                    for dh in range(2):
                        b = psum()
                        for mt in range(2):
                            P.add('pe', e_.matmul(
                                banks[b][:, :], lhsT=Vb[:, mt, h * 256 + dh * 128: h * 256 + (dh + 1) * 128], rhs=eT[:, h * 2 + mt, :],
                                start=(mt == 0), stop=(mt == 1)),
                                reads=[('V', mt), ('eT', h * 2 + mt)], writes=[PS(b)])
                        P.add('dve', e_.tensor_tensor(out=oT[:, h * 2 + dh, :], in0=banks[b][:, :], in1=rD, op=ALU.mult),
                              reads=[PS(b), 'rD'], writes=[('oT', h * 2 + dh)])

                def o_handler(o, b, tt=tt):
                    residual_add(b, o, tt, 1.0)
                proj_res(Wo_b, 'wo', KT, lambda k: oT[:, k, :], lambda k: ('oT', k), TT, o_handler)
            P.fence()

        def mix(l, sq_i):
            import os
            STOP = os.environ.get("MIXSTOP", "Z")
            cur[0] = D_off
            wp = [carve(KT * 128, BF16).rearrange("p (k n) -> p k n", k=KT) for _ in range(2)]
            sqb = [carve(TT, BF16) for _ in range(2)]

            def b2(i):
                return arena[:, h_off + i * 1024 + 512:h_off + (i + 1) * 1024]
            sm = b2(7)
            smc = [0]

            def small(n=NGP):
                a = sm[:, smc[0]:smc[0] + n]
                smc[0] += 16
                assert smc[0] <= 512
                return a
            rotc = small(); rots = small(); car_r = small(); car_i = small()
            cr_t = [small() for _ in range(4)]
            magT = carve(NGP * FR).rearrange("p (g t) -> p g t", g=NGP)
            tabc = carve(NGP * FR, BF16).rearrange("p (g t) -> p g t", g=NGP)
            tabs = carve(NGP * FR, BF16).rearrange("p (g t) -> p g t", g=NGP)
            lB = carve(NGP * 2 * 128, BF16).rearrange("p (g c n) -> p g c n", g=NGP, c=2)
            lC = carve(NGP * 3 * 128, BF16).rearrange("p (g c n) -> p g c n", g=NGP, c=3)
            dgd = carve(3 * 128, BF16).rearrange("p (c n) -> p c n", c=3)
            wglu = carve(3 * 384, BF16).rearrange("p (c n) -> p c n", c=3)
            BDp = carve(2 * 128, BF16).rearrange("p (c n) -> p c n", c=2)
            cwT = carve(3 * 32).rearrange("p (c j) -> p c j", c=3)
            dg = carve(CONVW * 128, BF16).rearrange("p (j n) -> p j n", j=CONVW)
            sgc = b2(0); cf2 = b2(1); lnm = b2(2); lnr = b2(3)
            pin = b2(4).bitcast(BF16).rearrange("p (c t) -> p c t", c=2)
            zin = [b2(5)[:, 0:256].bitcast(BF16), b2(5)[:, 256:512].bitcast(BF16)]
            ztmp = [b2(6)[:, 0:256].bitcast(BF16), b2(6)[:, 256:512].bitcast(BF16)]
            work_off = cur[0]
            lam_r = small(); lam_i = small(); dtv = small(); mag = small()
            ar_ = small(); ai_ = small(); zr = small(); zi = small(); fq = small()
            tA = small(); tB = small(); tC = small()
            ph = carve(NGP * FR).rearrange("p (g t) -> p g t", g=NGP)
            ph2 = carve(NGP * FR).rearrange("p (g t) -> p g t", g=NGP)
            Bn_r = carve(NGP * 16).rearrange("p (g k) -> p g k", g=NGP)
            Bn_i = carve(NGP * 16).rearrange("p (g k) -> p g k", g=NGP)
            Bb_r = carve(NGP * 16).rearrange("p (g k) -> p g k", g=NGP)
            Bb_i = carve(NGP * 16).rearrange("p (g k) -> p g k", g=NGP)
            BD = carve(NGP * 2 * 128).rearrange("p (g c k) -> p g c k", g=NGP, c=2)
            Cn = carve(2 * NGP * 64).rearrange("p (c g n) -> p c g n", c=2, g=NGP)
            CD = carve(128)
            st_l = carve(896)
            st_p = carve(256)
            prep_end = cur[0]
            cur[0] = work_off
            u_bf = carve(3 * TT, BF16).rearrange("p (c t) -> p c t", c=3)
            u_bf2 = carve(3 * TT, BF16).rearrange("p (c t) -> p c t", c=3)
            zz = [carve(4 * FR).rearrange("p (g t) -> p g t", g=4) for _ in range(2)]
            pp = [carve(4 * FR, BF16).rearrange("p (g t) -> p g t", g=4) for _ in range(4)]
            ypre = carve(3 * TT).rearrange("p (c t) -> p c t", c=3)
            ytmp = carve(3 * TT).rearrange("p (c t) -> p c t", c=3)
            yg_bf = carve(3 * TT, BF16).rearrange("p (c t) -> p c t", c=3)
            up = carve(2 * (16 + TT)).rearrange("p (c t) -> p c t", c=2)
            sA = carve(16 + TT); sB = carve(16 + TT)
            hc = carve(3 * (32 + TT), BF16).rearrange("p (c t) -> p c t", c=3)
            cf = carve(3 * TT).rearrange("p (c t) -> p c t", c=3)
            cur[0] = max(cur[0], prep_end)
            ymix = hT[:, :, TT:2 * TT]

            ch = 'lp0'
            for c_ in ('lp0', 'lp1', 'lp2', 'lp3'):
                P.chan_sync(c_)
            P.add('sp', e_.dma_start(out=st_l[0:12, 0:128], in_=dr['ssm_lambda_re'][l].rearrange("(g two) p -> g (two p)", two=2)),
                  writes=['stl_a'], chan=ch, kind='batch')
            P.add('sp', e_.dma_start(out=st_l[0:12, 128:256], in_=dr['ssm_lambda_im'][l].rearrange("(g two) p -> g (two p)", two=2)),
                  writes=['stl_b'], chan=ch, kind='batch')
            P.add('sp', e_.dma_start(out=st_l[0:12, 768:770], in_=dr['ssm_log_dt'][l].rearrange("(g two) -> g two", two=2)),
                  writes=['stl_c'], chan=ch, kind='batch')
            P.add('pool', e_.memset(st_l[0:32, 256:640], 0.0), writes=['stl_d'])
            P.add('sp', e_.dma_start(out=st_l[0:31, 256:640], in_=dr['conv_w'][l]), writes=['stl_d'], chan=ch, kind='batch')
            for two in range(2):
                P.add('dve', e_.tensor_scalar(out=st_l[0:12, 640 + two * 64:640 + (two + 1) * 64], in0=ones_f[0:12, 0:64],
                                              scalar1=st_l[0:12, 768 + two:769 + two], scalar2=None, op0=ALU.mult),
                      reads=['stl_c', 'cst'], writes=['stl_e%d' % two])
            slk = ['stl_a', 'stl_b', 'stl_c', 'stl_d', 'stl_e0', 'stl_e1']
            if STOP == 'A1b':
                P.fence()
                return
            bq = psum()
            for i, c0 in enumerate((0, 128, 640)):
                P.add('pe', e_.transpose(out=banks[bq][:, i * 16:i * 16 + 12], in_=st_l[0:12, c0:c0 + 128],
                                         identity=ident[0:12, 0:12]),
                      reads=slk + ['cst'], writes=[PS(bq)])
            for ct in range(3):
                P.add('pe', e_.transpose(out=banks[bq][:, 64 + ct * 32:64 + ct * 32 + 32],
                                                         in_=st_l[0:32, 256 + ct * 128:256 + (ct + 1) * 128], identity=ident[0:32, 0:32]),
                      reads=slk + ['cst'], writes=[PS(bq)])
            if STOP == 'A1c':
                P.add('dve', e_.tensor_copy(out=CD[:, 0:12], in_=banks[bq][:, 0:12]), reads=[PS(bq)], writes=['s5p'])
                P.fence()
                return
            P.add('dve', e_.tensor_copy(out=lam_r, in_=banks[bq][:, 0:12]), reads=[PS(bq)], writes=['s5p'])
            P.add('dve', e_.tensor_copy(out=lam_i, in_=banks[bq][:, 16:28]), reads=[PS(bq)], writes=['s5p'])
            if STOP == 'A1d':
                P.fence()
                return
            P.add('act', e_.activation(out=dtv, in_=banks[bq][:, 32:44], func=AF.Exp), reads=[PS(bq)], writes=['s5p1'])
            if STOP == 'A1e':
                P.fence()
                return
            P.add('dve', e_.tensor_copy(out=cwT.rearrange("p c j -> p (c j)"), in_=banks[bq][:, 64:160]),
                  reads=[PS(bq)], writes=['cwT'])
            if STOP == 'A1':
                P.fence()
                return
            SK = ['s5p', 's5p1', 's5p2']

            def dv(fn, r=SK, w=('s5p2',)):
                P.add('dve', fn, reads=list(r), writes=list(w))
            dv(e_.tensor_tensor(out=tA, in0=lam_r, in1=dtv, op=ALU.mult))
            P.add('act', e_.activation(out=mag, in_=tA, func=AF.Exp), reads=SK, writes=['s5p1'])
            dv(e_.scalar_tensor_tensor(out=fq, in0=lam_i, scalar=float(1.0 / (2 * np.pi)), in1=dtv, op0=ALU.mult, op1=ALU.mult))
            for g in range(NGP):
                dv(e_.tensor_scalar(out=ph[:, g, :], in0=iota1, scalar1=fq[:, g:g + 1], scalar2=None, op0=ALU.mult),
                   r=SK + ['cst', 'tabf'], w=['tabf'])
            phf = ph.rearrange("p g t -> p (g t)")
            ph2f = ph2.rearrange("p g t -> p (g t)")
            dv(e_.tensor_scalar(out=ph2f, in0=phf, scalar1=MAGIC, scalar2=MAGIC, op0=ALU.add, op1=ALU.subtract),
               r=['tabf'], w=['tabf2'])
            dv(e_.tensor_tensor(out=ph2f, in0=phf, in1=ph2f, op=ALU.subtract), r=['tabf', 'tabf2'], w=['tabf2'])
            P.add('act', e_.activation(out=ph2f, in_=ph2f, func=AF.Sin, scale=float(2 * np.pi)), reads=['tabf2'], writes=['tabf2'])
            dv(e_.tensor_copy(out=tabs.rearrange("p g t -> p (g t)"), in_=ph2f), r=['tabf2'], w=['tabs'])
            dv(e_.tensor_copy(out=rots, in_=ph2[:, :, FR - 1]), r=['tabf2'], w=['rot'])
            dv(e_.tensor_copy(out=tB, in_=ph2[:, :, 0]), r=['tabf2'], w=['s5p2'])
            dv(e_.tensor_scalar(out=phf, in0=phf, scalar1=0.25, scalar2=None, op0=ALU.add), r=['tabf'], w=['tabf'])
            dv(e_.tensor_scalar(out=ph2f, in0=phf, scalar1=MAGIC, scalar2=MAGIC, op0=ALU.add, op1=ALU.subtract),
               r=['tabf', 'tabf2', 'tabs', 'rot', 's5p2'], w=['tabf2'])
            dv(e_.tensor_tensor(out=ph2f, in0=phf, in1=ph2f, op=ALU.subtract), r=['tabf', 'tabf2'], w=['tabf2'])
            P.add('act', e_.activation(out=ph2f, in_=ph2f, func=AF.Sin, scale=float(2 * np.pi)), reads=['tabf2'], writes=['tabf2'])
            dv(e_.tensor_copy(out=tabc.rearrange("p g t -> p (g t)"), in_=ph2f), r=['tabf2'], w=['tabc'])
            dv(e_.tensor_copy(out=rotc, in_=ph2[:, :, FR - 1]), r=['tabf2'], w=['rot'])
            dv(e_.tensor_copy(out=tC, in_=ph2[:, :, 0]), r=['tabf2'], w=['s5p2'])
            for g in range(NGP):
                dv(e_.tensor_scalar(out=magT[:, g, :], in0=ones_f, scalar1=mag[:, g:g + 1], scalar2=None, op0=ALU.mult),
                   r=SK + ['cst'], w=['magT'])
            dv(e_.tensor_tensor(out=ar_, in0=mag, in1=tC, op=ALU.mult))
            dv(e_.tensor_tensor(out=ai_, in0=mag, in1=tB, op=ALU.mult))
            dv(e_.tensor_tensor(out=tA, in0=lam_r, in1=lam_r, op=ALU.mult))
            dv(e_.tensor_tensor(out=tB, in0=lam_i, in1=lam_i, op=ALU.mult))
            dv(e_.tensor_tensor(out=tA, in0=tA, in1=tB, op=ALU.add))
            dv(e_.reciprocal(out=tA, in_=tA))
            dv(e_.tensor_scalar(out=tB, in0=ar_, scalar1=-1.0, scalar2=None, op0=ALU.add))
            dv(e_.tensor_tensor(out=zr, in0=tB, in1=lam_r, op=ALU.mult))
            dv(e_.tensor_tensor(out=tC, in0=ai_, in1=lam_i, op=ALU.mult))
            dv(e_.tensor_tensor(out=zr, in0=zr, in1=tC, op=ALU.add))
            dv(e_.tensor_tensor(out=zr, in0=zr, in1=tA, op=ALU.mult))
            dv(e_.tensor_tensor(out=zi, in0=ai_, in1=lam_r, op=ALU.mult))
            dv(e_.tensor_tensor(out=tC, in0=tB, in1=lam_i, op=ALU.mult))
            dv(e_.tensor_tensor(out=zi, in0=zi, in1=tC, op=ALU.subtract))
            dv(e_.tensor_tensor(out=zi, in0=zi, in1=tA, op=ALU.mult))
            if STOP == 'A2':
                P.fence()
                return
            for (src, dstt) in (('ssm_b_re', Bn_r), ('ssm_b_im', Bn_i)):
                for two in range(2):
                    P.add('sp', e_.dma_start(
                        out=dstt[two * 64:(two + 1) * 64, :, :],
                        in_=dr[src][l].rearrange("(g two) p k -> two p g k", two=2)[two]),
                        writes=['Bn'], chan='lp2', kind='batch')
            zrb = ph[:, :, 0:16]
            zib = ph2[:, :, 0:16]
            for g in range(NGP):
                dv(e_.tensor_scalar(out=ph[:, g, 0:16], in0=ones_f[:, 0:16], scalar1=zr[:, g:g + 1], scalar2=None, op0=ALU.mult),
                   r=SK + ['cst', 'tabf', 'tabf2', 'tabs', 'tabc', 'rot'], w=['tabf'])
                dv(e_.tensor_scalar(out=ph2[:, g, 0:16], in0=ones_f[:, 0:16], scalar1=zi[:, g:g + 1], scalar2=None, op0=ALU.mult),
                   r=SK + ['cst', 'tabf', 'tabf2', 'tabs', 'tabc', 'rot'], w=['tabf2'])
            BK = SK + ['Bn', 'Bb', 'tabf', 'tabf2']
            dv(e_.tensor_tensor(out=Bb_r, in0=Bn_r, in1=zrb, op=ALU.mult), r=BK, w=['Bb'])
            dv(e_.tensor_tensor(out=Bb_i, in0=Bn_i, in1=zib, op=ALU.mult), r=BK, w=['Bb'])
            dv(e_.tensor_tensor(out=Bb_r, in0=Bb_r, in1=Bb_i, op=ALU.subtract), r=BK, w=['Bb'])
            dv(e_.tensor_tensor(out=Bb_i, in0=Bn_i, in1=zrb, op=ALU.mult), r=BK, w=['Bb'])
            dv(e_.tensor_tensor(out=Bn_r, in0=Bn_r, in1=zib, op=ALU.mult), r=BK, w=['Bn'])
            dv(e_.tensor_tensor(out=Bb_i, in0=Bb_i, in1=Bn_r, op=ALU.add), r=BK, w=['Bb'])
            dv(e_.memset(BD.rearrange("p g c k -> p (g c k)"), 0.0), r=[], w=['BD'])
            for c, Bb in enumerate((Bb_r, Bb_i)):
                for two in range(2):
                    for j in range(4):
                        c0 = j * 32 + two * 16
                        dv(e_.tensor_copy(out=BD[two * 64:(two + 1) * 64, j::4, c, c0:c0 + 16],
                                          in_=Bb[two * 64:(two + 1) * 64, j::4, :]),
                           r=['Bb', 'BD'], w=['BD'])
            for g in range(NGP):
                bt = psum()
                for c in range(2):
                    P.add('pe', e_.transpose(out=banks[bt][:, c * 128:(c + 1) * 128], in_=BD[:, g, c, :], identity=ident),
                          reads=['BD', 'cst'], writes=[PS(bt)])
                P.add('act', e_.copy(out=lB[:, g, :, :], in_=banks[bt][:, 0:256].rearrange("p (c n) -> p c n", c=2)),
                      reads=[PS(bt)], writes=['lB'])
            if STOP == 'A3':
                P.fence()
                return
            for c, src in enumerate(('ssm_c_re', 'ssm_c_im')):
                P.add('sp', e_.dma_start(out=Cn[0:32, c, :, :],
                                                                in_=dr[src][l].rearrange("(g two) k p -> (two k) g p", two=2)),
                      writes=['Cn'], chan='lp3', kind='batch')
            P.add('pool', e_.memset(lC.rearrange("p g c n -> p (g c n)"), 0.0), writes=['lC'])
            for g in range(NGP):
                j4 = (g % 4) * 32
                for c in range(2):
                    dv(e_.tensor_scalar(out=CD[0:32, 0:64], in0=Cn[0:32, c, g, :], scalar1=m0[0:32, :], scalar2=None, op0=ALU.mult),
                       r=['Cn', 'cst', 'CD'], w=['CD'])
                    dv(e_.tensor_scalar(out=CD[0:32, 64:128], in0=Cn[0:32, c, g, :], scalar1=m1[0:32, :], scalar2=None, op0=ALU.mult),
                       r=['Cn', 'cst', 'CD'], w=['CD'])
                    bt = psum()
                    P.add('pe', e_.transpose(out=banks[bt][:, 0:32], in_=CD[0:32, :], identity=ident[0:32, 0:32]),
                          reads=['CD', 'cst'], writes=[PS(bt)])
                    if c == 0:
                        P.add('act', e_.copy(out=lC[:, g, 0, j4:j4 + 32], in_=banks[bt][:, 0:32]), reads=[PS(bt), 'lC'], writes=['lC'])
                        P.add('act', e_.activation(out=lC[:, g, 1, j4:j4 + 32], in_=banks[bt][:, 0:32], func=AF.Copy, scale=-1.0),
                              reads=[PS(bt), 'lC'], writes=['lC'])
                    else:
                        P.add('act', e_.activation(out=lC[:, g, 2, j4:j4 + 32], in_=banks[bt][:, 0:32], func=AF.Copy, scale=-1.0),
                              reads=[PS(bt), 'lC'], writes=['lC'])
            if STOP == 'A4':
                P.fence()
                return
            for ct in range(3):
                dv(e_.tensor_scalar(out=dgd[:, ct, :], in0=ident, scalar1=pcol('ssm_d', l, ct, 3), scalar2=None, op0=ALU.mult),
                   r=['cst', 'pvec'], w=['dgd'])
                load_piece(dr['ssm_w_glu'][l][ct * 128:(ct + 1) * 128, :], lambda st: st[:, 0:384], wglu[:, ct, :], ['wglu'])
            P.add('pool', e_.memset(st_p[:, 0:256], 0.0), writes=['stp'])
            spks = []
            for g in range(4):
                t_, h_ = g // 2, g % 2
                P.add('sp', e_.dma_start(out=st_p[h_ * 64:(h_ + 1) * 64, t_ * 128 + h_ * 64:t_ * 128 + (h_ + 1) * 64],
                                                                     in_=dr['pool_w'][l, g]),
                      reads=['stp'], writes=['stp%d' % g], chan='lp1', kind='batch')
                spks.append('stp%d' % g)
            P.add('pool', e_.tensor_copy(out=BDp.rearrange("p c n -> p (c n)"), in_=st_p[:, 0:256]), reads=['stp'] + spks, writes=['BDp'])

            P.fence()
            if STOP == 'A':
                return
            dv(e_.memset(car_r, 0.0), r=[], w=['car'])
            dv(e_.memset(car_i, 0.0), r=[], w=['car'])
            P.add('pool', e_.memset(up[:, :, 0:16], 0.0), writes=['up'])
            P.add('pool', e_.memset(hc[:, :, 0:32], 0.0), writes=['hc'])

            win = dr["w_in"][l].rearrange("(k p) n -> p k n", p=128)
            wout = dr["w_out"][l].rearrange("(k p) n -> p k n", p=128)
            u_bfs = [u_bf, u_bf2]

            def g_in(tt):
                ub = tt % 2
                ubf = u_bfs[ub]
                rmsnorm_tile(tt, "mix_norm", l, lambda k: hT[:, k, 0:TT], lambda k: ('h', k, 0), sqb)
                yield
                vbank = {}

                def in_handler(o, b, tt=tt):
                    if o < 3:
                        P.add('act', e_.copy(out=ubf[:, o, :], in_=banks[b][:, :]), reads=[PS(b)], writes=[('u', ub, o)])
                    elif o < 5:
                        P.add('act', e_.copy(out=up[:, o - 3, 16:16 + TT], in_=banks[b][:, :]), reads=[PS(b), 'up'], writes=['up'])
                    elif o < 8:
                        P.add('act', e_.copy(out=cf[:, o - 5, :], in_=banks[b][:, :]), reads=[PS(b)], writes=[('cf', o - 5)])
                    else:
                        ct = o - 8
                        P.add('act', e_.activation(out=sgc, in_=banks[b][:, :], func=AF.Sigmoid), reads=[PS(b)], writes=['sgc'])
                        P.add('dve', e_.tensor_tensor(out=hc[:, ct, 32:32 + TT], in0=cf[:, ct, :], in1=sgc, op=ALU.mult),
                              reads=[('cf', ct), 'sgc', 'hc'], writes=['hc'])
                yield from proj_g(win, 11, wp, lambda k: hT[:, k, 0:TT], lambda k: ('h', k, 0), TT, in_handler, 'm')

            def g_pc(tt):
                for t_ in range(2):
                    U = up[:, t_, :]
                    NP_ = 16 + TT
                    P.add('pool', e_.tensor_tensor(out=sA[:, 1:NP_], in0=U[:, 1:NP_], in1=U[:, 0:NP_ - 1], op=ALU.add),
                          reads=['up', 'sA'], writes=['sA'])
                    P.add('pool', e_.tensor_tensor(out=sB[:, 3:NP_], in0=sA[:, 3:NP_], in1=sA[:, 1:NP_ - 2], op=ALU.add),
                          reads=['sA', 'sB'], writes=['sB'])
                    if t_ == 0:
                        wins = ((sA, 2, 0), (sB, 4, 1))
                    else:
                        P.add('pool', e_.tensor_tensor(out=sA[:, 7:NP_], in0=sB[:, 7:NP_], in1=sB[:, 3:NP_ - 4], op=ALU.add),
                              reads=['sA', 'sB'], writes=['sA'])
                        P.add('pool', e_.tensor_tensor(out=sB[:, 15:NP_], in0=sA[:, 15:NP_], in1=sA[:, 7:NP_ - 8], op=ALU.add),
                              reads=['sA', 'sB'], writes=['sB'])
                        wins = ((sA, 8, 0), (sB, 16, 1))
                    for (sw, w_, hf) in wins:
                        ps_ = slice(hf * 64, (hf + 1) * 64)
                        dv(e_.scalar_tensor_tensor(
                            out=pin[ps_, t_, :], in0=sw[ps_, 16:16 + TT], scalar=1.0 / w_, in1=U[ps_, 16:16 + TT], op0=ALU.mult, op1=ALU.subtract),
                           r=['sA', 'sB', 'up', 'pin'], w=['pin'])
                        if tt == 0:
                            wi = {2: 0, 4: 1, 8: 2, 16: 3}[w_]
                            itab = cst[ps_, 400 + wi * 16:400 + (wi + 1) * 16]
                            dv(e_.tensor_tensor(out=lnm[ps_, 0:16], in0=sw[ps_, 16:32], in1=itab, op=ALU.mult),
                               r=['sA', 'sB', 'cst', 'lnm'], w=['lnm'])
                            dv(e_.tensor_tensor(out=pin[ps_, t_, 0:16], in0=lnm[ps_, 0:16], in1=U[ps_, 16:32], op=ALU.subtract),
                               r=['lnm', 'up', 'pin'], w=['pin'])
                    b = psum()
                    P.add('pe', e_.matmul(banks[b][:, :], lhsT=BDp[:, t_, :], rhs=pin[:, t_, :], start=True, stop=True),
                          reads=['BDp', 'pin'], writes=[PS(b)])
                    dv(e_.tensor_scalar(out=ymix[:, 3 + t_, :], in0=banks[b][:, :], scalar1=pcol('pool_scale', l, t_, 2), scalar2=None, op0=ALU.mult),
                       r=[PS(b), 'pvec'], w=[('ym', 3 + t_)])
                P.add('pool', e_.tensor_copy(out=up[:, :, 0:16], in_=up[:, :, TT:TT + 16]), reads=['up', 'pin'], writes=['up'])
                yield
                for ct in range(3):
                    for j in range(CONVW):
                        dv(e_.tensor_scalar(out=dg[:, j, :], in0=ident, scalar1=cwT[:, ct, j:j + 1], scalar2=None, op0=ALU.mult),
                           r=['cwT', 'cst', 'dg'], w=['dg'])
                    b = psum()
                    for j in range(CONVW):
                        P.add('pe', e_.matmul(banks[b][:, :], lhsT=dg[:, j, :], rhs=hc[:, ct, 2 + j:2 + j + TT],
                                                                        start=(j == 0), stop=(j == CONVW - 1)),
                              reads=['dg', 'hc'], writes=[PS(b)])
                    dv(e_.tensor_scalar(out=cf[:, ct, :], in0=banks[b][:, :], scalar1=pcol('conv_b', l, ct, 3), scalar2=None, op0=ALU.add),
                       r=[PS(b), 'pvec'], w=[('cf', ct)])
                P.add('pool', e_.tensor_copy(out=hc[:, :, 0:32], in_=hc[:, :, TT:TT + 32]), reads=['hc'], writes=['hc'])
                yield
                b1 = psum(); b2 = psum()
                cfb = cf2.bitcast(BF16)[:, 0:TT]
                cfq = cf2.bitcast(BF16)[:, TT:2 * TT]
                for ct in range(3):
                    P.add('act', e_.copy(out=cfb, in_=cf[:, ct, :]), reads=[('cf', ct)], writes=['cfb'])
                    P.add('pe', e_.matmul(banks[b1][:, :], lhsT=ones_b, rhs=cfb, start=(ct == 0), stop=(ct == 2)),
                          reads=['cfb', 'onesb'], writes=[PS(b1)])
                    P.add('act', e_.activation(out=cfq, in_=cf[:, ct, :], func=AF.Square), reads=[('cf', ct)], writes=['cfq'])
                    P.add('pe', e_.matmul(banks[b2][:, :], lhsT=ones_b, rhs=cfq, start=(ct == 0), stop=(ct == 2)),
                          reads=['cfq', 'onesb'], writes=[PS(b2)])
                P.add('act', e_.activation(out=lnm, in_=banks[b1][:, :], func=AF.Copy, scale=1.0 / D_CONV), reads=[PS(b1)], writes=['lnm'])
                dv(e_.tensor_tensor(out=lnr, in0=lnm, in1=lnm, op=ALU.mult), r=['lnm'], w=['lnr'])
                dv(e_.scalar_tensor_tensor(out=lnr, in0=banks[b2][:, :], scalar=1.0 / D_CONV, in1=lnr, op0=ALU.mult, op1=ALU.subtract),
                   r=[PS(b2), 'lnr'], w=['lnr'])
                P.add('act', e_.activation(out=lnr, in_=lnr, func=AF.Sqrt, bias=epsc, scale=1.0), reads=['lnr', 'cst'], writes=['lnr'])
                dv(e_.reciprocal(out=lnr, in_=lnr), r=['lnr'], w=['lnr'])
                for ct in range(3):
                    dv(e_.tensor_tensor(out=cf[:, ct, :], in0=cf[:, ct, :], in1=lnm, op=ALU.subtract), r=[('cf', ct), 'lnm'], w=[('cf', ct)])
                    dv(e_.tensor_tensor(out=cf[:, ct, :], in0=cf[:, ct, :], in1=lnr, op=ALU.mult), r=[('cf', ct), 'lnr'], w=[('cf', ct)])
                    dv(e_.tensor_scalar(out=cf[:, ct, :], in0=cf[:, ct, :], scalar1=pcol('conv_ln_g', l, ct, 3),
                                        scalar2=pcol('conv_ln_b', l, ct, 3), op0=ALU.mult, op1=ALU.add),
                       r=[('cf', ct), 'pvec'], w=[('cf', ct)])
                    P.add('act', e_.activation(out=ymix[:, 5 + ct, :], in_=cf[:, ct, :], func=AF.Silu),
                          reads=[('cf', ct)], writes=[('ym', 5 + ct)])
                yield

            def s5(tt, pump):
                ub = tt % 2
                ubf = u_bfs[ub]
                def emit_B(fr, gq):
                    ps_cur[0] = 'A_b'
                    bR = psum(); bI = psum()
                    ps_cur[0] = 'all'
                    for j in range(4):
                        g = gq * 4 + j
                        ct_u = g // 4
                        urows = ubf[:, ct_u, fr * FR:(fr + 1) * FR]
                        for c, bb in ((0, bR), (1, bI)):
                            P.add('pe', e_.matmul(
                                banks[bb][:, j * FR:(j + 1) * FR], lhsT=lB[:, g, c, :], rhs=urows,
                                start=True, stop=True),
                                reads=['lB', ('u', ub, ct_u)], writes=[PS(bb)])
                    return bR, bI
                NFR = TT // FR
                nextB = emit_B(0, 0)
                for fr in range(NFR):
                    ps_cur[0] = 'A_by'
                    by = psum()
                    ps_cur[0] = 'all'
                    for gq in range(3):
                        bR, bI = nextB
                        if gq < 2:
                            nextB = emit_B(fr, gq + 1)
                        elif fr + 1 < NFR:
                            nextB = emit_B(fr + 1, 0)
                        zb = gq % 2
                        tc_ = tabc[:, gq * 4:(gq + 1) * 4, :].rearrange("p g t -> p (g t)")
                        ts_ = tabs[:, gq * 4:(gq + 1) * 4, :].rearrange("p g t -> p (g t)")
                        zr_in, zi_in = zin[0], zin[1]
                        dv(e_.tensor_tensor(out=zr_in, in0=banks[bR][:, :], in1=tc_, op=ALU.mult),
                           r=[PS(bR), 'tabc', 'zin0'], w=['zin0'])
                        dv(e_.tensor_tensor(out=ztmp[0], in0=banks[bI][:, :], in1=ts_, op=ALU.mult),
                           r=[PS(bI), 'tabs', 'zt0'], w=['zt0'])
                        dv(e_.tensor_tensor(out=zi_in, in0=banks[bI][:, :], in1=tc_, op=ALU.mult),
                           r=[PS(bI), 'tabc', 'zin1'], w=['zin1'])
                        dv(e_.tensor_tensor(out=ztmp[1], in0=banks[bR][:, :], in1=ts_, op=ALU.mult),
                           r=[PS(bR), 'tabs', 'zt1'], w=['zt1'])
                        dv(e_.tensor_tensor(out=zr_in, in0=zr_in, in1=ztmp[0], op=ALU.add), r=['zin0', 'zt0'], w=['zin0'])
                        dv(e_.tensor_tensor(out=zi_in, in0=zi_in, in1=ztmp[1], op=ALU.subtract), r=['zin1', 'zt1'], w=['zin1'])
                        for j in range(4):
                            g = gq * 4 + j
                            dv(e_.tensor_tensor_scan(out=zz[0][:, j, :], data0=magT[:, g, :], data1=zr_in[:, j * FR:(j + 1) * FR],
                                                                        initial=car_r[:, g:g + 1], op0=ALU.mult, op1=ALU.add),
                               r=['magT', 'zin0', 'car', 'zz0'], w=['zz0'])
                            dv(e_.tensor_tensor_scan(out=zz[1][:, j, :], data0=magT[:, g, :], data1=zi_in[:, j * FR:(j + 1) * FR],
                                                                        initial=car_i[:, g:g + 1], op0=ALU.mult, op1=ALU.add),
                               r=['magT', 'zin1', 'car', 'zz1'], w=['zz1'])
                        tc3 = tabc[:, gq * 4:(gq + 1) * 4, :]
                        ts3 = tabs[:, gq * 4:(gq + 1) * 4, :]
                        dv(e_.tensor_tensor(out=pp[0], in0=zz[0], in1=tc3, op=ALU.mult), r=['zz0', 'tabc', 'pp0'], w=['pp0'])
                        dv(e_.tensor_tensor(out=pp[1], in0=zz[1], in1=ts3, op=ALU.mult), r=['zz1', 'tabs', 'pp1'], w=['pp1'])
                        dv(e_.tensor_tensor(out=pp[2], in0=zz[0], in1=ts3, op=ALU.mult), r=['zz0', 'tabs', 'pp2'], w=['pp2'])
                        dv(e_.tensor_tensor(out=pp[3], in0=zz[1], in1=tc3, op=ALU.mult), r=['zz1', 'tabc', 'pp3'], w=['pp3'])
                        gs = slice(gq * 4, gq * 4 + 4)
                        zrl = zz[0][:, :, FR - 1]
                        zil = zz[1][:, :, FR - 1]
                        P.add('pool', e_.tensor_tensor(out=cr_t[0][:, gs], in0=rotc[:, gs], in1=zrl, op=ALU.mult),
                              reads=['rot', 'zz0', 'crt'], writes=['crt'])
                        P.add('pool', e_.tensor_tensor(out=cr_t[1][:, gs], in0=rots[:, gs], in1=zil, op=ALU.mult),
                              reads=['rot', 'zz1', 'crt'], writes=['crt'])
                        P.add('pool', e_.tensor_tensor(out=cr_t[2][:, gs], in0=rots[:, gs], in1=zrl, op=ALU.mult),
                              reads=['rot', 'zz0', 'crt'], writes=['crt'])
                        P.add('pool', e_.tensor_tensor(out=cr_t[3][:, gs], in0=rotc[:, gs], in1=zil, op=ALU.mult),
                              reads=['rot', 'zz1', 'crt'], writes=['crt'])
                        P.add('pool', e_.tensor_tensor(out=car_r[:, gs], in0=cr_t[0][:, gs], in1=cr_t[1][:, gs], op=ALU.subtract),
                              reads=['crt', 'car'], writes=['car'])
                        P.add('pool', e_.tensor_tensor(out=car_i[:, gs], in0=cr_t[2][:, gs], in1=cr_t[3][:, gs], op=ALU.add),
                              reads=['crt', 'car'], writes=['car'])
                        for j in range(4):
                            g = gq * 4 + j
                            for i, (pi, ci) in enumerate(((0, 0), (1, 1), (2, 2), (3, 2))):
                                P.add('pe', e_.matmul(
                                    banks[by][:, gq * FR:(gq + 1) * FR], lhsT=lC[:, g, ci, :], rhs=pp[pi][:, j, :],
                                    start=(i == 0 and j == 0), stop=False),
                                    reads=['lC', 'pp%d' % pi], writes=[PS(by)])
                        P.add('pe', e_.matmul(banks[by][:, gq * FR:(gq + 1) * FR], lhsT=dgd[:, gq, :],
                                              rhs=ubf[:, gq, fr * FR:(fr + 1) * FR], start=False, stop=True),
                              reads=['dgd', ('u', ub, gq)], writes=[PS(by)])
                        pump()
                    P.add('act', e_.copy(out=ypre[:, :, fr * FR:(fr + 1) * FR],
                                                               in_=banks[by][:, 0:3 * FR].rearrange("p (c t) -> p c t", c=3)),
                          reads=[PS(by)], writes=['ypre'])

            def post(tt):
                ypf = ypre.rearrange("p c t -> p (c t)")
                ytf = ytmp.rearrange("p c t -> p (c t)")
                P.add('act', e_.activation(out=ytf, in_=ypf, func=AF.Square), reads=['ypre'], writes=['ytmp'])
                dv(e_.tensor_scalar(out=ytf, in0=ytf, scalar1=0.044715, scalar2=1.0, op0=ALU.mult, op1=ALU.add), r=['ytmp'], w=['ytmp'])
                dv(e_.tensor_tensor(out=ytf, in0=ytf, in1=ypf, op=ALU.mult), r=['ytmp', 'ypre'], w=['ytmp'])
                P.add('act', e_.activation(out=ytf, in_=ytf, func=AF.Sigmoid, scale=1.5957691216057308), reads=['ytmp'], writes=['ytmp'])
                dv(e_.tensor_tensor(out=ypf, in0=ypf, in1=ytf, op=ALU.mult), r=['ytmp', 'ypre'], w=['ypre'])
                P.add('act', e_.copy(out=yg_bf.rearrange("p c t -> p (c t)"), in_=ypf), reads=['ypre'], writes=['ygbf'])
                for co in range(3):
                    b = psum()
                    for ci in range(3):
                        P.add('pe', e_.matmul(banks[b][:, :], lhsT=wglu[:, ci, co * 128:(co + 1) * 128], rhs=yg_bf[:, ci, :],
                                                                          start=(ci == 0), stop=(ci == 2)),
                              reads=['wglu', 'ygbf'], writes=[PS(b)])
                    P.add('act', e_.activation(out=ytmp[:, co, :], in_=banks[b][:, :], func=AF.Sigmoid),
                          reads=[PS(b), 'ytmp'], writes=['ytmp'])
                    dv(e_.tensor_tensor(out=ymix[:, co, :], in0=ypre[:, co, :], in1=ytmp[:, co, :], op=ALU.mult),
                       r=['ypre', 'ytmp'], w=[('ym', co)])

            def g_out(tt):

                def out_handler(o, b, tt=tt):
                    residual_add(b, o, tt, 1.0)
                yield from proj_g(wout, KT, wp, lambda k: ymix[:, k, :], lambda k: ('ym', k), TT, out_handler, 'm')

            def drain(g):
                for _ in g:
                    pass

            import itertools
            drain(g_in(0))
            for tt in range(NTT):
                gens = []
                if tt > 0:
                    gens.append(g_out(tt - 1))
                gens.append(g_pc(tt))
                if tt + 1 < NTT:
                    gens.append(g_in(tt + 1))
                B = itertools.chain(*gens)

                def pump(B=B):
                    ps_cur[0] = 'B'
                    next(B, None)
                    ps_cur[0] = 'all'
                s5(tt, pump)
                ps_cur[0] = 'B'
                drain(B)
                ps_cur[0] = 'all'
                post(tt)
            drain(g_out(NTT - 1))
            P.fence()

        for sq_i in range(n_seq):
            for j in range(S // 128):
                s_ = stage_slot()
                P.add('sp', e_.dma_start(out=stage[s_], in_=dr['x'][sq_i, j * 128:(j + 1) * 128, :]),
                      writes=[('st', s_)], chan='st%d' % s_)
                tt = j // 4
                for kh in range(2):
                    b = psum()
                    for kk in range(4):
                        k = kh * 4 + kk
                        P.add('pe', e_.transpose(out=banks[b][:, kk * 128:(kk + 1) * 128],
                                                                                in_=stage[s_][:, k * 128:(k + 1) * 128], identity=ident),
                              reads=[('st', s_), 'cst'], writes=[PS(b)])
                    eng = 'act' if kh == 0 else 'dve'
                    if eng == 'act':
                        fn = e_.copy(out=xres[:, kh * 4:(kh + 1) * 4, j * 128:(j + 1) * 128],
                                                               in_=banks[b][:, :].rearrange("p (k t) -> p k t", k=4))
                    else:
                        fn = e_.tensor_copy(out=xres[:, kh * 4:(kh + 1) * 4, j * 128:(j + 1) * 128],
                                                                      in_=banks[b][:, :].rearrange("p (k t) -> p k t", k=4))
                    P.add(eng, fn, reads=[PS(b)], writes=[('x', kh * 4 + kk, tt) for kk in range(4)])
            P.fence()
            for l in range(n_layers):
                if 'ffn1' in stages:
                    ffn(l, "ffn1")
                if 'mix' in stages:
                    mix(l, sq_i)
                if 'xattn' in stages:
                    xattn(l, sq_i)
                if 'ffn2' in stages:
                    ffn(l, "ffn2")
            cur[0] = D_off
            sqb = [carve(TT, BF16) for _ in range(4)]
            cur[0] = h_off
            yn = carve(KT * TT).rearrange("p (k t) -> p k t", k=KT)
            for tt in range(NTT):
                rmsnorm_tile(tt, "final_norm", 0, lambda k: yn[:, k, :], lambda k: ('yn', k), sqb)
                for jb in range(4):
                    s_ = stage_slot()
                    for kh in range(2):
                        b = psum()
                        for kk in range(4):
                            k = kh * 4 + kk
                            P.add('pe', e_.transpose(out=banks[b][:, kk * 128:(kk + 1) * 128],
                                                                                    in_=yn[:, k, jb * 128:(jb + 1) * 128], identity=ident),
                                  reads=[('yn', k), 'cst'], writes=[PS(b)])
                        if kh == 0:
                            P.add('act', e_.copy(out=stage[s_][:, 0:512], in_=banks[b][:, :]), reads=[PS(b)], writes=[('st', s_)])
                        else:
                            P.add('dve', e_.tensor_copy(out=stage[s_][:, 512:1024], in_=banks[b][:, :]), reads=[PS(b)], writes=[('st', s_)])
                    r0 = tt * TT + jb * 128
                    P.add('sp', e_.dma_start(out=out[sq_i, r0:r0 + 128, :], in_=stage[s_]),
                          reads=[('st', s_)], writes=[('outd', s_)], chan='out', kind='batch')
            P.fence()
        last_out = [o.id for o in P.ops if o.chan == 'out']
        P.add('sp', None, extra_deps=last_out)

        P.finalize()
        with nc.Block() as block:
            engines = {'pe': block.tensor, 'act': block.scalar, 'dve': block.vector, 'pool': block.gpsimd, 'sp': block.sync}
            P.emit(nc, engines, esem, csem)
    return nc


def make_consts():
    c = np.zeros((128, 512), np.float32)
    c[:, 0:128] = np.eye(128, dtype=np.float32)
    c[:, 128:256] = 1.0
    c[:, 256:384] = np.arange(1, 129, dtype=np.float32)[None, :]
    p = np.arange(128)
    c[:, 384] = ((p % 32) < 16).astype(np.float32)
    c[:, 385] = ((p % 32) >= 16).astype(np.float32)
    c[:, 386] = EPS
    for wi, w in enumerate((2, 4, 8, 16)):
        c[:, 400 + wi * 16:400 + (wi + 1) * 16] = 1.0 / np.minimum(np.arange(1, 17), w)[None, :]
    return c


_NC_CACHE = {}


def kernel(**inputs):
    n_cores = 8
    x = np.ascontiguousarray(inputs["x"], dtype=np.float32)
    mem = np.ascontiguousarray(inputs["mem"], dtype=np.float32)
    per = x.shape[0] // n_cores
    if 'nc' not in _NC_CACHE:
        _NC_CACHE['nc'] = build_nc(n_seq=per)
    nc = _NC_CACHE['nc']
    consts = make_consts()
    params = {n: np.ascontiguousarray(inputs[n], dtype=np.float32) for n in PARAM_NAMES}
    in_maps = []
    for c in range(n_cores):
        m = {"x": x[c * per:(c + 1) * per], "mem": mem[c * per:(c + 1) * per], "consts": consts}
        m.update(params)
        in_maps.append(m)
    res = run_bass_kernel_spmd(nc, in_maps, core_ids=list(range(n_cores)))
    return np.concatenate([r["out"] for r in res.results], axis=0)
```

```python
import numpy as np
import concourse.bass as bass
import concourse.mybir as mybir
from concourse.bass_utils import run_bass_kernel_spmd

F32 = mybir.dt.float32
BF16 = mybir.dt.bfloat16
AF = mybir.ActivationFunctionType
ALU = mybir.AluOpType

DEPTH = 2
D = 1024
S = 2048
FF = 2816
MEM = 256
D_SSM, D_POOL, D_CONV = 384, 256, 384
D_IN = 1408
CONVW = 31
EPS = 1e-6
KT = 8
TT = 512
NTT = S // TT
FR = 128
NGP = 12
MAGIC = 12582912.0
ENGS = ['pe', 'act', 'dve', 'pool', 'sp']

PARAM_NAMES = ["ffn1_norm", "ffn1_w_gate", "ffn1_w_up", "ffn1_w_down", "mix_norm", "w_in", "w_out",
               "ssm_lambda_re", "ssm_lambda_im", "ssm_log_dt", "ssm_b_re", "ssm_b_im", "ssm_c_re",
               "ssm_c_im", "ssm_d", "ssm_w_glu", "pool_w", "pool_scale", "conv_w", "conv_b",
               "conv_ln_g", "conv_ln_b", "xattn_norm", "mem_norm", "xattn_wq", "xattn_wk",
               "xattn_wv", "xattn_wo", "ffn2_norm", "ffn2_w_gate", "ffn2_w_up", "ffn2_w_down",
               "final_norm"]


class Op:
    __slots__ = ('id', 'eng', 'fn', 'deps', 'pos', 'chan', 'chan_val', 'sig', 'sig_idx', 'waits', 'bsnap')


class _Rec:
    def __getattr__(self, name):
        return lambda *a, **k: (name, a, k)


e_ = _Rec()


class Prog:
    def __init__(self):
        self.ops = []
        self.last_write = {}
        self.readers = {}
        self.pos = {e: 0 for e in ENGS}
        self.chan_cnt = {}
        self.chan_kind = {}
        self.last_op = {e: None for e in ENGS}

    def add(self, eng, fn, reads=(), writes=(), chan=None, kind='serial', extra_deps=()):
        o = Op()
        o.id = len(self.ops)
        o.eng = eng
        o.fn = fn
        o.pos = self.pos[eng]
        self.pos[eng] += 1
        o.chan = chan
        o.sig = False
        o.sig_idx = 0
        o.chan_val = 0
        if chan is not None:
            self.chan_kind.setdefault(chan, kind)
            self.chan_cnt[chan] = self.chan_cnt.get(chan, 0) + 16
            o.chan_val = self.chan_cnt[chan]
        deps = set(extra_deps)
        for r in reads:
            w = self.last_write.get(r)
            if w is not None:
                deps.add(w)
        for w in writes:
            lw = self.last_write.get(w)
            if lw is not None:
                deps.add(lw)
            rd = self.readers.get(w)
            if rd:
                deps.update(rd.values())
        for w in writes:
            self.last_write[w] = o.id
            self.readers[w] = {}
        for r in reads:
            self.readers.setdefault(r, {})[eng] = o.id
        deps.discard(o.id)
        o.deps = deps
        o.bsnap = None
        for d in deps:
            c = self.ops[d].chan
            if c is not None and self.chan_kind[c] == 'batch':
                if o.bsnap is None:
                    o.bsnap = {}
                o.bsnap[c] = self.chan_cnt[c] - (16 if c == chan else 0)
        self.ops.append(o)
        if fn is not None:
            self.last_op[eng] = o.id
        return o.id

    def chan_sync(self, chan):
        o_id = self.add('sp', None)
        self.ops[o_id].bsnap = {('force', chan): self.chan_cnt.get(chan, 0)}

    def fence(self):
        last = [v for v in self.last_op.values() if v is not None]
        for e in ENGS:
            self.add(e, None, extra_deps=last)

    def finalize(self):
        ops = self.ops
        for o in ops:
            w = {}
            for d in o.deps:
                do = ops[d]
                if do.chan is not None:
                    key = ('c', do.chan)
                    val = do.chan_val if self.chan_kind[do.chan] == 'serial' else o.bsnap[do.chan]
                    w[key] = max(w.get(key, 0), val)
                    continue
                if do.eng == o.eng:
                    if o.eng in ('pe', 'sp'):
                        continue
                key = ('e', do.eng)
                prev = w.get(key)
                if prev is None or ops[prev].pos < do.pos:
                    w[key] = d
            if o.bsnap:
                for bk, bv in o.bsnap.items():
                    if isinstance(bk, tuple) and bk[0] == 'force' and bv > 0:
                        w[('c', bk[1])] = max(w.get(('c', bk[1]), 0), bv)
            o.waits = w
            for key, v in w.items():
                if key[0] == 'e':
                    ops[v].sig = True
        cnt = {e: 0 for e in ENGS}
        for o in ops:
            if o.sig:
                cnt[o.eng] += 1
                o.sig_idx = cnt[o.eng]

    def emit(self, nc, engines, esem, csem):
        ops = self.ops
        for e in ENGS:
            lst = [o for o in ops if o.eng == e]

            def body(eng, lst=lst, e=e):
                waited = {}
                for o in lst:
                    for key, v in o.waits.items():
                        if key[0] == 'c':
                            sem, val = csem[key[1]], v
                        else:
                            sem, val = esem[key[1]], ops[v].sig_idx
                        if waited.get(key, 0) >= val:
                            continue
                        waited[key] = val
                        eng.wait_ge(sem, val)
                    if o.fn is None:
                        assert not o.sig
                        continue
                    name, a, k = o.fn
                    inst = getattr(eng, name)(*a, **k)
                    if o.chan is not None:
                        inst.then_inc(csem[o.chan], 16)
                    elif o.sig:
                        inst.then_inc(esem[e], 1)
            engines[e](body)


def build_nc(n_seq=2, n_layers=DEPTH, stages=('ffn1', 'mix', 'xattn', 'ffn2')):
    nc = bass.Bass("TRN2", target_bir_lowering=False)
    dr = {}
    dr['x'] = nc.dram_tensor("x", [n_seq, S, D], F32, kind="ExternalInput").ap()
    dr['mem'] = nc.dram_tensor("mem", [n_seq, MEM, D], F32, kind="ExternalInput").ap()
    shapes = dict(
        ffn1_norm=[DEPTH, D], ffn1_w_gate=[DEPTH, D, FF], ffn1_w_up=[DEPTH, D, FF], ffn1_w_down=[DEPTH, FF, D],
        mix_norm=[DEPTH, D], w_in=[DEPTH, D, D_IN], w_out=[DEPTH, D, D],
        ssm_lambda_re=[DEPTH, 24, 64], ssm_lambda_im=[DEPTH, 24, 64], ssm_log_dt=[DEPTH, 24],
        ssm_b_re=[DEPTH, 24, 64, 16], ssm_b_im=[DEPTH, 24, 64, 16], ssm_c_re=[DEPTH, 24, 16, 64],
        ssm_c_im=[DEPTH, 24, 16, 64], ssm_d=[DEPTH, 384], ssm_w_glu=[DEPTH, 384, 384],
        pool_w=[DEPTH, 4, 64, 64], pool_scale=[DEPTH, 256], conv_w=[DEPTH, 31, 384], conv_b=[DEPTH, 384],
        conv_ln_g=[DEPTH, 384], conv_ln_b=[DEPTH, 384], xattn_norm=[DEPTH, D], mem_norm=[DEPTH, D],
        xattn_wq=[DEPTH, D, D], xattn_wk=[DEPTH, D, D], xattn_wv=[DEPTH, D, D], xattn_wo=[DEPTH, D, D],
        ffn2_norm=[DEPTH, D], ffn2_w_gate=[DEPTH, D, FF], ffn2_w_up=[DEPTH, D, FF], ffn2_w_down=[DEPTH, FF, D],
        final_norm=[D])
    for n in PARAM_NAMES:
        dr[n] = nc.dram_tensor(n, shapes[n], F32, kind="ExternalInput").ap()
    dr['consts'] = nc.dram_tensor("consts", [128, 512], F32, kind="ExternalInput").ap()
    out = nc.dram_tensor("out", [n_seq, S, D], F32, kind="ExternalOutput").ap()

    P = Prog()
    NW = 53000
    import contextlib
    with contextlib.ExitStack() as es:
        arena = es.enter_context(nc.sbuf_tensor("arena", [128, NW], F32))
        banks = [es.enter_context(nc.psum_tensor("ps%d" % i, [128, 512], F32)) for i in range(8)]
        esem = {e: es.enter_context(nc.semaphore("sem_" + e)) for e in ENGS}
        chans = ['st0', 'st1', 'st2', 'st3', 'par', 'cst', 'out', 'lp0', 'lp1', 'lp2', 'lp3']
        csem = {c: es.enter_context(nc.semaphore("ch_" + c)) for c in chans}

        cur = [0]

        def carve(nelem, dt=F32):
            words = nelem if dt == F32 else (nelem + 1) // 2
            words = (words + 7) // 8 * 8
            off = cur[0]
            cur[0] += words
            assert cur[0] <= NW, "arena overflow %d" % cur[0]
            a = arena[:, off:off + words]
            if dt != F32:
                a = a.bitcast(dt)[:, 0:nelem]
            else:
                a = a[:, 0:nelem]
            return a

        xres = carve(KT * S).rearrange("p (k t) -> p k t", k=KT)
        h_off = cur[0]
        hT = carve(KT * S, BF16).rearrange("p (k t) -> p k t", k=KT)
        h_end = cur[0]
        stage = [carve(1024) for _ in range(4)]
        cst = carve(512)
        ident = cst[:, 0:128]
        ones_f = cst[:, 128:256]
        iota1 = cst[:, 256:384]
        m0 = cst[:, 384:385]
        m1 = cst[:, 385:386]
        epsc = cst[:, 386:387]
        ident_b = carve(128, BF16)
        ones_b = carve(128, BF16)
        pvec = carve(128)
        D_off = cur[0]

        psn = [0]

        def psum():
            b = psn[0] % 8
            psn[0] += 1
            return b

        def PS(b):
            return ('ps', b)

        stn = [0]

        def stage_slot():
            s_ = stn[0] % 4
            stn[0] += 1
            return s_

        P.add('sp', e_.dma_start(out=cst, in_=dr['consts']), writes=['cst'], chan='cst', kind='batch')
        P.add('dve', e_.tensor_copy(out=ident_b, in_=ident), reads=['cst'], writes=['identb'])
        P.add('dve', e_.tensor_copy(out=ones_b, in_=ones_f), reads=['cst'], writes=['onesb'])

        prow = {}
        rows = []
        r = 0
        for n in ["ffn1_norm", "mix_norm", "xattn_norm", "mem_norm", "ffn2_norm"]:
            prow[n] = r
            rows.append((n, r, dr[n].rearrange("l (k p) -> (l k) p", p=128), DEPTH * 8))
            r += DEPTH * 8
        prow["final_norm"] = r
        rows.append(("final_norm", r, dr["final_norm"].rearrange("(k p) -> k p", p=128), 8))
        r += 8
        for n, w in [("ssm_d", 3), ("pool_scale", 2), ("conv_b", 3), ("conv_ln_g", 3), ("conv_ln_b", 3)]:
            prow[n] = r
            rows.append((n, r, dr[n].rearrange("l (k p) -> (l k) p", p=128), DEPTH * w))
            r += DEPTH * w
        NPROW = r
        assert NPROW <= 128
        st_par = stage[3]
        for (n, r0, src, cnt) in rows:
            P.add('sp', e_.dma_start(out=st_par[r0:r0 + cnt, 0:128], in_=src),
                  writes=[('st', 3)], chan='par', kind='batch')
        bpar = psum()
        P.add('pe', e_.transpose(out=banks[bpar][:, 0:NPROW], in_=st_par[0:NPROW, 0:128],
                                          identity=ident[0:NPROW, 0:NPROW]),
              reads=[('st', 3), 'cst'], writes=[PS(bpar)])
        P.add('dve', e_.tensor_copy(out=pvec[:, 0:NPROW], in_=banks[bpar][:, 0:NPROW]),
              reads=[PS(bpar)], writes=['pvec'])

        def pcol(name, l, k, width):
            c = prow[name] + l * width + k
            return pvec[:, c:c + 1]

        def load_piece(src_ap, dst_view_fn, cast_out, cast_keys_w, cast_keys_r=()):
            s_ = stage_slot()
            dst = dst_view_fn(stage[s_])
            P.add('sp', e_.dma_start(out=dst, in_=src_ap), writes=[('st', s_)], chan='st%d' % s_)
            if s_ % 4 == 0:
                P.add('pool', e_.tensor_copy(out=cast_out, in_=dst), reads=[('st', s_)] + list(cast_keys_r),
                      writes=list(cast_keys_w))
            else:
                P.add('act', e_.copy(out=cast_out, in_=dst), reads=[('st', s_)] + list(cast_keys_r),
                      writes=list(cast_keys_w))

        def rmsnorm_tile(tt, gname, l, out_fn, out_keys, sqbuf, src=None, src_keys=None, ncols=TT, gw=8):
            if src is None:
                src = lambda k: xres[:, k, tt * TT:(tt + 1) * TT]
                src_keys = lambda k: ('x', k, tt)
            b = psum()
            for k in range(KT):
                sq = sqbuf[k % len(sqbuf)]
                P.add('act', e_.activation(out=sq[:, 0:ncols], in_=src(k), func=AF.Square),
                      reads=[src_keys(k)], writes=[('sq', id(sqbuf), k % len(sqbuf))])
                P.add('pe', e_.matmul(banks[b][:, 0:ncols], lhsT=ones_b, rhs=sq[:, 0:ncols],
                                                           start=(k == 0), stop=(k == KT - 1)),
                      reads=[('sq', id(sqbuf), k % len(sqbuf)), 'onesb'], writes=[PS(b)])
            P.add('act', e_.activation(out=banks[b][:, 0:ncols], in_=banks[b][:, 0:ncols], func=AF.Sqrt,
                                                bias=epsc, scale=1.0 / D),
                  reads=[PS(b), 'cst'], writes=[PS(b)])
            P.add('dve', e_.reciprocal(out=banks[b][:, 0:ncols], in_=banks[b][:, 0:ncols]),
                  reads=[PS(b)], writes=[PS(b)])
            for k in range(KT):
                P.add('dve', e_.scalar_tensor_tensor(out=out_fn(k), in0=src(k), scalar=pcol(gname, l, k, gw),
                                                                   in1=banks[b][:, 0:ncols], op0=ALU.mult, op1=ALU.mult),
                      reads=[src_keys(k), PS(b), 'pvec'], writes=[out_keys(k)])

        def residual_add(b, k, tt, scale):
            xs = xres[:, k, tt * TT:(tt + 1) * TT]
            P.add('dve', e_.scalar_tensor_tensor(out=xs, in0=banks[b][:, :], scalar=scale, in1=xs,
                                                          op0=ALU.mult, op1=ALU.add),
                  reads=[PS(b), ('x', k, tt)], writes=[('x', k, tt)])

        def proj(Wv, n_out_tiles, wp, rhs_fn, rhs_keys, ncols, handler, tag):
            def load(o):
                buf = o % 2
                load_piece(Wv[:, :, o * 128:(o + 1) * 128],
                           lambda st: st.rearrange("p (k n) -> p k n", k=KT),
                           wp[buf], [('wp', tag, buf)])
            load(0)
            for o in range(n_out_tiles):
                if o + 1 < n_out_tiles:
                    load(o + 1)
                buf = o % 2
                b = psum()
                for k in range(KT):
                    P.add('pe', e_.matmul(banks[b][:, 0:ncols], lhsT=wp[buf][:, k, :],
                                                                      rhs=rhs_fn(k), start=(k == 0), stop=(k == KT - 1)),
                          reads=[('wp', tag, buf), rhs_keys(k)], writes=[PS(b)])
                handler(o, b)

        def proj_res(Wb, wkey, n_out_tiles, rhs_fn, rhs_keys, ncols, handler):
            for o in range(n_out_tiles):
                b = psum()
                for k in range(KT):
                    P.add('pe', e_.matmul(banks[b][:, 0:ncols], lhsT=Wb[:, k, o * 128:(o + 1) * 128],
                                          rhs=rhs_fn(k), start=(k == 0), stop=(k == KT - 1)),
                          reads=[(wkey, k), rhs_keys(k)], writes=[PS(b)])
                handler(o, b)

        def ffn(l, which):
            gname = which + "_norm"
            wg = dr[which + "_w_gate"][l].rearrange("(k p) f -> p k f", p=128)
            wu = dr[which + "_w_up"][l].rearrange("(k p) f -> p k f", p=128)
            wd = dr[which + "_w_down"][l].rearrange("(f p) d -> p f d", p=128)
            cur[0] = D_off
            Wg = [carve(KT * 256, BF16).rearrange("p (k n) -> p k n", k=KT) for _ in range(2)]
            Wu = [carve(KT * 256, BF16).rearrange("p (k n) -> p k n", k=KT) for _ in range(2)]
            Wd = [carve(2 * D, BF16).rearrange("p (f n) -> p f n", f=2) for _ in range(2)]
            a_bf = [[carve(TT, BF16) for _ in range(2)] for _ in range(2)]
            sg = [carve(TT) for _ in range(2)]
            sqb = [carve(TT, BF16) for _ in range(4)]
            for tt in range(NTT):
                rmsnorm_tile(tt, gname, l, lambda k, tt=tt: hT[:, k, tt * TT:(tt + 1) * TT],
                             lambda k, tt=tt: ('h', k, tt), sqb)
            NCH = FF // 256

            def load_chunk(c):
                buf = c % 2
                for half in range(2):
                    for (W, Wb, nm) in ((wg, Wg, 'wg'), (wu, Wu, 'wu')):
                        load_piece(W[:, half * 4:(half + 1) * 4, c * 256:(c + 1) * 256],
                                   lambda st: st.rearrange("p (k n) -> p k n", k=4),
                                   Wb[buf][:, half * 4:(half + 1) * 4, :], [(nm, buf, half)])
                for fl in range(2):
                    load_piece(wd[:, c * 2 + fl, :], lambda st: st, Wd[buf][:, fl, :], [('wd', buf, fl)])

            pend = [None]
            un = [0]

            def down(c, tt, ab):
                buf = c % 2
                for dm in range(KT):
                    b = psum()
                    for fl in range(2):
                        P.add('pe', e_.matmul(banks[b][:, :], lhsT=Wd[buf][:, fl, dm * 128:(dm + 1) * 128],
                                                                          rhs=a_bf[ab][fl], start=(fl == 0), stop=(fl == 1)),
                              reads=[('wd', buf, fl), ('a', ab, fl)], writes=[PS(b)])
                    residual_add(b, dm, tt, 0.5)

            load_chunk(0)
            for c in range(NCH):
                buf = c % 2
                for tt in range(NTT):
                    ab = un[0] % 2
                    un[0] += 1
                    for fl in range(2):
                        bg = psum()
                        bu = psum()
                        for (Wb, nm, b) in ((Wg, 'wg', bg), (Wu, 'wu', bu)):
                            for k in range(KT):
                                P.add('pe', e_.matmul(
                                    banks[b][:, :], lhsT=Wb[buf][:, k, fl * 128:(fl + 1) * 128],
                                    rhs=hT[:, k, tt * TT:(tt + 1) * TT], start=(k == 0), stop=(k == KT - 1)),
                                    reads=[(nm, buf, k // 4), ('h', k, tt)], writes=[PS(b)])
                        P.add('act', e_.activation(out=sg[fl], in_=banks[bg][:, :], func=AF.Silu),
                              reads=[PS(bg)], writes=[('sg', fl)])
                        P.add('dve', e_.tensor_tensor(out=a_bf[ab][fl], in0=banks[bu][:, :],
                                                                                 in1=sg[fl], op=ALU.mult),
                              reads=[PS(bu), ('sg', fl)], writes=[('a', ab, fl)])
                    if pend[0] is not None:
                        down(*pend[0])
                    pend[0] = (c, tt, ab)
                    if tt == 0 and c + 1 < NCH:
                        load_chunk(c + 1)
            down(*pend[0])
            P.fence()

        def xattn(l, sq_i):
            cur[0] = D_off
            sqb = [carve(TT, BF16) for _ in range(2)]
            Kt = carve(KT * MEM, BF16).rearrange("p (o m) -> p o m", o=KT)
            Vb = carve(2 * D, BF16).rearrange("p (t n) -> p t n", t=2)
            rD = carve(TT)
            Wq_b = carve(KT * D, BF16).rearrange("p (k n) -> p k n", k=KT)
            Wo_b = carve(KT * D, BF16).rearrange("p (k n) -> p k n", k=KT)
            x_off = cur[0]
            wp = [carve(KT * 128, BF16).rearrange("p (k n) -> p k n", k=KT) for _ in range(2)]
            memT = carve(KT * MEM).rearrange("p (k m) -> p k m", k=KT)
            mT = carve(KT * MEM, BF16).rearrange("p (k m) -> p k m", k=KT)
            wvp = [carve(D, BF16) for _ in range(2)]
            x_end = cur[0]
            cur[0] = x_off
            qT = carve(KT * TT, BF16).rearrange("p (k t) -> p k t", k=KT)
            eT = carve(8 * TT, BF16).rearrange("p (j t) -> p j t", j=8)
            oT = carve(KT * TT, BF16).rearrange("p (k t) -> p k t", k=KT)
            cur[0] = max(cur[0], x_end)
            wq = dr["xattn_wq"][l].rearrange("(k p) n -> p k n", p=128)
            wo = dr["xattn_wo"][l].rearrange("(k p) n -> p k n", p=128)
            for mt in range(2):
                s_ = stage_slot()
                P.add('sp', e_.dma_start(out=stage[s_], in_=dr['mem'][sq_i, mt * 128:(mt + 1) * 128, :]),
                      writes=[('st', s_)], chan='st%d' % s_)
                for kh in range(2):
                    b = psum()
                    for kk in range(4):
                        k = kh * 4 + kk
                        P.add('pe', e_.transpose(out=banks[b][:, kk * 128:(kk + 1) * 128],
                                                                                in_=stage[s_][:, k * 128:(k + 1) * 128], identity=ident),
                              reads=[('st', s_), 'cst'], writes=[PS(b)])
                    P.add('act', e_.copy(out=memT[:, kh * 4:(kh + 1) * 4, mt * 128:(mt + 1) * 128],
                                                                     in_=banks[b][:, :].rearrange("p (k m) -> p k m", k=4)),
                          reads=[PS(b)], writes=[('memT', kh)])
            rmsnorm_tile(0, "mem_norm", l, lambda k: mT[:, k, :], lambda k: ('mT', k), sqb,
                         src=lambda k: memT[:, k, :], src_keys=lambda k: ('memT', k // 4), ncols=MEM)
            wk = dr["xattn_wk"][l].rearrange("(k p) n -> p k n", p=128)

            def k_handler(o, b):
                P.add('act', e_.activation(out=Kt[:, o, :], in_=banks[b][:, 0:MEM], func=AF.Copy, scale=0.0625),
                      reads=[PS(b)], writes=[('Kt', o)])
            proj(wk, KT, wp, lambda k: mT[:, k, :], lambda k: ('mT', k), MEM, k_handler, 'x')
            wv = dr["xattn_wv"][l].rearrange("(k p) n -> p k n", p=128)
            vb = [psum() for _ in range(4)]
            for k in range(KT):
                buf = k % 2
                load_piece(wv[:, k, :], lambda st: st, wvp[buf], [('wvp', buf)])
                for mt in range(2):
                    for ch in range(2):
                        b = vb[mt * 2 + ch]
                        P.add('pe', e_.matmul(
                            banks[b][:, :], lhsT=mT[:, k, mt * 128:(mt + 1) * 128], rhs=wvp[buf][:, ch * 512:(ch + 1) * 512],
                            start=(k == 0), stop=(k == KT - 1)),
                            reads=[('mT', k), ('wvp', buf)], writes=[PS(b)])
            for mt in range(2):
                for ch in range(2):
                    b = vb[mt * 2 + ch]
                    P.add('act', e_.copy(out=Vb[:, mt, ch * 512:(ch + 1) * 512], in_=banks[b][:, :]),
                          reads=[PS(b)], writes=[('V', mt)])
            for k in range(KT):
                load_piece(wq[:, k, :], lambda st: st, Wq_b[:, k, :], [('wq', k)])
            for k in range(KT):
                load_piece(wo[:, k, :], lambda st: st, Wo_b[:, k, :], [('wo', k)])
            P.fence()
            for tt in range(NTT):
                rmsnorm_tile(tt, "xattn_norm", l, lambda k: hT[:, k, 0:TT], lambda k: ('h', k, 0), sqb)

                def q_handler(o, b):
                    P.add('act', e_.copy(out=qT[:, o, :], in_=banks[b][:, :]), reads=[PS(b)], writes=[('qT', o)])
                proj_res(Wq_b, 'wq', KT, lambda k: hT[:, k, 0:TT], lambda k: ('h', k, 0), TT, q_handler)
                for h in range(4):
                    for mt in range(2):
                        b = psum()
                        for half in range(2):
                            P.add('pe', e_.matmul(
                                banks[b][:, :], lhsT=Kt[:, h * 2 + half, mt * 128:(mt + 1) * 128], rhs=qT[:, h * 2 + half, :],
                                start=(half == 0), stop=(half == 1)),
                                reads=[('Kt', h * 2 + half), ('qT', h * 2 + half)], writes=[PS(b)])
                        P.add('act', e_.activation(out=eT[:, h * 2 + mt, :], in_=banks[b][:, :], func=AF.Exp),
                              reads=[PS(b)], writes=[('eT', h * 2 + mt)])
                    bd = psum()
                    for mt in range(2):
                        P.add('pe', e_.matmul(banks[bd][:, :], lhsT=ones_b, rhs=eT[:, h * 2 + mt, :],
                                                                          start=(mt == 0), stop=(mt == 1)),
                              reads=[('eT', h * 2 + mt), 'onesb'], writes=[PS(bd)])
                    P.add('dve', e_.reciprocal(out=rD, in_=banks[bd][:, :]), reads=[PS(bd)], writes=['rD'])
                    for dh in range(2):
                        b = psum()
                        for mt in range(2):
                            P.add('pe', e_.matmul(
                                banks[b][:, :], lhsT=Vb[:, mt, h * 256 + dh * 128: h * 256 + (dh + 1) * 128], rhs=eT[:, h * 2 + mt, :],
                                start=(mt == 0), stop=(mt == 1)),
                                reads=[('V', mt), ('eT', h * 2 + mt)], writes=[PS(b)])
                        P.add('dve', e_.tensor_tensor(out=oT[:, h * 2 + dh, :], in0=banks[b][:, :], in1=rD, op=ALU.mult),
                              reads=[PS(b), 'rD'], writes=[('oT', h * 2 + dh)])

                def o_handler(o, b, tt=tt):
                    residual_add(b, o, tt, 1.0)
                proj_res(Wo_b, 'wo', KT, lambda k: oT[:, k, :], lambda k: ('oT', k), TT, o_handler)
            P.fence()

        def mix(l, sq_i):
            import os
            STOP = os.environ.get("MIXSTOP", "Z")
            cur[0] = D_off
            wp = [carve(KT * 128, BF16).rearrange("p (k n) -> p k n", k=KT) for _ in range(2)]
            sqb = [carve(TT, BF16) for _ in range(2)]

            def b2(i):
                return arena[:, h_off + i * 1024 + 512:h_off + (i + 1) * 1024]
            sm = b2(7)
            smc = [0]

            def small(n=NGP):
                a = sm[:, smc[0]:smc[0] + n]
                smc[0] += 16
                assert smc[0] <= 512
                return a
            rotc = small(); rots = small(); car_r = small(); car_i = small()
            cr_t = [small() for _ in range(4)]
            magT = carve(NGP * FR).rearrange("p (g t) -> p g t", g=NGP)
            tabc = carve(NGP * FR, BF16).rearrange("p (g t) -> p g t", g=NGP)
            tabs = carve(NGP * FR, BF16).rearrange("p (g t) -> p g t", g=NGP)
            lB = carve(NGP * 2 * 128, BF16).rearrange("p (g c n) -> p g c n", g=NGP, c=2)
            lC = carve(NGP * 3 * 128, BF16).rearrange("p (g c n) -> p g c n", g=NGP, c=3)
            dgd = carve(3 * 128, BF16).rearrange("p (c n) -> p c n", c=3)
            wglu = carve(3 * 384, BF16).rearrange("p (c n) -> p c n", c=3)
            BDp = carve(2 * 128, BF16).rearrange("p (c n) -> p c n", c=2)
            cwT = carve(3 * 32).rearrange("p (c j) -> p c j", c=3)
            dg = carve(CONVW * 128, BF16).rearrange("p (j n) -> p j n", j=CONVW)
            sgc = b2(0); cf2 = b2(1); lnm = b2(2); lnr = b2(3)
            pin = b2(4).bitcast(BF16).rearrange("p (c t) -> p c t", c=2)
            zin = [b2(5)[:, 0:256].bitcast(BF16), b2(5)[:, 256:512].bitcast(BF16)]
            ztmp = [b2(6)[:, 0:256].bitcast(BF16), b2(6)[:, 256:512].bitcast(BF16)]
            work_off = cur[0]
            lam_r = small(); lam_i = small(); dtv = small(); mag = small()
            ar_ = small(); ai_ = small(); zr = small(); zi = small(); fq = small()
            tA = small(); tB = small(); tC = small()
            ph = carve(NGP * FR).rearrange("p (g t) -> p g t", g=NGP)
            ph2 = carve(NGP * FR).rearrange("p (g t) -> p g t", g=NGP)
            Bn_r = carve(NGP * 16).rearrange("p (g k) -> p g k", g=NGP)
            Bn_i = carve(NGP * 16).rearrange("p (g k) -> p g k", g=NGP)
            Bb_r = carve(NGP * 16).rearrange("p (g k) -> p g k", g=NGP)
            Bb_i = carve(NGP * 16).rearrange("p (g k) -> p g k", g=NGP)
            BD = carve(NGP * 2 * 128).rearrange("p (g c k) -> p g c k", g=NGP, c=2)
            Cn = carve(2 * NGP * 64).rearrange("p (c g n) -> p c g n", c=2, g=NGP)
            CD = carve(128)
            st_l = carve(896)
            st_p = carve(256)
            prep_end = cur[0]
            cur[0] = work_off
            u_bf = carve(3 * TT, BF16).rearrange("p (c t) -> p c t", c=3)
            zz = [carve(4 * FR).rearrange("p (g t) -> p g t", g=4) for _ in range(2)]
            pp = [carve(4 * FR, BF16).rearrange("p (g t) -> p g t", g=4) for _ in range(4)]
            ypre = carve(3 * TT).rearrange("p (c t) -> p c t", c=3)
            ytmp = carve(3 * TT).rearrange("p (c t) -> p c t", c=3)
            yg_bf = carve(3 * TT, BF16).rearrange("p (c t) -> p c t", c=3)
            up = carve(2 * (16 + TT)).rearrange("p (c t) -> p c t", c=2)
            sA = carve(16 + TT); sB = carve(16 + TT)
            hc = carve(3 * (32 + TT), BF16).rearrange("p (c t) -> p c t", c=3)
            cf = carve(3 * TT).rearrange("p (c t) -> p c t", c=3)
            cur[0] = max(cur[0], prep_end)
            ymix = hT[:, :, TT:2 * TT]

            ch = 'lp0'
            for c_ in ('lp0', 'lp1', 'lp2', 'lp3'):
                P.chan_sync(c_)
            P.add('sp', e_.dma_start(out=st_l[0:12, 0:128], in_=dr['ssm_lambda_re'][l].rearrange("(g two) p -> g (two p)", two=2)),
                  writes=['stl_a'], chan=ch, kind='batch')
            P.add('sp', e_.dma_start(out=st_l[0:12, 128:256], in_=dr['ssm_lambda_im'][l].rearrange("(g two) p -> g (two p)", two=2)),
                  writes=['stl_b'], chan=ch, kind='batch')
            P.add('sp', e_.dma_start(out=st_l[0:12, 768:770], in_=dr['ssm_log_dt'][l].rearrange("(g two) -> g two", two=2)),
                  writes=['stl_c'], chan=ch, kind='batch')
            P.add('pool', e_.memset(st_l[0:32, 256:640], 0.0), writes=['stl_d'])
            P.add('sp', e_.dma_start(out=st_l[0:31, 256:640], in_=dr['conv_w'][l]), writes=['stl_d'], chan=ch, kind='batch')
            for two in range(2):
                P.add('dve', e_.tensor_scalar(out=st_l[0:12, 640 + two * 64:640 + (two + 1) * 64], in0=ones_f[0:12, 0:64],
                                              scalar1=st_l[0:12, 768 + two:769 + two], scalar2=None, op0=ALU.mult),
                      reads=['stl_c', 'cst'], writes=['stl_e%d' % two])
            slk = ['stl_a', 'stl_b', 'stl_c', 'stl_d', 'stl_e0', 'stl_e1']
            if STOP == 'A1b':
                P.fence()
                return
            bq = psum()
            for i, c0 in enumerate((0, 128, 640)):
                P.add('pe', e_.transpose(out=banks[bq][:, i * 16:i * 16 + 12], in_=st_l[0:12, c0:c0 + 128],
                                         identity=ident[0:12, 0:12]),
                      reads=slk + ['cst'], writes=[PS(bq)])
            for ct in range(3):
                P.add('pe', e_.transpose(out=banks[bq][:, 64 + ct * 32:64 + ct * 32 + 32],
                                                         in_=st_l[0:32, 256 + ct * 128:256 + (ct + 1) * 128], identity=ident[0:32, 0:32]),
                      reads=slk + ['cst'], writes=[PS(bq)])
            if STOP == 'A1c':
                P.add('dve', e_.tensor_copy(out=CD[:, 0:12], in_=banks[bq][:, 0:12]), reads=[PS(bq)], writes=['s5p'])
                P.fence()
                return
            P.add('dve', e_.tensor_copy(out=lam_r, in_=banks[bq][:, 0:12]), reads=[PS(bq)], writes=['s5p'])
            P.add('dve', e_.tensor_copy(out=lam_i, in_=banks[bq][:, 16:28]), reads=[PS(bq)], writes=['s5p'])
            if STOP == 'A1d':
                P.fence()
                return
            P.add('act', e_.activation(out=dtv, in_=banks[bq][:, 32:44], func=AF.Exp), reads=[PS(bq)], writes=['s5p1'])
            if STOP == 'A1e':
                P.fence()
                return
            P.add('dve', e_.tensor_copy(out=cwT.rearrange("p c j -> p (c j)"), in_=banks[bq][:, 64:160]),
                  reads=[PS(bq)], writes=['cwT'])
            if STOP == 'A1':
                P.fence()
                return
            SK = ['s5p', 's5p1', 's5p2']

            def dv(fn, r=SK, w=('s5p2',)):
                P.add('dve', fn, reads=list(r), writes=list(w))
            dv(e_.tensor_tensor(out=tA, in0=lam_r, in1=dtv, op=ALU.mult))
            P.add('act', e_.activation(out=mag, in_=tA, func=AF.Exp), reads=SK, writes=['s5p1'])
            dv(e_.scalar_tensor_tensor(out=fq, in0=lam_i, scalar=float(1.0 / (2 * np.pi)), in1=dtv, op0=ALU.mult, op1=ALU.mult))
            for g in range(NGP):
                dv(e_.tensor_scalar(out=ph[:, g, :], in0=iota1, scalar1=fq[:, g:g + 1], scalar2=None, op0=ALU.mult),
                   r=SK + ['cst', 'tabf'], w=['tabf'])
            phf = ph.rearrange("p g t -> p (g t)")
            ph2f = ph2.rearrange("p g t -> p (g t)")
            dv(e_.tensor_scalar(out=ph2f, in0=phf, scalar1=MAGIC, scalar2=MAGIC, op0=ALU.add, op1=ALU.subtract),
               r=['tabf'], w=['tabf2'])
            dv(e_.tensor_tensor(out=ph2f, in0=phf, in1=ph2f, op=ALU.subtract), r=['tabf', 'tabf2'], w=['tabf2'])
            P.add('act', e_.activation(out=ph2f, in_=ph2f, func=AF.Sin, scale=float(2 * np.pi)), reads=['tabf2'], writes=['tabf2'])
            dv(e_.tensor_copy(out=tabs.rearrange("p g t -> p (g t)"), in_=ph2f), r=['tabf2'], w=['tabs'])
            dv(e_.tensor_copy(out=rots, in_=ph2[:, :, FR - 1]), r=['tabf2'], w=['rot'])
            dv(e_.tensor_copy(out=tB, in_=ph2[:, :, 0]), r=['tabf2'], w=['s5p2'])
            dv(e_.tensor_scalar(out=phf, in0=phf, scalar1=0.25, scalar2=None, op0=ALU.add), r=['tabf'], w=['tabf'])
            dv(e_.tensor_scalar(out=ph2f, in0=phf, scalar1=MAGIC, scalar2=MAGIC, op0=ALU.add, op1=ALU.subtract),
               r=['tabf', 'tabf2', 'tabs', 'rot', 's5p2'], w=['tabf2'])
            dv(e_.tensor_tensor(out=ph2f, in0=phf, in1=ph2f, op=ALU.subtract), r=['tabf', 'tabf2'], w=['tabf2'])
            P.add('act', e_.activation(out=ph2f, in_=ph2f, func=AF.Sin, scale=float(2 * np.pi)), reads=['tabf2'], writes=['tabf2'])
            dv(e_.tensor_copy(out=tabc.rearrange("p g t -> p (g t)"), in_=ph2f), r=['tabf2'], w=['tabc'])
            dv(e_.tensor_copy(out=rotc, in_=ph2[:, :, FR - 1]), r=['tabf2'], w=['rot'])
            dv(e_.tensor_copy(out=tC, in_=ph2[:, :, 0]), r=['tabf2'], w=['s5p2'])
            for g in range(NGP):
                dv(e_.tensor_scalar(out=magT[:, g, :], in0=ones_f, scalar1=mag[:, g:g + 1], scalar2=None, op0=ALU.mult),
                   r=SK + ['cst'], w=['magT'])
            dv(e_.tensor_tensor(out=ar_, in0=mag, in1=tC, op=ALU.mult))
            dv(e_.tensor_tensor(out=ai_, in0=mag, in1=tB, op=ALU.mult))
            dv(e_.tensor_tensor(out=tA, in0=lam_r, in1=lam_r, op=ALU.mult))
            dv(e_.tensor_tensor(out=tB, in0=lam_i, in1=lam_i, op=ALU.mult))
            dv(e_.tensor_tensor(out=tA, in0=tA, in1=tB, op=ALU.add))
            dv(e_.reciprocal(out=tA, in_=tA))
            dv(e_.tensor_scalar(out=tB, in0=ar_, scalar1=-1.0, scalar2=None, op0=ALU.add))
            dv(e_.tensor_tensor(out=zr, in0=tB, in1=lam_r, op=ALU.mult))
            dv(e_.tensor_tensor(out=tC, in0=ai_, in1=lam_i, op=ALU.mult))
            dv(e_.tensor_tensor(out=zr, in0=zr, in1=tC, op=ALU.add))
            dv(e_.tensor_tensor(out=zr, in0=zr, in1=tA, op=ALU.mult))
            dv(e_.tensor_tensor(out=zi, in0=ai_, in1=lam_r, op=ALU.mult))
            dv(e_.tensor_tensor(out=tC, in0=tB, in1=lam_i, op=ALU.mult))
            dv(e_.tensor_tensor(out=zi, in0=zi, in1=tC, op=ALU.subtract))
            dv(e_.tensor_tensor(out=zi, in0=zi, in1=tA, op=ALU.mult))
            if STOP == 'A2':
                P.fence()
                return
            for (src, dstt) in (('ssm_b_re', Bn_r), ('ssm_b_im', Bn_i)):
                for two in range(2):
                    P.add('sp', e_.dma_start(
                        out=dstt[two * 64:(two + 1) * 64, :, :],
                        in_=dr[src][l].rearrange("(g two) p k -> two p g k", two=2)[two]),
                        writes=['Bn'], chan='lp2', kind='batch')
            zrb = ph[:, :, 0:16]
            zib = ph2[:, :, 0:16]
            for g in range(NGP):
                dv(e_.tensor_scalar(out=ph[:, g, 0:16], in0=ones_f[:, 0:16], scalar1=zr[:, g:g + 1], scalar2=None, op0=ALU.mult),
                   r=SK + ['cst', 'tabf', 'tabf2', 'tabs', 'tabc', 'rot'], w=['tabf'])
                dv(e_.tensor_scalar(out=ph2[:, g, 0:16], in0=ones_f[:, 0:16], scalar1=zi[:, g:g + 1], scalar2=None, op0=ALU.mult),
                   r=SK + ['cst', 'tabf', 'tabf2', 'tabs', 'tabc', 'rot'], w=['tabf2'])
            BK = SK + ['Bn', 'Bb', 'tabf', 'tabf2']
            dv(e_.tensor_tensor(out=Bb_r, in0=Bn_r, in1=zrb, op=ALU.mult), r=BK, w=['Bb'])
            dv(e_.tensor_tensor(out=Bb_i, in0=Bn_i, in1=zib, op=ALU.mult), r=BK, w=['Bb'])
            dv(e_.tensor_tensor(out=Bb_r, in0=Bb_r, in1=Bb_i, op=ALU.subtract), r=BK, w=['Bb'])
            dv(e_.tensor_tensor(out=Bb_i, in0=Bn_i, in1=zrb, op=ALU.mult), r=BK, w=['Bb'])
            dv(e_.tensor_tensor(out=Bn_r, in0=Bn_r, in1=zib, op=ALU.mult), r=BK, w=['Bn'])
            dv(e_.tensor_tensor(out=Bb_i, in0=Bb_i, in1=Bn_r, op=ALU.add), r=BK, w=['Bb'])
            dv(e_.memset(BD.rearrange("p g c k -> p (g c k)"), 0.0), r=[], w=['BD'])
            for c, Bb in enumerate((Bb_r, Bb_i)):
                for two in range(2):
                    for j in range(4):
                        c0 = j * 32 + two * 16
                        dv(e_.tensor_copy(out=BD[two * 64:(two + 1) * 64, j::4, c, c0:c0 + 16],
                                          in_=Bb[two * 64:(two + 1) * 64, j::4, :]),
                           r=['Bb', 'BD'], w=['BD'])
            for g in range(NGP):
                bt = psum()
                for c in range(2):
                    P.add('pe', e_.transpose(out=banks[bt][:, c * 128:(c + 1) * 128], in_=BD[:, g, c, :], identity=ident),
                          reads=['BD', 'cst'], writes=[PS(bt)])
                P.add('act', e_.copy(out=lB[:, g, :, :], in_=banks[bt][:, 0:256].rearrange("p (c n) -> p c n", c=2)),
                      reads=[PS(bt)], writes=['lB'])
            if STOP == 'A3':
                P.fence()
                return
            for c, src in enumerate(('ssm_c_re', 'ssm_c_im')):
                P.add('sp', e_.dma_start(out=Cn[0:32, c, :, :],
                                                                in_=dr[src][l].rearrange("(g two) k p -> (two k) g p", two=2)),
                      writes=['Cn'], chan='lp3', kind='batch')
            P.add('pool', e_.memset(lC.rearrange("p g c n -> p (g c n)"), 0.0), writes=['lC'])
            for g in range(NGP):
                j4 = (g % 4) * 32
                for c in range(2):
                    dv(e_.tensor_scalar(out=CD[0:32, 0:64], in0=Cn[0:32, c, g, :], scalar1=m0[0:32, :], scalar2=None, op0=ALU.mult),
                       r=['Cn', 'cst', 'CD'], w=['CD'])
                    dv(e_.tensor_scalar(out=CD[0:32, 64:128], in0=Cn[0:32, c, g, :], scalar1=m1[0:32, :], scalar2=None, op0=ALU.mult),
                       r=['Cn', 'cst', 'CD'], w=['CD'])
                    bt = psum()
                    P.add('pe', e_.transpose(out=banks[bt][:, 0:32], in_=CD[0:32, :], identity=ident[0:32, 0:32]),
                          reads=['CD', 'cst'], writes=[PS(bt)])
                    if c == 0:
                        P.add('act', e_.copy(out=lC[:, g, 0, j4:j4 + 32], in_=banks[bt][:, 0:32]), reads=[PS(bt), 'lC'], writes=['lC'])
                        P.add('act', e_.activation(out=lC[:, g, 1, j4:j4 + 32], in_=banks[bt][:, 0:32], func=AF.Copy, scale=-1.0),
                              reads=[PS(bt), 'lC'], writes=['lC'])
                    else:
                        P.add('act', e_.activation(out=lC[:, g, 2, j4:j4 + 32], in_=banks[bt][:, 0:32], func=AF.Copy, scale=-1.0),
                              reads=[PS(bt), 'lC'], writes=['lC'])
            if STOP == 'A4':
                P.fence()
                return
            for ct in range(3):
                dv(e_.tensor_scalar(out=dgd[:, ct, :], in0=ident, scalar1=pcol('ssm_d', l, ct, 3), scalar2=None, op0=ALU.mult),
                   r=['cst', 'pvec'], w=['dgd'])
                load_piece(dr['ssm_w_glu'][l][ct * 128:(ct + 1) * 128, :], lambda st: st[:, 0:384], wglu[:, ct, :], ['wglu'])
            P.add('pool', e_.memset(st_p[:, 0:256], 0.0), writes=['stp'])
            spks = []
            for g in range(4):
                t_, h_ = g // 2, g % 2
                P.add('sp', e_.dma_start(out=st_p[h_ * 64:(h_ + 1) * 64, t_ * 128 + h_ * 64:t_ * 128 + (h_ + 1) * 64],
                                                                     in_=dr['pool_w'][l, g]),
                      reads=['stp'], writes=['stp%d' % g], chan='lp1', kind='batch')
                spks.append('stp%d' % g)
            P.add('pool', e_.tensor_copy(out=BDp.rearrange("p c n -> p (c n)"), in_=st_p[:, 0:256]), reads=['stp'] + spks, writes=['BDp'])

            P.fence()
            if STOP == 'A':
                return
            dv(e_.memset(car_r, 0.0), r=[], w=['car'])
            dv(e_.memset(car_i, 0.0), r=[], w=['car'])
            P.add('pool', e_.memset(up[:, :, 0:16], 0.0), writes=['up'])
            P.add('pool', e_.memset(hc[:, :, 0:32], 0.0), writes=['hc'])

            win = dr["w_in"][l].rearrange("(k p) n -> p k n", p=128)
            wout = dr["w_out"][l].rearrange("(k p) n -> p k n", p=128)
            for tt in range(NTT):
                rmsnorm_tile(tt, "mix_norm", l, lambda k: hT[:, k, 0:TT], lambda k: ('h', k, 0), sqb)

                vbank = {}

                def in_handler(o, b, tt=tt):
                    if o < 3:
                        P.add('act', e_.copy(out=u_bf[:, o, :], in_=banks[b][:, :]), reads=[PS(b)], writes=[('u', o)])
                    elif o < 5:
                        P.add('act', e_.copy(out=up[:, o - 3, 16:16 + TT], in_=banks[b][:, :]), reads=[PS(b), 'up'], writes=['up'])
                    elif o < 8:
                        P.add('act', e_.copy(out=cf[:, o - 5, :], in_=banks[b][:, :]), reads=[PS(b)], writes=[('cf', o - 5)])
                    else:
                        ct = o - 8
                        P.add('act', e_.activation(out=sgc, in_=banks[b][:, :], func=AF.Sigmoid), reads=[PS(b)], writes=['sgc'])
                        P.add('dve', e_.tensor_tensor(out=hc[:, ct, 32:32 + TT], in0=cf[:, ct, :], in1=sgc, op=ALU.mult),
                              reads=[('cf', ct), 'sgc', 'hc'], writes=['hc'])
                proj(win, 11, wp, lambda k: hT[:, k, 0:TT], lambda k: ('h', k, 0), TT, in_handler, 'm')

                if STOP == 'D':
                    break
                for t_ in range(2):
                    U = up[:, t_, :]
                    NP_ = 16 + TT
                    P.add('pool', e_.tensor_tensor(out=sA[:, 1:NP_], in0=U[:, 1:NP_], in1=U[:, 0:NP_ - 1], op=ALU.add),
                          reads=['up', 'sA'], writes=['sA'])
                    P.add('pool', e_.tensor_tensor(out=sB[:, 3:NP_], in0=sA[:, 3:NP_], in1=sA[:, 1:NP_ - 2], op=ALU.add),
                          reads=['sA', 'sB'], writes=['sB'])
                    if t_ == 0:
                        wins = ((sA, 2, 0), (sB, 4, 1))
                    else:
                        P.add('pool', e_.tensor_tensor(out=sA[:, 7:NP_], in0=sB[:, 7:NP_], in1=sB[:, 3:NP_ - 4], op=ALU.add),
                              reads=['sA', 'sB'], writes=['sA'])
                        P.add('pool', e_.tensor_tensor(out=sB[:, 15:NP_], in0=sA[:, 15:NP_], in1=sA[:, 7:NP_ - 8], op=ALU.add),
                              reads=['sA', 'sB'], writes=['sB'])
                        wins = ((sA, 8, 0), (sB, 16, 1))
                    for (sw, w_, hf) in wins:
                        ps_ = slice(hf * 64, (hf + 1) * 64)
                        dv(e_.scalar_tensor_tensor(
                            out=pin[ps_, t_, :], in0=sw[ps_, 16:16 + TT], scalar=1.0 / w_, in1=U[ps_, 16:16 + TT], op0=ALU.mult, op1=ALU.subtract),
                           r=['sA', 'sB', 'up', 'pin'], w=['pin'])
                        if tt == 0:
                            wi = {2: 0, 4: 1, 8: 2, 16: 3}[w_]
                            itab = cst[ps_, 400 + wi * 16:400 + (wi + 1) * 16]
                            dv(e_.tensor_tensor(out=lnm[ps_, 0:16], in0=sw[ps_, 16:32], in1=itab, op=ALU.mult),
                               r=['sA', 'sB', 'cst', 'lnm'], w=['lnm'])
                            dv(e_.tensor_tensor(out=pin[ps_, t_, 0:16], in0=lnm[ps_, 0:16], in1=U[ps_, 16:32], op=ALU.subtract),
                               r=['lnm', 'up', 'pin'], w=['pin'])
                    b = psum()
                    P.add('pe', e_.matmul(banks[b][:, :], lhsT=BDp[:, t_, :], rhs=pin[:, t_, :], start=True, stop=True),
                          reads=['BDp', 'pin'], writes=[PS(b)])
                    dv(e_.tensor_scalar(out=ymix[:, 3 + t_, :], in0=banks[b][:, :], scalar1=pcol('pool_scale', l, t_, 2), scalar2=None, op0=ALU.mult),
                       r=[PS(b), 'pvec'], w=[('ym', 3 + t_)])
                P.add('pool', e_.tensor_copy(out=up[:, :, 0:16], in_=up[:, :, TT:TT + 16]), reads=['up', 'pin'], writes=['up'])
                if STOP == 'E':
                    break
                for ct in range(3):
                    for j in range(CONVW):
                        dv(e_.tensor_scalar(out=dg[:, j, :], in0=ident, scalar1=cwT[:, ct, j:j + 1], scalar2=None, op0=ALU.mult),
                           r=['cwT', 'cst', 'dg'], w=['dg'])
                    b = psum()
                    for j in range(CONVW):
                        P.add('pe', e_.matmul(banks[b][:, :], lhsT=dg[:, j, :], rhs=hc[:, ct, 2 + j:2 + j + TT],
                                                                        start=(j == 0), stop=(j == CONVW - 1)),
                              reads=['dg', 'hc'], writes=[PS(b)])
                    dv(e_.tensor_scalar(out=cf[:, ct, :], in0=banks[b][:, :], scalar1=pcol('conv_b', l, ct, 3), scalar2=None, op0=ALU.add),
                       r=[PS(b), 'pvec'], w=[('cf', ct)])
                P.add('pool', e_.tensor_copy(out=hc[:, :, 0:32], in_=hc[:, :, TT:TT + 32]), reads=['hc'], writes=['hc'])
                b1 = psum(); b2 = psum()
                cfb = cf2.bitcast(BF16)[:, 0:TT]
                cfq = cf2.bitcast(BF16)[:, TT:2 * TT]
                for ct in range(3):
                    P.add('act', e_.copy(out=cfb, in_=cf[:, ct, :]), reads=[('cf', ct)], writes=['cfb'])
                    P.add('pe', e_.matmul(banks[b1][:, :], lhsT=ones_b, rhs=cfb, start=(ct == 0), stop=(ct == 2)),
                          reads=['cfb', 'onesb'], writes=[PS(b1)])
                    P.add('act', e_.activation(out=cfq, in_=cf[:, ct, :], func=AF.Square), reads=[('cf', ct)], writes=['cfq'])
                    P.add('pe', e_.matmul(banks[b2][:, :], lhsT=ones_b, rhs=cfq, start=(ct == 0), stop=(ct == 2)),
                          reads=['cfq', 'onesb'], writes=[PS(b2)])
                P.add('act', e_.activation(out=lnm, in_=banks[b1][:, :], func=AF.Copy, scale=1.0 / D_CONV), reads=[PS(b1)], writes=['lnm'])
                dv(e_.tensor_tensor(out=lnr, in0=lnm, in1=lnm, op=ALU.mult), r=['lnm'], w=['lnr'])
                dv(e_.scalar_tensor_tensor(out=lnr, in0=banks[b2][:, :], scalar=1.0 / D_CONV, in1=lnr, op0=ALU.mult, op1=ALU.subtract),
                   r=[PS(b2), 'lnr'], w=['lnr'])
                P.add('act', e_.activation(out=lnr, in_=lnr, func=AF.Sqrt, bias=epsc, scale=1.0), reads=['lnr', 'cst'], writes=['lnr'])
                dv(e_.reciprocal(out=lnr, in_=lnr), r=['lnr'], w=['lnr'])
                for ct in range(3):
                    dv(e_.tensor_tensor(out=cf[:, ct, :], in0=cf[:, ct, :], in1=lnm, op=ALU.subtract), r=[('cf', ct), 'lnm'], w=[('cf', ct)])
                    dv(e_.tensor_tensor(out=cf[:, ct, :], in0=cf[:, ct, :], in1=lnr, op=ALU.mult), r=[('cf', ct), 'lnr'], w=[('cf', ct)])
                    dv(e_.tensor_scalar(out=cf[:, ct, :], in0=cf[:, ct, :], scalar1=pcol('conv_ln_g', l, ct, 3),
                                        scalar2=pcol('conv_ln_b', l, ct, 3), op0=ALU.mult, op1=ALU.add),
                       r=[('cf', ct), 'pvec'], w=[('cf', ct)])
                    P.add('act', e_.activation(out=ymix[:, 5 + ct, :], in_=cf[:, ct, :], func=AF.Silu),
                          reads=[('cf', ct)], writes=[('ym', 5 + ct)])
                if STOP == 'B':
                    break
                def emit_B(fr, gq):
                    bR = psum(); bI = psum()
                    for j in range(4):
                        g = gq * 4 + j
                        ct_u = g // 4
                        urows = u_bf[:, ct_u, fr * FR:(fr + 1) * FR]
                        for c, bb in ((0, bR), (1, bI)):
                            P.add('pe', e_.matmul(
                                banks[bb][:, j * FR:(j + 1) * FR], lhsT=lB[:, g, c, :], rhs=urows,
                                start=True, stop=True),
                                reads=['lB', ('u', ct_u)], writes=[PS(bb)])
                    return bR, bI
                NFR = TT // FR
                nextB = emit_B(0, 0)
                for fr in range(NFR):
                    by = psum()
                    for gq in range(3):
                        bR, bI = nextB
                        if gq < 2:
                            nextB = emit_B(fr, gq + 1)
                        elif fr + 1 < NFR:
                            nextB = emit_B(fr + 1, 0)
                        zb = gq % 2
                        tc_ = tabc[:, gq * 4:(gq + 1) * 4, :].rearrange("p g t -> p (g t)")
                        ts_ = tabs[:, gq * 4:(gq + 1) * 4, :].rearrange("p g t -> p (g t)")
                        zr_in, zi_in = zin[0], zin[1]
                        dv(e_.tensor_tensor(out=zr_in, in0=banks[bR][:, :], in1=tc_, op=ALU.mult),
                           r=[PS(bR), 'tabc', 'zin0'], w=['zin0'])
                        dv(e_.tensor_tensor(out=ztmp[0], in0=banks[bI][:, :], in1=ts_, op=ALU.mult),
                           r=[PS(bI), 'tabs', 'zt0'], w=['zt0'])
                        dv(e_.tensor_tensor(out=zi_in, in0=banks[bI][:, :], in1=tc_, op=ALU.mult),
                           r=[PS(bI), 'tabc', 'zin1'], w=['zin1'])
                        dv(e_.tensor_tensor(out=ztmp[1], in0=banks[bR][:, :], in1=ts_, op=ALU.mult),
                           r=[PS(bR), 'tabs', 'zt1'], w=['zt1'])
                        dv(e_.tensor_tensor(out=zr_in, in0=zr_in, in1=ztmp[0], op=ALU.add), r=['zin0', 'zt0'], w=['zin0'])
                        dv(e_.tensor_tensor(out=zi_in, in0=zi_in, in1=ztmp[1], op=ALU.subtract), r=['zin1', 'zt1'], w=['zin1'])
                        for j in range(4):
                            g = gq * 4 + j
                            dv(e_.tensor_tensor_scan(out=zz[0][:, j, :], data0=magT[:, g, :], data1=zr_in[:, j * FR:(j + 1) * FR],
                                                                        initial=car_r[:, g:g + 1], op0=ALU.mult, op1=ALU.add),
                               r=['magT', 'zin0', 'car', 'zz0'], w=['zz0'])
                            dv(e_.tensor_tensor_scan(out=zz[1][:, j, :], data0=magT[:, g, :], data1=zi_in[:, j * FR:(j + 1) * FR],
                                                                        initial=car_i[:, g:g + 1], op0=ALU.mult, op1=ALU.add),
                               r=['magT', 'zin1', 'car', 'zz1'], w=['zz1'])
                        tc3 = tabc[:, gq * 4:(gq + 1) * 4, :]
                        ts3 = tabs[:, gq * 4:(gq + 1) * 4, :]
                        dv(e_.tensor_tensor(out=pp[0], in0=zz[0], in1=tc3, op=ALU.mult), r=['zz0', 'tabc', 'pp0'], w=['pp0'])
                        dv(e_.tensor_tensor(out=pp[1], in0=zz[1], in1=ts3, op=ALU.mult), r=['zz1', 'tabs', 'pp1'], w=['pp1'])
                        dv(e_.tensor_tensor(out=pp[2], in0=zz[0], in1=ts3, op=ALU.mult), r=['zz0', 'tabs', 'pp2'], w=['pp2'])
                        dv(e_.tensor_tensor(out=pp[3], in0=zz[1], in1=tc3, op=ALU.mult), r=['zz1', 'tabc', 'pp3'], w=['pp3'])
                        gs = slice(gq * 4, gq * 4 + 4)
                        zrl = zz[0][:, :, FR - 1]
                        zil = zz[1][:, :, FR - 1]
                        P.add('pool', e_.tensor_tensor(out=cr_t[0][:, gs], in0=rotc[:, gs], in1=zrl, op=ALU.mult),
                              reads=['rot', 'zz0', 'crt'], writes=['crt'])
                        P.add('pool', e_.tensor_tensor(out=cr_t[1][:, gs], in0=rots[:, gs], in1=zil, op=ALU.mult),
                              reads=['rot', 'zz1', 'crt'], writes=['crt'])
                        P.add('pool', e_.tensor_tensor(out=cr_t[2][:, gs], in0=rots[:, gs], in1=zrl, op=ALU.mult),
                              reads=['rot', 'zz0', 'crt'], writes=['crt'])
                        P.add('pool', e_.tensor_tensor(out=cr_t[3][:, gs], in0=rotc[:, gs], in1=zil, op=ALU.mult),
                              reads=['rot', 'zz1', 'crt'], writes=['crt'])
                        P.add('pool', e_.tensor_tensor(out=car_r[:, gs], in0=cr_t[0][:, gs], in1=cr_t[1][:, gs], op=ALU.subtract),
                              reads=['crt', 'car'], writes=['car'])
                        P.add('pool', e_.tensor_tensor(out=car_i[:, gs], in0=cr_t[2][:, gs], in1=cr_t[3][:, gs], op=ALU.add),
                              reads=['crt', 'car'], writes=['car'])
                        for j in range(4):
                            g = gq * 4 + j
                            for i, (pi, ci) in enumerate(((0, 0), (1, 1), (2, 2), (3, 2))):
                                P.add('pe', e_.matmul(
                                    banks[by][:, gq * FR:(gq + 1) * FR], lhsT=lC[:, g, ci, :], rhs=pp[pi][:, j, :],
                                    start=(i == 0 and j == 0), stop=False),
                                    reads=['lC', 'pp%d' % pi], writes=[PS(by)])
                        P.add('pe', e_.matmul(banks[by][:, gq * FR:(gq + 1) * FR], lhsT=dgd[:, gq, :],
                                              rhs=u_bf[:, gq, fr * FR:(fr + 1) * FR], start=False, stop=True),
                              reads=['dgd', ('u', gq)], writes=[PS(by)])
                    P.add('act', e_.copy(out=ypre[:, :, fr * FR:(fr + 1) * FR],
                                                               in_=banks[by][:, 0:3 * FR].rearrange("p (c t) -> p c t", c=3)),
                          reads=[PS(by)], writes=['ypre'])
                if STOP == 'C':
                    break
                ypf = ypre.rearrange("p c t -> p (c t)")
                ytf = ytmp.rearrange("p c t -> p (c t)")
                P.add('act', e_.activation(out=ytf, in_=ypf, func=AF.Square), reads=['ypre'], writes=['ytmp'])
                dv(e_.tensor_scalar(out=ytf, in0=ytf, scalar1=0.044715, scalar2=1.0, op0=ALU.mult, op1=ALU.add), r=['ytmp'], w=['ytmp'])
                dv(e_.tensor_tensor(out=ytf, in0=ytf, in1=ypf, op=ALU.mult), r=['ytmp', 'ypre'], w=['ytmp'])
                P.add('act', e_.activation(out=ytf, in_=ytf, func=AF.Sigmoid, scale=1.5957691216057308), reads=['ytmp'], writes=['ytmp'])
                dv(e_.tensor_tensor(out=ypf, in0=ypf, in1=ytf, op=ALU.mult), r=['ytmp', 'ypre'], w=['ypre'])
                P.add('act', e_.copy(out=yg_bf.rearrange("p c t -> p (c t)"), in_=ypf), reads=['ypre'], writes=['ygbf'])
                for co in range(3):
                    b = psum()
                    for ci in range(3):
                        P.add('pe', e_.matmul(banks[b][:, :], lhsT=wglu[:, ci, co * 128:(co + 1) * 128], rhs=yg_bf[:, ci, :],
                                                                          start=(ci == 0), stop=(ci == 2)),
                              reads=['wglu', 'ygbf'], writes=[PS(b)])
                    P.add('act', e_.activation(out=ytmp[:, co, :], in_=banks[b][:, :], func=AF.Sigmoid),
                          reads=[PS(b), 'ytmp'], writes=['ytmp'])
                    dv(e_.tensor_tensor(out=ymix[:, co, :], in0=ypre[:, co, :], in1=ytmp[:, co, :], op=ALU.mult),
                       r=['ypre', 'ytmp'], w=[('ym', co)])
                if STOP == 'F':
                    break

                def out_handler(o, b, tt=tt):
                    residual_add(b, o, tt, 1.0)
                proj(wout, KT, wp, lambda k: ymix[:, k, :], lambda k: ('ym', k), TT, out_handler, 'm')
            P.fence()

        for sq_i in range(n_seq):
            for j in range(S // 128):
                s_ = stage_slot()
                P.add('sp', e_.dma_start(out=stage[s_], in_=dr['x'][sq_i, j * 128:(j + 1) * 128, :]),
                      writes=[('st', s_)], chan='st%d' % s_)
                tt = j // 4
                for kh in range(2):
                    b = psum()
                    for kk in range(4):
                        k = kh * 4 + kk
                        P.add('pe', e_.transpose(out=banks[b][:, kk * 128:(kk + 1) * 128],
                                                                                in_=stage[s_][:, k * 128:(k + 1) * 128], identity=ident),
                              reads=[('st', s_), 'cst'], writes=[PS(b)])
                    eng = 'act' if kh == 0 else 'dve'
                    if eng == 'act':
                        fn = e_.copy(out=xres[:, kh * 4:(kh + 1) * 4, j * 128:(j + 1) * 128],
                                                               in_=banks[b][:, :].rearrange("p (k t) -> p k t", k=4))
                    else:
                        fn = e_.tensor_copy(out=xres[:, kh * 4:(kh + 1) * 4, j * 128:(j + 1) * 128],
                                                                      in_=banks[b][:, :].rearrange("p (k t) -> p k t", k=4))
                    P.add(eng, fn, reads=[PS(b)], writes=[('x', kh * 4 + kk, tt) for kk in range(4)])
            P.fence()
            for l in range(n_layers):
                if 'ffn1' in stages:
                    ffn(l, "ffn1")
                if 'mix' in stages:
                    mix(l, sq_i)
                if 'xattn' in stages:
                    xattn(l, sq_i)
                if 'ffn2' in stages:
                    ffn(l, "ffn2")
            cur[0] = D_off
            sqb = [carve(TT, BF16) for _ in range(4)]
            cur[0] = h_off
            yn = carve(KT * TT).rearrange("p (k t) -> p k t", k=KT)
            for tt in range(NTT):
                rmsnorm_tile(tt, "final_norm", 0, lambda k: yn[:, k, :], lambda k: ('yn', k), sqb)
                for jb in range(4):
                    s_ = stage_slot()
                    for kh in range(2):
                        b = psum()
                        for kk in range(4):
                            k = kh * 4 + kk
                            P.add('pe', e_.transpose(out=banks[b][:, kk * 128:(kk + 1) * 128],
                                                                                    in_=yn[:, k, jb * 128:(jb + 1) * 128], identity=ident),
                                  reads=[('yn', k), 'cst'], writes=[PS(b)])
                        if kh == 0:
                            P.add('act', e_.copy(out=stage[s_][:, 0:512], in_=banks[b][:, :]), reads=[PS(b)], writes=[('st', s_)])
                        else:
                            P.add('dve', e_.tensor_copy(out=stage[s_][:, 512:1024], in_=banks[b][:, :]), reads=[PS(b)], writes=[('st', s_)])
                    r0 = tt * TT + jb * 128
                    P.add('sp', e_.dma_start(out=out[sq_i, r0:r0 + 128, :], in_=stage[s_]),
                          reads=[('st', s_)], writes=[('outd', s_)], chan='out', kind='batch')
            P.fence()
        last_out = [o.id for o in P.ops if o.chan == 'out']
        P.add('sp', None, extra_deps=last_out)

        P.finalize()
        with nc.Block() as block:
            engines = {'pe': block.tensor, 'act': block.scalar, 'dve': block.vector, 'pool': block.gpsimd, 'sp': block.sync}
            P.emit(nc, engines, esem, csem)
    return nc


def make_consts():
    c = np.zeros((128, 512), np.float32)
    c[:, 0:128] = np.eye(128, dtype=np.float32)
    c[:, 128:256] = 1.0
    c[:, 256:384] = np.arange(1, 129, dtype=np.float32)[None, :]
    p = np.arange(128)
    c[:, 384] = ((p % 32) < 16).astype(np.float32)
    c[:, 385] = ((p % 32) >= 16).astype(np.float32)
    c[:, 386] = EPS
    for wi, w in enumerate((2, 4, 8, 16)):
        c[:, 400 + wi * 16:400 + (wi + 1) * 16] = 1.0 / np.minimum(np.arange(1, 17), w)[None, :]
    return c


_NC_CACHE = {}


def kernel(**inputs):
    n_cores = 8
    x = np.ascontiguousarray(inputs["x"], dtype=np.float32)
    mem = np.ascontiguousarray(inputs["mem"], dtype=np.float32)
    per = x.shape[0] // n_cores
    if 'nc' not in _NC_CACHE:
        _NC_CACHE['nc'] = build_nc(n_seq=per)
    nc = _NC_CACHE['nc']
    consts = make_consts()
    params = {n: np.ascontiguousarray(inputs[n], dtype=np.float32) for n in PARAM_NAMES}
    in_maps = []
    for c in range(n_cores):
        m = {"x": x[c * per:(c + 1) * per], "mem": mem[c * per:(c + 1) * per], "consts": consts}
        m.update(params)
        in_maps.append(m)
    res = run_bass_kernel_spmd(nc, in_maps, core_ids=list(range(n_cores)))
    return np.concatenate([r["out"] for r in res.results], axis=0)
```

```python
import numpy as np
import concourse.bass as bass
import concourse.mybir as mybir
from concourse.bass_utils import run_bass_kernel_spmd

F32 = mybir.dt.float32
BF16 = mybir.dt.bfloat16
AF = mybir.ActivationFunctionType
ALU = mybir.AluOpType

DEPTH = 2
D = 1024
S = 2048
FF = 2816
MEM = 256
D_SSM, D_POOL, D_CONV = 384, 256, 384
D_IN = 1408
CONVW = 31
EPS = 1e-6
KT = 8
TT = 512
NTT = S // TT
FR = 128
NGP = 12
MAGIC = 12582912.0
ENGS = ['pe', 'act', 'dve', 'pool', 'sp']

PARAM_NAMES = ["ffn1_norm", "ffn1_w_gate", "ffn1_w_up", "ffn1_w_down", "mix_norm", "w_in", "w_out",
               "ssm_lambda_re", "ssm_lambda_im", "ssm_log_dt", "ssm_b_re", "ssm_b_im", "ssm_c_re",
               "ssm_c_im", "ssm_d", "ssm_w_glu", "pool_w", "pool_scale", "conv_w", "conv_b",
               "conv_ln_g", "conv_ln_b", "xattn_norm", "mem_norm", "xattn_wq", "xattn_wk",
               "xattn_wv", "xattn_wo", "ffn2_norm", "ffn2_w_gate", "ffn2_w_up", "ffn2_w_down",
               "final_norm"]


class Op:
    __slots__ = ('id', 'eng', 'fn', 'deps', 'pos', 'chan', 'chan_val', 'sig', 'sig_idx', 'waits', 'bsnap')


class _Rec:
    def __getattr__(self, name):
        return lambda *a, **k: (name, a, k)


e_ = _Rec()


class Prog:
    def __init__(self):
        self.ops = []
        self.last_write = {}
        self.readers = {}
        self.pos = {e: 0 for e in ENGS}
        self.chan_cnt = {}
        self.chan_kind = {}
        self.last_op = {e: None for e in ENGS}

    def add(self, eng, fn, reads=(), writes=(), chan=None, kind='serial', extra_deps=()):
        o = Op()
        o.id = len(self.ops)
        o.eng = eng
        o.fn = fn
        o.pos = self.pos[eng]
        self.pos[eng] += 1
        o.chan = chan
        o.sig = False
        o.sig_idx = 0
        o.chan_val = 0
        if chan is not None:
            self.chan_kind.setdefault(chan, kind)
            self.chan_cnt[chan] = self.chan_cnt.get(chan, 0) + 16
            o.chan_val = self.chan_cnt[chan]
        deps = set(extra_deps)
        for r in reads:
            w = self.last_write.get(r)
            if w is not None:
                deps.add(w)
        for w in writes:
            lw = self.last_write.get(w)
            if lw is not None:
                deps.add(lw)
            rd = self.readers.get(w)
            if rd:
                deps.update(rd.values())
        for w in writes:
            self.last_write[w] = o.id
            self.readers[w] = {}
        for r in reads:
            self.readers.setdefault(r, {})[eng] = o.id
        deps.discard(o.id)
        o.deps = deps
        o.bsnap = None
        for d in deps:
            c = self.ops[d].chan
            if c is not None and self.chan_kind[c] == 'batch':
                if o.bsnap is None:
                    o.bsnap = {}
                o.bsnap[c] = self.chan_cnt[c] - (16 if c == chan else 0)
        self.ops.append(o)
        if fn is not None:
            self.last_op[eng] = o.id
        return o.id

    def chan_sync(self, chan):
        o_id = self.add('sp', None)
        self.ops[o_id].bsnap = {('force', chan): self.chan_cnt.get(chan, 0)}

    def fence(self):
        last = [v for v in self.last_op.values() if v is not None]
        for e in ENGS:
            self.add(e, None, extra_deps=last)

    def finalize(self):
        ops = self.ops
        for o in ops:
            w = {}
            for d in o.deps:
                do = ops[d]
                if do.chan is not None:
                    key = ('c', do.chan)
                    val = do.chan_val if self.chan_kind[do.chan] == 'serial' else o.bsnap[do.chan]
                    w[key] = max(w.get(key, 0), val)
                    continue
                if do.eng == o.eng:
                    if o.eng in ('pe', 'sp'):
                        continue
                key = ('e', do.eng)
                prev = w.get(key)
                if prev is None or ops[prev].pos < do.pos:
                    w[key] = d
            if o.bsnap:
                for bk, bv in o.bsnap.items():
                    if isinstance(bk, tuple) and bk[0] == 'force' and bv > 0:
                        w[('c', bk[1])] = max(w.get(('c', bk[1]), 0), bv)
            o.waits = w
            for key, v in w.items():
                if key[0] == 'e':
                    ops[v].sig = True
        cnt = {e: 0 for e in ENGS}
        for o in ops:
            if o.sig:
                cnt[o.eng] += 1
                o.sig_idx = cnt[o.eng]

    def emit(self, nc, engines, esem, csem):
        ops = self.ops
        for e in ENGS:
            lst = [o for o in ops if o.eng == e]

            def body(eng, lst=lst, e=e):
                waited = {}
                for o in lst:
                    for key, v in o.waits.items():
                        if key[0] == 'c':
                            sem, val = csem[key[1]], v
                        else:
                            sem, val = esem[key[1]], ops[v].sig_idx
                        if waited.get(key, 0) >= val:
                            continue
                        waited[key] = val
                        eng.wait_ge(sem, val)
                    if o.fn is None:
                        assert not o.sig
                        continue
                    name, a, k = o.fn
                    inst = getattr(eng, name)(*a, **k)
                    if o.chan is not None:
                        inst.then_inc(csem[o.chan], 16)
                    elif o.sig:
                        inst.then_inc(esem[e], 1)
            engines[e](body)


def build_nc(n_seq=2, n_layers=DEPTH, stages=('ffn1', 'mix', 'xattn', 'ffn2')):
    nc = bass.Bass("TRN2", target_bir_lowering=False)
    dr = {}
    dr['x'] = nc.dram_tensor("x", [n_seq, S, D], F32, kind="ExternalInput").ap()
    dr['mem'] = nc.dram_tensor("mem", [n_seq, MEM, D], F32, kind="ExternalInput").ap()
    shapes = dict(
        ffn1_norm=[DEPTH, D], ffn1_w_gate=[DEPTH, D, FF], ffn1_w_up=[DEPTH, D, FF], ffn1_w_down=[DEPTH, FF, D],
        mix_norm=[DEPTH, D], w_in=[DEPTH, D, D_IN], w_out=[DEPTH, D, D],
        ssm_lambda_re=[DEPTH, 24, 64], ssm_lambda_im=[DEPTH, 24, 64], ssm_log_dt=[DEPTH, 24],
        ssm_b_re=[DEPTH, 24, 64, 16], ssm_b_im=[DEPTH, 24, 64, 16], ssm_c_re=[DEPTH, 24, 16, 64],
        ssm_c_im=[DEPTH, 24, 16, 64], ssm_d=[DEPTH, 384], ssm_w_glu=[DEPTH, 384, 384],
        pool_w=[DEPTH, 4, 64, 64], pool_scale=[DEPTH, 256], conv_w=[DEPTH, 31, 384], conv_b=[DEPTH, 384],
        conv_ln_g=[DEPTH, 384], conv_ln_b=[DEPTH, 384], xattn_norm=[DEPTH, D], mem_norm=[DEPTH, D],
        xattn_wq=[DEPTH, D, D], xattn_wk=[DEPTH, D, D], xattn_wv=[DEPTH, D, D], xattn_wo=[DEPTH, D, D],
        ffn2_norm=[DEPTH, D], ffn2_w_gate=[DEPTH, D, FF], ffn2_w_up=[DEPTH, D, FF], ffn2_w_down=[DEPTH, FF, D],
        final_norm=[D])
    for n in PARAM_NAMES:
        dr[n] = nc.dram_tensor(n, shapes[n], F32, kind="ExternalInput").ap()
    dr['consts'] = nc.dram_tensor("consts", [128, 512], F32, kind="ExternalInput").ap()
    out = nc.dram_tensor("out", [n_seq, S, D], F32, kind="ExternalOutput").ap()

    P = Prog()
    NW = 53000
    import contextlib
    with contextlib.ExitStack() as es:
        arena = es.enter_context(nc.sbuf_tensor("arena", [128, NW], F32))
        banks = [es.enter_context(nc.psum_tensor("ps%d" % i, [128, 512], F32)) for i in range(8)]
        esem = {e: es.enter_context(nc.semaphore("sem_" + e)) for e in ENGS}
        chans = ['st0', 'st1', 'st2', 'st3', 'par', 'cst', 'out', 'lp0', 'lp1', 'lp2', 'lp3']
        csem = {c: es.enter_context(nc.semaphore("ch_" + c)) for c in chans}

        cur = [0]

        def carve(nelem, dt=F32):
            words = nelem if dt == F32 else (nelem + 1) // 2
            words = (words + 7) // 8 * 8
            off = cur[0]
            cur[0] += words
            assert cur[0] <= NW, "arena overflow %d" % cur[0]
            a = arena[:, off:off + words]
            if dt != F32:
                a = a.bitcast(dt)[:, 0:nelem]
            else:
                a = a[:, 0:nelem]
            return a

        xres = carve(KT * S).rearrange("p (k t) -> p k t", k=KT)
        h_off = cur[0]
        hT = carve(KT * S, BF16).rearrange("p (k t) -> p k t", k=KT)
        h_end = cur[0]
        stage = [carve(1024) for _ in range(4)]
        cst = carve(512)
        ident = cst[:, 0:128]
        ones_f = cst[:, 128:256]
        iota1 = cst[:, 256:384]
        m0 = cst[:, 384:385]
        m1 = cst[:, 385:386]
        epsc = cst[:, 386:387]
        ident_b = carve(128, BF16)
        ones_b = carve(128, BF16)
        pvec = carve(128)
        D_off = cur[0]

        psn = [0]

        def psum():
            b = psn[0] % 8
            psn[0] += 1
            return b

        def PS(b):
            return ('ps', b)

        stn = [0]

        def stage_slot():
            s_ = stn[0] % 4
            stn[0] += 1
            return s_

        P.add('sp', e_.dma_start(out=cst, in_=dr['consts']), writes=['cst'], chan='cst', kind='batch')
        P.add('dve', e_.tensor_copy(out=ident_b, in_=ident), reads=['cst'], writes=['identb'])
        P.add('dve', e_.tensor_copy(out=ones_b, in_=ones_f), reads=['cst'], writes=['onesb'])

        prow = {}
        rows = []
        r = 0
        for n in ["ffn1_norm", "mix_norm", "xattn_norm", "mem_norm", "ffn2_norm"]:
            prow[n] = r
            rows.append((n, r, dr[n].rearrange("l (k p) -> (l k) p", p=128), DEPTH * 8))
            r += DEPTH * 8
        prow["final_norm"] = r
        rows.append(("final_norm", r, dr["final_norm"].rearrange("(k p) -> k p", p=128), 8))
        r += 8
        for n, w in [("ssm_d", 3), ("pool_scale", 2), ("conv_b", 3), ("conv_ln_g", 3), ("conv_ln_b", 3)]:
            prow[n] = r
            rows.append((n, r, dr[n].rearrange("l (k p) -> (l k) p", p=128), DEPTH * w))
            r += DEPTH * w
        NPROW = r
        assert NPROW <= 128
        st_par = stage[3]
        for (n, r0, src, cnt) in rows:
            P.add('sp', e_.dma_start(out=st_par[r0:r0 + cnt, 0:128], in_=src),
                  writes=[('st', 3)], chan='par', kind='batch')
        bpar = psum()
        P.add('pe', e_.transpose(out=banks[bpar][:, 0:NPROW], in_=st_par[0:NPROW, 0:128],
                                          identity=ident[0:NPROW, 0:NPROW]),
              reads=[('st', 3), 'cst'], writes=[PS(bpar)])
        P.add('dve', e_.tensor_copy(out=pvec[:, 0:NPROW], in_=banks[bpar][:, 0:NPROW]),
              reads=[PS(bpar)], writes=['pvec'])

        def pcol(name, l, k, width):
            c = prow[name] + l * width + k
            return pvec[:, c:c + 1]

        def load_piece(src_ap, dst_view_fn, cast_out, cast_keys_w, cast_keys_r=()):
            s_ = stage_slot()
            dst = dst_view_fn(stage[s_])
            P.add('sp', e_.dma_start(out=dst, in_=src_ap), writes=[('st', s_)], chan='st%d' % s_)
            if s_ % 4 == 0:
                P.add('pool', e_.tensor_copy(out=cast_out, in_=dst), reads=[('st', s_)] + list(cast_keys_r),
                      writes=list(cast_keys_w))
            else:
                P.add('act', e_.copy(out=cast_out, in_=dst), reads=[('st', s_)] + list(cast_keys_r),
                      writes=list(cast_keys_w))

        def rmsnorm_tile(tt, gname, l, out_fn, out_keys, sqbuf, src=None, src_keys=None, ncols=TT, gw=8):
            if src is None:
                src = lambda k: xres[:, k, tt * TT:(tt + 1) * TT]
                src_keys = lambda k: ('x', k, tt)
            b = psum()
            for k in range(KT):
                sq = sqbuf[k % len(sqbuf)]
                P.add('act', e_.activation(out=sq[:, 0:ncols], in_=src(k), func=AF.Square),
                      reads=[src_keys(k)], writes=[('sq', id(sqbuf), k % len(sqbuf))])
                P.add('pe', e_.matmul(banks[b][:, 0:ncols], lhsT=ones_b, rhs=sq[:, 0:ncols],
                                                           start=(k == 0), stop=(k == KT - 1)),
                      reads=[('sq', id(sqbuf), k % len(sqbuf)), 'onesb'], writes=[PS(b)])
            P.add('act', e_.activation(out=banks[b][:, 0:ncols], in_=banks[b][:, 0:ncols], func=AF.Sqrt,
                                                bias=epsc, scale=1.0 / D),
                  reads=[PS(b), 'cst'], writes=[PS(b)])
            P.add('dve', e_.reciprocal(out=banks[b][:, 0:ncols], in_=banks[b][:, 0:ncols]),
                  reads=[PS(b)], writes=[PS(b)])
            for k in range(KT):
                P.add('dve', e_.scalar_tensor_tensor(out=out_fn(k), in0=src(k), scalar=pcol(gname, l, k, gw),
                                                                   in1=banks[b][:, 0:ncols], op0=ALU.mult, op1=ALU.mult),
                      reads=[src_keys(k), PS(b), 'pvec'], writes=[out_keys(k)])

        def residual_add(b, k, tt, scale):
            xs = xres[:, k, tt * TT:(tt + 1) * TT]
            P.add('dve', e_.scalar_tensor_tensor(out=xs, in0=banks[b][:, :], scalar=scale, in1=xs,
                                                          op0=ALU.mult, op1=ALU.add),
                  reads=[PS(b), ('x', k, tt)], writes=[('x', k, tt)])

        def proj(Wv, n_out_tiles, wp, rhs_fn, rhs_keys, ncols, handler, tag):
            def load(o):
                buf = o % 2
                load_piece(Wv[:, :, o * 128:(o + 1) * 128],
                           lambda st: st.rearrange("p (k n) -> p k n", k=KT),
                           wp[buf], [('wp', tag, buf)])
            load(0)
            for o in range(n_out_tiles):
                if o + 1 < n_out_tiles:
                    load(o + 1)
                buf = o % 2
                b = psum()
                for k in range(KT):
                    P.add('pe', e_.matmul(banks[b][:, 0:ncols], lhsT=wp[buf][:, k, :],
                                                                      rhs=rhs_fn(k), start=(k == 0), stop=(k == KT - 1)),
                          reads=[('wp', tag, buf), rhs_keys(k)], writes=[PS(b)])
                handler(o, b)

        def proj_res(Wb, wkey, n_out_tiles, rhs_fn, rhs_keys, ncols, handler):
            for o in range(n_out_tiles):
                b = psum()
                for k in range(KT):
                    P.add('pe', e_.matmul(banks[b][:, 0:ncols], lhsT=Wb[:, k, o * 128:(o + 1) * 128],
                                          rhs=rhs_fn(k), start=(k == 0), stop=(k == KT - 1)),
                          reads=[(wkey, k), rhs_keys(k)], writes=[PS(b)])
                handler(o, b)

        def ffn(l, which, prenormed=False, next_norm=None):
            gname = which + "_norm"
            wg = dr[which + "_w_gate"][l].rearrange("(k p) f -> p k f", p=128)
            wu = dr[which + "_w_up"][l].rearrange("(k p) f -> p k f", p=128)
            wd = dr[which + "_w_down"][l].rearrange("(f p) d -> p f d", p=128)
            cur[0] = D_off
            Wg = [carve(KT * 256, BF16).rearrange("p (k n) -> p k n", k=KT) for _ in range(2)]
            Wu = [carve(KT * 256, BF16).rearrange("p (k n) -> p k n", k=KT) for _ in range(2)]
            Wd = [carve(2 * D, BF16).rearrange("p (f n) -> p f n", f=2) for _ in range(2)]
            a_bf = [[carve(TT, BF16) for _ in range(2)] for _ in range(2)]
            sg = [carve(TT) for _ in range(2)]
            sqb = [carve(TT, BF16) for _ in range(4)]
            if not prenormed:
                for tt in range(NTT):
                    rmsnorm_tile(tt, gname, l, lambda k, tt=tt: hT[:, k, tt * TT:(tt + 1) * TT],
                                 lambda k, tt=tt: ('h', k, tt), sqb)
            NCH = FF // 256

            def load_chunk(c):
                buf = c % 2
                for half in range(2):
                    for (W, Wb, nm) in ((wg, Wg, 'wg'), (wu, Wu, 'wu')):
                        load_piece(W[:, half * 4:(half + 1) * 4, c * 256:(c + 1) * 256],
                                   lambda st: st.rearrange("p (k n) -> p k n", k=4),
                                   Wb[buf][:, half * 4:(half + 1) * 4, :], [(nm, buf, half)])
                for fl in range(2):
                    load_piece(wd[:, c * 2 + fl, :], lambda st: st, Wd[buf][:, fl, :], [('wd', buf, fl)])

            pend = [None]
            un = [0]

            def down(c, tt, ab):
                buf = c % 2
                for dm in range(KT):
                    b = psum()
                    for fl in range(2):
                        P.add('pe', e_.matmul(banks[b][:, :], lhsT=Wd[buf][:, fl, dm * 128:(dm + 1) * 128],
                                                                          rhs=a_bf[ab][fl], start=(fl == 0), stop=(fl == 1)),
                              reads=[('wd', buf, fl), ('a', ab, fl)], writes=[PS(b)])
                    residual_add(b, dm, tt, 0.5)
                if c == NCH - 1 and next_norm is not None:
                    rmsnorm_tile(tt, next_norm[1] + "_norm", next_norm[0], lambda k, tt=tt: hT[:, k, tt * TT:(tt + 1) * TT],
                                 lambda k, tt=tt: ('h', k, tt), sqb)

            load_chunk(0)
            for c in range(NCH):
                buf = c % 2
                for tt in range(NTT):
                    ab = un[0] % 2
                    un[0] += 1
                    for fl in range(2):
                        bg = psum()
                        bu = psum()
                        for (Wb, nm, b) in ((Wg, 'wg', bg), (Wu, 'wu', bu)):
                            for k in range(KT):
                                P.add('pe', e_.matmul(
                                    banks[b][:, :], lhsT=Wb[buf][:, k, fl * 128:(fl + 1) * 128],
                                    rhs=hT[:, k, tt * TT:(tt + 1) * TT], start=(k == 0), stop=(k == KT - 1)),
                                    reads=[(nm, buf, k // 4), ('h', k, tt)], writes=[PS(b)])
                        P.add('act', e_.activation(out=sg[fl], in_=banks[bg][:, :], func=AF.Silu),
                              reads=[PS(bg)], writes=[('sg', fl)])
                        P.add('dve', e_.tensor_tensor(out=a_bf[ab][fl], in0=banks[bu][:, :],
                                                                                 in1=sg[fl], op=ALU.mult),
                              reads=[PS(bu), ('sg', fl)], writes=[('a', ab, fl)])
                    if pend[0] is not None:
                        down(*pend[0])
                    pend[0] = (c, tt, ab)
                    if tt == 0 and c + 1 < NCH:
                        load_chunk(c + 1)
            down(*pend[0])
            P.fence()

        def xattn(l, sq_i):
            cur[0] = D_off
            sqb = [carve(TT, BF16) for _ in range(2)]
            Kt = carve(KT * MEM, BF16).rearrange("p (o m) -> p o m", o=KT)
            Vb = carve(2 * D, BF16).rearrange("p (t n) -> p t n", t=2)
            rD = carve(TT)
            hx = carve(KT * TT, BF16).rearrange("p (k t) -> p k t", k=KT)
            Wq_b = carve(KT * D, BF16).rearrange("p (k n) -> p k n", k=KT)
            Wo_b = carve(KT * D, BF16).rearrange("p (k n) -> p k n", k=KT)
            x_off = cur[0]
            wp = [carve(KT * 128, BF16).rearrange("p (k n) -> p k n", k=KT) for _ in range(2)]
            memT = carve(KT * MEM).rearrange("p (k m) -> p k m", k=KT)
            mT = carve(KT * MEM, BF16).rearrange("p (k m) -> p k m", k=KT)
            wvp = [carve(D, BF16) for _ in range(2)]
            x_end = cur[0]
            cur[0] = x_off
            qT = carve(KT * TT, BF16).rearrange("p (k t) -> p k t", k=KT)
            eT = carve(8 * TT, BF16).rearrange("p (j t) -> p j t", j=8)
            oT = carve(KT * TT, BF16).rearrange("p (k t) -> p k t", k=KT)
            cur[0] = max(cur[0], x_end)
            wq = dr["xattn_wq"][l].rearrange("(k p) n -> p k n", p=128)
            wo = dr["xattn_wo"][l].rearrange("(k p) n -> p k n", p=128)
            for mt in range(2):
                s_ = stage_slot()
                P.add('sp', e_.dma_start(out=stage[s_], in_=dr['mem'][sq_i, mt * 128:(mt + 1) * 128, :]),
                      writes=[('st', s_)], chan='st%d' % s_)
                for kh in range(2):
                    b = psum()
                    for kk in range(4):
                        k = kh * 4 + kk
                        P.add('pe', e_.transpose(out=banks[b][:, kk * 128:(kk + 1) * 128],
                                                                                in_=stage[s_][:, k * 128:(k + 1) * 128], identity=ident),
                              reads=[('st', s_), 'cst'], writes=[PS(b)])
                    P.add('act', e_.copy(out=memT[:, kh * 4:(kh + 1) * 4, mt * 128:(mt + 1) * 128],
                                                                     in_=banks[b][:, :].rearrange("p (k m) -> p k m", k=4)),
                          reads=[PS(b)], writes=[('memT', kh)])
            rmsnorm_tile(0, "mem_norm", l, lambda k: mT[:, k, :], lambda k: ('mT', k), sqb,
                         src=lambda k: memT[:, k, :], src_keys=lambda k: ('memT', k // 4), ncols=MEM)
            wk = dr["xattn_wk"][l].rearrange("(k p) n -> p k n", p=128)

            def k_handler(o, b):
                P.add('act', e_.activation(out=Kt[:, o, :], in_=banks[b][:, 0:MEM], func=AF.Copy, scale=0.0625),
                      reads=[PS(b)], writes=[('Kt', o)])
            proj(wk, KT, wp, lambda k: mT[:, k, :], lambda k: ('mT', k), MEM, k_handler, 'x')
            wv = dr["xattn_wv"][l].rearrange("(k p) n -> p k n", p=128)
            vb = [psum() for _ in range(4)]
            for k in range(KT):
                buf = k % 2
                load_piece(wv[:, k, :], lambda st: st, wvp[buf], [('wvp', buf)])
                for mt in range(2):
                    for ch in range(2):
                        b = vb[mt * 2 + ch]
                        P.add('pe', e_.matmul(
                            banks[b][:, :], lhsT=mT[:, k, mt * 128:(mt + 1) * 128], rhs=wvp[buf][:, ch * 512:(ch + 1) * 512],
                            start=(k == 0), stop=(k == KT - 1)),
                            reads=[('mT', k), ('wvp', buf)], writes=[PS(b)])
            for mt in range(2):
                for ch in range(2):
                    b = vb[mt * 2 + ch]
                    P.add('act', e_.copy(out=Vb[:, mt, ch * 512:(ch + 1) * 512], in_=banks[b][:, :]),
                          reads=[PS(b)], writes=[('V', mt)])
            for k in range(KT):
                load_piece(wq[:, k, :], lambda st: st, Wq_b[:, k, :], [('wq', k)])
            for k in range(KT):
                load_piece(wo[:, k, :], lambda st: st, Wo_b[:, k, :], [('wo', k)])
            P.fence()
            rmsnorm_tile(0, "xattn_norm", l, lambda k: hx[:, k, :], lambda k: ('hx', k), sqb)
            for tt in range(NTT):

                def q_handler(o, b):
                    P.add('act', e_.copy(out=qT[:, o, :], in_=banks[b][:, :]), reads=[PS(b)], writes=[('qT', o)])
                proj_res(Wq_b, 'wq', KT, lambda k: hx[:, k, :], lambda k: ('hx', k), TT, q_handler)
                if tt + 1 < NTT:
                    rmsnorm_tile(tt + 1, "xattn_norm", l, lambda k: hx[:, k, :], lambda k: ('hx', k), sqb)
                for h in range(4):
                    for mt in range(2):
                        b = psum()
                        for half in range(2):
                            P.add('pe', e_.matmul(
                                banks[b][:, :], lhsT=Kt[:, h * 2 + half, mt * 128:(mt + 1) * 128], rhs=qT[:, h * 2 + half, :],
                                start=(half == 0), stop=(half == 1)),
                                reads=[('Kt', h * 2 + half), ('qT', h * 2 + half)], writes=[PS(b)])
                        P.add('act', e_.activation(out=eT[:, h * 2 + mt, :], in_=banks[b][:, :], func=AF.Exp),
                              reads=[PS(b)], writes=[('eT', h * 2 + mt)])
                    bd = psum()
                    for mt in range(2):
                        P.add('pe', e_.matmul(banks[bd][:, :], lhsT=ones_b, rhs=eT[:, h * 2 + mt, :],
                                                                          start=(mt == 0), stop=(mt == 1)),
                              reads=[('eT', h * 2 + mt), 'onesb'], writes=[PS(bd)])
                    P.add('dve', e_.reciprocal(out=rD, in_=banks[bd][:, :]), reads=[PS(bd)], writes=['rD'])
                    for dh in range(2):
                        b = psum()
                        for mt in range(2):
                            P.add('pe', e_.matmul(
                                banks[b][:, :], lhsT=Vb[:, mt, h * 256 + dh * 128: h * 256 + (dh + 1) * 128], rhs=eT[:, h * 2 + mt, :],
                                start=(mt == 0), stop=(mt == 1)),
                                reads=[('V', mt), ('eT', h * 2 + mt)], writes=[PS(b)])
                        P.add('dve', e_.tensor_tensor(out=oT[:, h * 2 + dh, :], in0=banks[b][:, :], in1=rD, op=ALU.mult),
                              reads=[PS(b), 'rD'], writes=[('oT', h * 2 + dh)])

                def o_handler(o, b, tt=tt):
                    residual_add(b, o, tt, 1.0)
                proj_res(Wo_b, 'wo', KT, lambda k: oT[:, k, :], lambda k: ('oT', k), TT, o_handler)
                if 'ffn2' in stages:
                    rmsnorm_tile(tt, "ffn2_norm", l, lambda k, tt=tt: hT[:, k, tt * TT:(tt + 1) * TT],
                                 lambda k, tt=tt: ('h', k, tt), sqb)
            P.fence()

        def mix(l, sq_i):
            import os
            STOP = os.environ.get("MIXSTOP", "Z")
            cur[0] = D_off
            wp = [carve(KT * 128, BF16).rearrange("p (k n) -> p k n", k=KT) for _ in range(2)]
            sqb = [carve(TT, BF16) for _ in range(2)]

            def b2(i):
                return arena[:, h_off + i * 1024 + 512:h_off + (i + 1) * 1024]
            sm = b2(7)
            smc = [0]

            def small(n=NGP):
                a = sm[:, smc[0]:smc[0] + n]
                smc[0] += 16
                assert smc[0] <= 512
                return a
            rotc = small(); rots = small(); car_r = small(); car_i = small()
            cr_t = [small() for _ in range(4)]
            magT = carve(NGP * FR).rearrange("p (g t) -> p g t", g=NGP)
            tabc = carve(NGP * FR, BF16).rearrange("p (g t) -> p g t", g=NGP)
            tabs = carve(NGP * FR, BF16).rearrange("p (g t) -> p g t", g=NGP)
            lB = carve(NGP * 2 * 128, BF16).rearrange("p (g c n) -> p g c n", g=NGP, c=2)
            lC = carve(NGP * 3 * 128, BF16).rearrange("p (g c n) -> p g c n", g=NGP, c=3)
            dgd = carve(3 * 128, BF16).rearrange("p (c n) -> p c n", c=3)
            wglu = carve(3 * 384, BF16).rearrange("p (c n) -> p c n", c=3)
            BDp = carve(2 * 128, BF16).rearrange("p (c n) -> p c n", c=2)
            cwT = carve(3 * 32).rearrange("p (c j) -> p c j", c=3)
            dg = carve(CONVW * 128, BF16).rearrange("p (j n) -> p j n", j=CONVW)
            sgc = b2(0); cf2 = b2(1); lnm = b2(2); lnr = b2(3)
            pin = b2(4).bitcast(BF16).rearrange("p (c t) -> p c t", c=2)
            zin = [b2(5)[:, 0:256].bitcast(BF16), b2(5)[:, 256:512].bitcast(BF16)]
            ztmp = [b2(6)[:, 0:256].bitcast(BF16), b2(6)[:, 256:512].bitcast(BF16)]
            work_off = cur[0]
            lam_r = small(); lam_i = small(); dtv = small(); mag = small()
            ar_ = small(); ai_ = small(); zr = small(); zi = small(); fq = small()
            tA = small(); tB = small(); tC = small()
            ph = carve(NGP * FR).rearrange("p (g t) -> p g t", g=NGP)
            ph2 = carve(NGP * FR).rearrange("p (g t) -> p g t", g=NGP)
            Bn_r = carve(NGP * 16).rearrange("p (g k) -> p g k", g=NGP)
            Bn_i = carve(NGP * 16).rearrange("p (g k) -> p g k", g=NGP)
            Bb_r = carve(NGP * 16).rearrange("p (g k) -> p g k", g=NGP)
            Bb_i = carve(NGP * 16).rearrange("p (g k) -> p g k", g=NGP)
            BD = carve(NGP * 2 * 128).rearrange("p (g c k) -> p g c k", g=NGP, c=2)
            Cn = carve(2 * NGP * 64).rearrange("p (c g n) -> p c g n", c=2, g=NGP)
            CD = carve(128)
            st_l = carve(896)
            st_p = carve(256)
            prep_end = cur[0]
            cur[0] = work_off
            u_bf = carve(3 * TT, BF16).rearrange("p (c t) -> p c t", c=3)
            zz = [carve(4 * FR).rearrange("p (g t) -> p g t", g=4) for _ in range(2)]
            pp = [carve(4 * FR, BF16).rearrange("p (g t) -> p g t", g=4) for _ in range(4)]
            zb = [carve(4 * FR, BF16).rearrange("p (g t) -> p g t", g=4) for _ in range(2)]
            ypre = carve(3 * TT).rearrange("p (c t) -> p c t", c=3)
            ytmp = carve(3 * TT).rearrange("p (c t) -> p c t", c=3)
            yg_bf = carve(3 * TT, BF16).rearrange("p (c t) -> p c t", c=3)
            up = carve(2 * (16 + TT)).rearrange("p (c t) -> p c t", c=2)
            sA = carve(16 + TT); sB = carve(16 + TT)
            hc = carve(3 * (32 + TT), BF16).rearrange("p (c t) -> p c t", c=3)
            cf = carve(3 * TT).rearrange("p (c t) -> p c t", c=3)
            cur[0] = max(cur[0], prep_end)
            ymix = hT[:, :, TT:2 * TT]

            ch = 'lp0'
            for c_ in ('lp0', 'lp1', 'lp2', 'lp3'):
                P.chan_sync(c_)
            P.add('sp', e_.dma_start(out=st_l[0:12, 0:128], in_=dr['ssm_lambda_re'][l].rearrange("(g two) p -> g (two p)", two=2)),
                  writes=['stl_a'], chan=ch, kind='batch')
            P.add('sp', e_.dma_start(out=st_l[0:12, 128:256], in_=dr['ssm_lambda_im'][l].rearrange("(g two) p -> g (two p)", two=2)),
                  writes=['stl_b'], chan=ch, kind='batch')
            P.add('sp', e_.dma_start(out=st_l[0:12, 768:770], in_=dr['ssm_log_dt'][l].rearrange("(g two) -> g two", two=2)),
                  writes=['stl_c'], chan=ch, kind='batch')
            P.add('pool', e_.memset(st_l[0:32, 256:640], 0.0), writes=['stl_d'])
            P.add('sp', e_.dma_start(out=st_l[0:31, 256:640], in_=dr['conv_w'][l]), writes=['stl_d'], chan=ch, kind='batch')
            for two in range(2):
                P.add('dve', e_.tensor_scalar(out=st_l[0:12, 640 + two * 64:640 + (two + 1) * 64], in0=ones_f[0:12, 0:64],
                                              scalar1=st_l[0:12, 768 + two:769 + two], scalar2=None, op0=ALU.mult),
                      reads=['stl_c', 'cst'], writes=['stl_e%d' % two])
            slk = ['stl_a', 'stl_b', 'stl_c', 'stl_d', 'stl_e0', 'stl_e1']
            if STOP == 'A1b':
                P.fence()
                return
            bq = psum()
            for i, c0 in enumerate((0, 128, 640)):
                P.add('pe', e_.transpose(out=banks[bq][:, i * 16:i * 16 + 12], in_=st_l[0:12, c0:c0 + 128],
                                         identity=ident[0:12, 0:12]),
                      reads=slk + ['cst'], writes=[PS(bq)])
            for ct in range(3):
                P.add('pe', e_.transpose(out=banks[bq][:, 64 + ct * 32:64 + ct * 32 + 32],
                                                         in_=st_l[0:32, 256 + ct * 128:256 + (ct + 1) * 128], identity=ident[0:32, 0:32]),
                      reads=slk + ['cst'], writes=[PS(bq)])
            if STOP == 'A1c':
                P.add('dve', e_.tensor_copy(out=CD[:, 0:12], in_=banks[bq][:, 0:12]), reads=[PS(bq)], writes=['s5p'])
                P.fence()
                return
            P.add('dve', e_.tensor_copy(out=lam_r, in_=banks[bq][:, 0:12]), reads=[PS(bq)], writes=['s5p'])
            P.add('dve', e_.tensor_copy(out=lam_i, in_=banks[bq][:, 16:28]), reads=[PS(bq)], writes=['s5p'])
            if STOP == 'A1d':
                P.fence()
                return
            P.add('act', e_.activation(out=dtv, in_=banks[bq][:, 32:44], func=AF.Exp), reads=[PS(bq)], writes=['s5p1'])
            if STOP == 'A1e':
                P.fence()
                return
            P.add('dve', e_.tensor_copy(out=cwT.rearrange("p c j -> p (c j)"), in_=banks[bq][:, 64:160]),
                  reads=[PS(bq)], writes=['cwT'])
            if STOP == 'A1':
                P.fence()
                return
            SK = ['s5p', 's5p1', 's5p2']

            def dv(fn, r=SK, w=('s5p2',)):
                P.add('dve', fn, reads=list(r), writes=list(w))
            dv(e_.tensor_tensor(out=tA, in0=lam_r, in1=dtv, op=ALU.mult))
            P.add('act', e_.activation(out=mag, in_=tA, func=AF.Exp), reads=SK, writes=['s5p1'])
            dv(e_.scalar_tensor_tensor(out=fq, in0=lam_i, scalar=float(1.0 / (2 * np.pi)), in1=dtv, op0=ALU.mult, op1=ALU.mult))
            for g in range(NGP):
                dv(e_.tensor_scalar(out=ph[:, g, :], in0=iota1, scalar1=fq[:, g:g + 1], scalar2=None, op0=ALU.mult),
                   r=SK + ['cst', 'tabf'], w=['tabf'])
            phf = ph.rearrange("p g t -> p (g t)")
            ph2f = ph2.rearrange("p g t -> p (g t)")
            dv(e_.tensor_scalar(out=ph2f, in0=phf, scalar1=MAGIC, scalar2=MAGIC, op0=ALU.add, op1=ALU.subtract),
               r=['tabf'], w=['tabf2'])
            dv(e_.tensor_tensor(out=ph2f, in0=phf, in1=ph2f, op=ALU.subtract), r=['tabf', 'tabf2'], w=['tabf2'])
            P.add('act', e_.activation(out=ph2f, in_=ph2f, func=AF.Sin, scale=float(2 * np.pi)), reads=['tabf2'], writes=['tabf2'])
            dv(e_.tensor_copy(out=tabs.rearrange("p g t -> p (g t)"), in_=ph2f), r=['tabf2'], w=['tabs'])
            dv(e_.tensor_copy(out=rots, in_=ph2[:, :, FR - 1]), r=['tabf2'], w=['rot'])
            dv(e_.tensor_copy(out=tB, in_=ph2[:, :, 0]), r=['tabf2'], w=['s5p2'])
            dv(e_.tensor_scalar(out=phf, in0=phf, scalar1=0.25, scalar2=None, op0=ALU.add), r=['tabf'], w=['tabf'])
            dv(e_.tensor_scalar(out=ph2f, in0=phf, scalar1=MAGIC, scalar2=MAGIC, op0=ALU.add, op1=ALU.subtract),
               r=['tabf', 'tabf2', 'tabs', 'rot', 's5p2'], w=['tabf2'])
            dv(e_.tensor_tensor(out=ph2f, in0=phf, in1=ph2f, op=ALU.subtract), r=['tabf', 'tabf2'], w=['tabf2'])
            P.add('act', e_.activation(out=ph2f, in_=ph2f, func=AF.Sin, scale=float(2 * np.pi)), reads=['tabf2'], writes=['tabf2'])
            dv(e_.tensor_copy(out=tabc.rearrange("p g t -> p (g t)"), in_=ph2f), r=['tabf2'], w=['tabc'])
            dv(e_.tensor_copy(out=rotc, in_=ph2[:, :, FR - 1]), r=['tabf2'], w=['rot'])
            dv(e_.tensor_copy(out=tC, in_=ph2[:, :, 0]), r=['tabf2'], w=['s5p2'])
            for g in range(NGP):
                dv(e_.tensor_scalar(out=magT[:, g, :], in0=ones_f, scalar1=mag[:, g:g + 1], scalar2=None, op0=ALU.mult),
                   r=SK + ['cst'], w=['magT'])
            dv(e_.tensor_tensor(out=ar_, in0=mag, in1=tC, op=ALU.mult))
            dv(e_.tensor_tensor(out=ai_, in0=mag, in1=tB, op=ALU.mult))
            dv(e_.tensor_tensor(out=tA, in0=lam_r, in1=lam_r, op=ALU.mult))
            dv(e_.tensor_tensor(out=tB, in0=lam_i, in1=lam_i, op=ALU.mult))
            dv(e_.tensor_tensor(out=tA, in0=tA, in1=tB, op=ALU.add))
            dv(e_.reciprocal(out=tA, in_=tA))
            dv(e_.tensor_scalar(out=tB, in0=ar_, scalar1=-1.0, scalar2=None, op0=ALU.add))
            dv(e_.tensor_tensor(out=zr, in0=tB, in1=lam_r, op=ALU.mult))
            dv(e_.tensor_tensor(out=tC, in0=ai_, in1=lam_i, op=ALU.mult))
            dv(e_.tensor_tensor(out=zr, in0=zr, in1=tC, op=ALU.add))
            dv(e_.tensor_tensor(out=zr, in0=zr, in1=tA, op=ALU.mult))
            dv(e_.tensor_tensor(out=zi, in0=ai_, in1=lam_r, op=ALU.mult))
            dv(e_.tensor_tensor(out=tC, in0=tB, in1=lam_i, op=ALU.mult))
            dv(e_.tensor_tensor(out=zi, in0=zi, in1=tC, op=ALU.subtract))
            dv(e_.tensor_tensor(out=zi, in0=zi, in1=tA, op=ALU.mult))
            if STOP == 'A2':
                P.fence()
                return
            for (src, dstt) in (('ssm_b_re', Bn_r), ('ssm_b_im', Bn_i)):
                for two in range(2):
                    P.add('sp', e_.dma_start(
                        out=dstt[two * 64:(two + 1) * 64, :, :],
                        in_=dr[src][l].rearrange("(g two) p k -> two p g k", two=2)[two]),
                        writes=['Bn'], chan='lp2', kind='batch')
            zrb = ph[:, :, 0:16]
            zib = ph2[:, :, 0:16]
            for g in range(NGP):
                dv(e_.tensor_scalar(out=ph[:, g, 0:16], in0=ones_f[:, 0:16], scalar1=zr[:, g:g + 1], scalar2=None, op0=ALU.mult),
                   r=SK + ['cst', 'tabf', 'tabf2', 'tabs', 'tabc', 'rot'], w=['tabf'])
                dv(e_.tensor_scalar(out=ph2[:, g, 0:16], in0=ones_f[:, 0:16], scalar1=zi[:, g:g + 1], scalar2=None, op0=ALU.mult),
                   r=SK + ['cst', 'tabf', 'tabf2', 'tabs', 'tabc', 'rot'], w=['tabf2'])
            BK = SK + ['Bn', 'Bb', 'tabf', 'tabf2']
            dv(e_.tensor_tensor(out=Bb_r, in0=Bn_r, in1=zrb, op=ALU.mult), r=BK, w=['Bb'])
            dv(e_.tensor_tensor(out=Bb_i, in0=Bn_i, in1=zib, op=ALU.mult), r=BK, w=['Bb'])
            dv(e_.tensor_tensor(out=Bb_r, in0=Bb_r, in1=Bb_i, op=ALU.subtract), r=BK, w=['Bb'])
            dv(e_.tensor_tensor(out=Bb_i, in0=Bn_i, in1=zrb, op=ALU.mult), r=BK, w=['Bb'])
            dv(e_.tensor_tensor(out=Bn_r, in0=Bn_r, in1=zib, op=ALU.mult), r=BK, w=['Bn'])
            dv(e_.tensor_tensor(out=Bb_i, in0=Bb_i, in1=Bn_r, op=ALU.add), r=BK, w=['Bb'])
            dv(e_.memset(BD.rearrange("p g c k -> p (g c k)"), 0.0), r=[], w=['BD'])
            for c, Bb in enumerate((Bb_r, Bb_i)):
                for two in range(2):
                    for j in range(4):
                        c0 = j * 32 + two * 16
                        dv(e_.tensor_copy(out=BD[two * 64:(two + 1) * 64, j::4, c, c0:c0 + 16],
                                          in_=Bb[two * 64:(two + 1) * 64, j::4, :]),
                           r=['Bb', 'BD'], w=['BD'])
            for g in range(NGP):
                bt = psum()
                for c in range(2):
                    P.add('pe', e_.transpose(out=banks[bt][:, c * 128:(c + 1) * 128], in_=BD[:, g, c, :], identity=ident),
                          reads=['BD', 'cst'], writes=[PS(bt)])
                P.add('act', e_.copy(out=lB[:, g, :, :], in_=banks[bt][:, 0:256].rearrange("p (c n) -> p c n", c=2)),
                      reads=[PS(bt)], writes=['lB'])
            if STOP == 'A3':
                P.fence()
                return
            for c, src in enumerate(('ssm_c_re', 'ssm_c_im')):
                P.add('sp', e_.dma_start(out=Cn[0:32, c, :, :],
                                                                in_=dr[src][l].rearrange("(g two) k p -> (two k) g p", two=2)),
                      writes=['Cn'], chan='lp3', kind='batch')
            P.add('pool', e_.memset(lC.rearrange("p g c n -> p (g c n)"), 0.0), writes=['lC'])
            for g in range(NGP):
                j4 = (g % 4) * 32
                for c in range(2):
                    dv(e_.tensor_scalar(out=CD[0:32, 0:64], in0=Cn[0:32, c, g, :], scalar1=m0[0:32, :], scalar2=None, op0=ALU.mult),
                       r=['Cn', 'cst', 'CD'], w=['CD'])
                    dv(e_.tensor_scalar(out=CD[0:32, 64:128], in0=Cn[0:32, c, g, :], scalar1=m1[0:32, :], scalar2=None, op0=ALU.mult),
                       r=['Cn', 'cst', 'CD'], w=['CD'])
                    bt = psum()
                    P.add('pe', e_.transpose(out=banks[bt][:, 0:32], in_=CD[0:32, :], identity=ident[0:32, 0:32]),
                          reads=['CD', 'cst'], writes=[PS(bt)])
                    if c == 0:
                        P.add('act', e_.copy(out=lC[:, g, 0, j4:j4 + 32], in_=banks[bt][:, 0:32]), reads=[PS(bt), 'lC'], writes=['lC'])
                        P.add('act', e_.activation(out=lC[:, g, 1, j4:j4 + 32], in_=banks[bt][:, 0:32], func=AF.Copy, scale=-1.0),
                              reads=[PS(bt), 'lC'], writes=['lC'])
                    else:
                        P.add('act', e_.activation(out=lC[:, g, 2, j4:j4 + 32], in_=banks[bt][:, 0:32], func=AF.Copy, scale=-1.0),
                              reads=[PS(bt), 'lC'], writes=['lC'])
            if STOP == 'A4':
                P.fence()
                return
            for ct in range(3):
                dv(e_.tensor_scalar(out=dgd[:, ct, :], in0=ident, scalar1=pcol('ssm_d', l, ct, 3), scalar2=None, op0=ALU.mult),
                   r=['cst', 'pvec'], w=['dgd'])
                load_piece(dr['ssm_w_glu'][l][ct * 128:(ct + 1) * 128, :], lambda st: st[:, 0:384], wglu[:, ct, :], ['wglu'])
            P.add('pool', e_.memset(st_p[:, 0:256], 0.0), writes=['stp'])
            spks = []
            for g in range(4):
                t_, h_ = g // 2, g % 2
                P.add('sp', e_.dma_start(out=st_p[h_ * 64:(h_ + 1) * 64, t_ * 128 + h_ * 64:t_ * 128 + (h_ + 1) * 64],
                                                                     in_=dr['pool_w'][l, g]),
                      reads=['stp'], writes=['stp%d' % g], chan='lp1', kind='batch')
                spks.append('stp%d' % g)
            P.add('pool', e_.tensor_copy(out=BDp.rearrange("p c n -> p (c n)"), in_=st_p[:, 0:256]), reads=['stp'] + spks, writes=['BDp'])

            P.fence()
            if STOP == 'A':
                return
            dv(e_.memset(car_r, 0.0), r=[], w=['car'])
            dv(e_.memset(car_i, 0.0), r=[], w=['car'])
            P.add('pool', e_.memset(up[:, :, 0:16], 0.0), writes=['up'])
            P.add('pool', e_.memset(hc[:, :, 0:32], 0.0), writes=['hc'])

            win = dr["w_in"][l].rearrange("(k p) n -> p k n", p=128)
            wout = dr["w_out"][l].rearrange("(k p) n -> p k n", p=128)
            for tt in range(NTT):
                rmsnorm_tile(tt, "mix_norm", l, lambda k: hT[:, k, 0:TT], lambda k: ('h', k, 0), sqb)

                vbank = {}

                def in_handler(o, b, tt=tt):
                    if o < 3:
                        P.add('act', e_.copy(out=u_bf[:, o, :], in_=banks[b][:, :]), reads=[PS(b)], writes=[('u', o)])
                    elif o < 5:
                        P.add('act', e_.copy(out=up[:, o - 3, 16:16 + TT], in_=banks[b][:, :]), reads=[PS(b), 'up'], writes=['up'])
                    elif o < 8:
                        P.add('act', e_.copy(out=cf[:, o - 5, :], in_=banks[b][:, :]), reads=[PS(b)], writes=[('cf', o - 5)])
                    else:
                        ct = o - 8
                        P.add('act', e_.activation(out=sgc, in_=banks[b][:, :], func=AF.Sigmoid), reads=[PS(b)], writes=['sgc'])
                        P.add('dve', e_.tensor_tensor(out=hc[:, ct, 32:32 + TT], in0=cf[:, ct, :], in1=sgc, op=ALU.mult),
                              reads=[('cf', ct), 'sgc', 'hc'], writes=['hc'])
                proj(win, 11, wp, lambda k: hT[:, k, 0:TT], lambda k: ('h', k, 0), TT, in_handler, 'm')

                if STOP == 'D':
                    break
                for t_ in range(2):
                    U = up[:, t_, :]
                    NP_ = 16 + TT
                    P.add('pool', e_.tensor_tensor(out=sA[:, 1:NP_], in0=U[:, 1:NP_], in1=U[:, 0:NP_ - 1], op=ALU.add),
                          reads=['up', 'sA'], writes=['sA'])
                    P.add('pool', e_.tensor_tensor(out=sB[:, 3:NP_], in0=sA[:, 3:NP_], in1=sA[:, 1:NP_ - 2], op=ALU.add),
                          reads=['sA', 'sB'], writes=['sB'])
                    if t_ == 0:
                        wins = ((sA, 2, 0), (sB, 4, 1))
                    else:
                        P.add('pool', e_.tensor_tensor(out=sA[:, 7:NP_], in0=sB[:, 7:NP_], in1=sB[:, 3:NP_ - 4], op=ALU.add),
                              reads=['sA', 'sB'], writes=['sA'])
                        P.add('pool', e_.tensor_tensor(out=sB[:, 15:NP_], in0=sA[:, 15:NP_], in1=sA[:, 7:NP_ - 8], op=ALU.add),
                              reads=['sA', 'sB'], writes=['sB'])
                        wins = ((sA, 8, 0), (sB, 16, 1))
                    for (sw, w_, hf) in wins:
                        ps_ = slice(hf * 64, (hf + 1) * 64)
                        dv(e_.scalar_tensor_tensor(
                            out=pin[ps_, t_, :], in0=sw[ps_, 16:16 + TT], scalar=1.0 / w_, in1=U[ps_, 16:16 + TT], op0=ALU.mult, op1=ALU.subtract),
                           r=['sA', 'sB', 'up', 'pin'], w=['pin'])
                        if tt == 0:
                            wi = {2: 0, 4: 1, 8: 2, 16: 3}[w_]
                            itab = cst[ps_, 400 + wi * 16:400 + (wi + 1) * 16]
                            dv(e_.tensor_tensor(out=lnm[ps_, 0:16], in0=sw[ps_, 16:32], in1=itab, op=ALU.mult),
                               r=['sA', 'sB', 'cst', 'lnm'], w=['lnm'])
                            dv(e_.tensor_tensor(out=pin[ps_, t_, 0:16], in0=lnm[ps_, 0:16], in1=U[ps_, 16:32], op=ALU.subtract),
                               r=['lnm', 'up', 'pin'], w=['pin'])
                    b = psum()
                    P.add('pe', e_.matmul(banks[b][:, :], lhsT=BDp[:, t_, :], rhs=pin[:, t_, :], start=True, stop=True),
                          reads=['BDp', 'pin'], writes=[PS(b)])
                    dv(e_.tensor_scalar(out=ymix[:, 3 + t_, :], in0=banks[b][:, :], scalar1=pcol('pool_scale', l, t_, 2), scalar2=None, op0=ALU.mult),
                       r=[PS(b), 'pvec'], w=[('ym', 3 + t_)])
                P.add('pool', e_.tensor_copy(out=up[:, :, 0:16], in_=up[:, :, TT:TT + 16]), reads=['up', 'pin'], writes=['up'])
                if STOP == 'E':
                    break
                for ct in range(3):
                    for j in range(CONVW):
                        dv(e_.tensor_scalar(out=dg[:, j, :], in0=ident, scalar1=cwT[:, ct, j:j + 1], scalar2=None, op0=ALU.mult),
                           r=['cwT', 'cst', 'dg'], w=['dg'])
                    b = psum()
                    for j in range(CONVW):
                        P.add('pe', e_.matmul(banks[b][:, :], lhsT=dg[:, j, :], rhs=hc[:, ct, 2 + j:2 + j + TT],
                                                                        start=(j == 0), stop=(j == CONVW - 1)),
                              reads=['dg', 'hc'], writes=[PS(b)])
                    dv(e_.tensor_scalar(out=cf[:, ct, :], in0=banks[b][:, :], scalar1=pcol('conv_b', l, ct, 3), scalar2=None, op0=ALU.add),
                       r=[PS(b), 'pvec'], w=[('cf', ct)])
                P.add('pool', e_.tensor_copy(out=hc[:, :, 0:32], in_=hc[:, :, TT:TT + 32]), reads=['hc'], writes=['hc'])
                b1 = psum(); b2 = psum()
                cfb = cf2.bitcast(BF16)[:, 0:TT]
                cfq = cf2.bitcast(BF16)[:, TT:2 * TT]
                for ct in range(3):
                    P.add('act', e_.copy(out=cfb, in_=cf[:, ct, :]), reads=[('cf', ct)], writes=['cfb'])
                    P.add('pe', e_.matmul(banks[b1][:, :], lhsT=ones_b, rhs=cfb, start=(ct == 0), stop=(ct == 2)),
                          reads=['cfb', 'onesb'], writes=[PS(b1)])
                    P.add('act', e_.activation(out=cfq, in_=cf[:, ct, :], func=AF.Square), reads=[('cf', ct)], writes=['cfq'])
                    P.add('pe', e_.matmul(banks[b2][:, :], lhsT=ones_b, rhs=cfq, start=(ct == 0), stop=(ct == 2)),
                          reads=['cfq', 'onesb'], writes=[PS(b2)])
                P.add('act', e_.activation(out=lnm, in_=banks[b1][:, :], func=AF.Copy, scale=1.0 / D_CONV), reads=[PS(b1)], writes=['lnm'])
                dv(e_.tensor_tensor(out=lnr, in0=lnm, in1=lnm, op=ALU.mult), r=['lnm'], w=['lnr'])
                dv(e_.scalar_tensor_tensor(out=lnr, in0=banks[b2][:, :], scalar=1.0 / D_CONV, in1=lnr, op0=ALU.mult, op1=ALU.subtract),
                   r=[PS(b2), 'lnr'], w=['lnr'])
                P.add('act', e_.activation(out=lnr, in_=lnr, func=AF.Sqrt, bias=epsc, scale=1.0), reads=['lnr', 'cst'], writes=['lnr'])
                dv(e_.reciprocal(out=lnr, in_=lnr), r=['lnr'], w=['lnr'])
                for ct in range(3):
                    dv(e_.tensor_tensor(out=cf[:, ct, :], in0=cf[:, ct, :], in1=lnm, op=ALU.subtract), r=[('cf', ct), 'lnm'], w=[('cf', ct)])
                    dv(e_.tensor_tensor(out=cf[:, ct, :], in0=cf[:, ct, :], in1=lnr, op=ALU.mult), r=[('cf', ct), 'lnr'], w=[('cf', ct)])
                    dv(e_.tensor_scalar(out=cf[:, ct, :], in0=cf[:, ct, :], scalar1=pcol('conv_ln_g', l, ct, 3),
                                        scalar2=pcol('conv_ln_b', l, ct, 3), op0=ALU.mult, op1=ALU.add),
                       r=[('cf', ct), 'pvec'], w=[('cf', ct)])
                    P.add('act', e_.activation(out=ymix[:, 5 + ct, :], in_=cf[:, ct, :], func=AF.Silu),
                          reads=[('cf', ct)], writes=[('ym', 5 + ct)])
                if STOP == 'B':
                    break
                def emit_B(fr, gq):
                    bR = psum(); bI = psum()
                    for j in range(4):
                        g = gq * 4 + j
                        ct_u = g // 4
                        urows = u_bf[:, ct_u, fr * FR:(fr + 1) * FR]
                        for c, bb in ((0, bR), (1, bI)):
                            P.add('pe', e_.matmul(
                                banks[bb][:, j * FR:(j + 1) * FR], lhsT=lB[:, g, c, :], rhs=urows,
                                start=True, stop=True),
                                reads=['lB', ('u', ct_u)], writes=[PS(bb)])
                    return bR, bI
                NFR = TT // FR
                nextB = emit_B(0, 0)
                for fr in range(NFR):
                    by = psum()
                    for gq in range(3):
                        bR, bI = nextB
                        if gq < 2:
                            nextB = emit_B(fr, gq + 1)
                        elif fr + 1 < NFR:
                            nextB = emit_B(fr + 1, 0)
                        pass
                        tc_ = tabc[:, gq * 4:(gq + 1) * 4, :].rearrange("p g t -> p (g t)")
                        ts_ = tabs[:, gq * 4:(gq + 1) * 4, :].rearrange("p g t -> p (g t)")
                        zr_in, zi_in = zin[0], zin[1]
                        dv(e_.tensor_tensor(out=zr_in, in0=banks[bR][:, :], in1=tc_, op=ALU.mult),
                           r=[PS(bR), 'tabc', 'zin0'], w=['zin0'])
                        dv(e_.tensor_tensor(out=ztmp[0], in0=banks[bI][:, :], in1=ts_, op=ALU.mult),
                           r=[PS(bI), 'tabs', 'zt0'], w=['zt0'])
                        dv(e_.tensor_tensor(out=zi_in, in0=banks[bI][:, :], in1=tc_, op=ALU.mult),
                           r=[PS(bI), 'tabc', 'zin1'], w=['zin1'])
                        dv(e_.tensor_tensor(out=ztmp[1], in0=banks[bR][:, :], in1=ts_, op=ALU.mult),
                           r=[PS(bR), 'tabs', 'zt1'], w=['zt1'])
                        dv(e_.tensor_tensor(out=zr_in, in0=zr_in, in1=ztmp[0], op=ALU.add), r=['zin0', 'zt0'], w=['zin0'])
                        dv(e_.tensor_tensor(out=zi_in, in0=zi_in, in1=ztmp[1], op=ALU.subtract), r=['zin1', 'zt1'], w=['zin1'])
                        for j in range(4):
                            g = gq * 4 + j
                            dv(e_.tensor_tensor_scan(out=zz[0][:, j, :], data0=magT[:, g, :], data1=zr_in[:, j * FR:(j + 1) * FR],
                                                                        initial=car_r[:, g:g + 1], op0=ALU.mult, op1=ALU.add),
                               r=['magT', 'zin0', 'car', 'zz0'], w=['zz0'])
                            dv(e_.tensor_tensor_scan(out=zz[1][:, j, :], data0=magT[:, g, :], data1=zi_in[:, j * FR:(j + 1) * FR],
                                                                        initial=car_i[:, g:g + 1], op0=ALU.mult, op1=ALU.add),
                               r=['magT', 'zin1', 'car', 'zz1'], w=['zz1'])
                        tc3 = tabc[:, gq * 4:(gq + 1) * 4, :]
                        ts3 = tabs[:, gq * 4:(gq + 1) * 4, :]
                        P.add('act', e_.copy(out=zb[0], in_=zz[0]), reads=['zz0', 'zb0'], writes=['zb0'])
                        P.add('act', e_.copy(out=zb[1], in_=zz[1]), reads=['zz1', 'zb1'], writes=['zb1'])
                        dv(e_.tensor_tensor(out=pp[0], in0=zb[0], in1=tc3, op=ALU.mult), r=['zb0', 'tabc', 'pp0'], w=['pp0'])
                        dv(e_.tensor_tensor(out=pp[2], in0=zb[0], in1=ts3, op=ALU.mult), r=['zb0', 'tabs', 'pp2'], w=['pp2'])
                        dv(e_.tensor_tensor(out=pp[1], in0=zb[1], in1=ts3, op=ALU.mult), r=['zb1', 'tabs', 'pp1'], w=['pp1'])
                        dv(e_.tensor_tensor(out=pp[3], in0=zb[1], in1=tc3, op=ALU.mult), r=['zb1', 'tabc', 'pp3'], w=['pp3'])
                        gs = slice(gq * 4, gq * 4 + 4)
                        zrl = zz[0][:, :, FR - 1]
                        zil = zz[1][:, :, FR - 1]
                        P.add('pool', e_.tensor_tensor(out=cr_t[0][:, gs], in0=rotc[:, gs], in1=zrl, op=ALU.mult),
                              reads=['rot', 'zz0', 'crt'], writes=['crt'])
                        P.add('pool', e_.tensor_tensor(out=cr_t[1][:, gs], in0=rots[:, gs], in1=zil, op=ALU.mult),
                              reads=['rot', 'zz1', 'crt'], writes=['crt'])
                        P.add('pool', e_.tensor_tensor(out=cr_t[2][:, gs], in0=rots[:, gs], in1=zrl, op=ALU.mult),
                              reads=['rot', 'zz0', 'crt'], writes=['crt'])
                        P.add('pool', e_.tensor_tensor(out=cr_t[3][:, gs], in0=rotc[:, gs], in1=zil, op=ALU.mult),
                              reads=['rot', 'zz1', 'crt'], writes=['crt'])
                        P.add('pool', e_.tensor_tensor(out=car_r[:, gs], in0=cr_t[0][:, gs], in1=cr_t[1][:, gs], op=ALU.subtract),
                              reads=['crt', 'car'], writes=['car'])
                        P.add('pool', e_.tensor_tensor(out=car_i[:, gs], in0=cr_t[2][:, gs], in1=cr_t[3][:, gs], op=ALU.add),
                              reads=['crt', 'car'], writes=['car'])
                        for j in range(4):
                            g = gq * 4 + j
                            for i, (pi, ci) in enumerate(((0, 0), (1, 1), (2, 2), (3, 2))):
                                P.add('pe', e_.matmul(
                                    banks[by][:, gq * FR:(gq + 1) * FR], lhsT=lC[:, g, ci, :], rhs=pp[pi][:, j, :],
                                    start=(i == 0 and j == 0), stop=False),
                                    reads=['lC', 'pp%d' % pi], writes=[PS(by)])
                        P.add('pe', e_.matmul(banks[by][:, gq * FR:(gq + 1) * FR], lhsT=dgd[:, gq, :],
                                              rhs=u_bf[:, gq, fr * FR:(fr + 1) * FR], start=False, stop=True),
                              reads=['dgd', ('u', gq)], writes=[PS(by)])
                    P.add('act', e_.copy(out=ypre[:, :, fr * FR:(fr + 1) * FR],
                                                               in_=banks[by][:, 0:3 * FR].rearrange("p (c t) -> p c t", c=3)),
                          reads=[PS(by)], writes=['ypre'])
                if STOP == 'C':
                    break
                ypf = ypre.rearrange("p c t -> p (c t)")
                ytf = ytmp.rearrange("p c t -> p (c t)")
                P.add('act', e_.activation(out=ytf, in_=ypf, func=AF.Square), reads=['ypre'], writes=['ytmp'])
                dv(e_.tensor_scalar(out=ytf, in0=ytf, scalar1=0.044715, scalar2=1.0, op0=ALU.mult, op1=ALU.add), r=['ytmp'], w=['ytmp'])
                dv(e_.tensor_tensor(out=ytf, in0=ytf, in1=ypf, op=ALU.mult), r=['ytmp', 'ypre'], w=['ytmp'])
                P.add('act', e_.activation(out=ytf, in_=ytf, func=AF.Sigmoid, scale=1.5957691216057308), reads=['ytmp'], writes=['ytmp'])
                dv(e_.tensor_tensor(out=ypf, in0=ypf, in1=ytf, op=ALU.mult), r=['ytmp', 'ypre'], w=['ypre'])
                P.add('act', e_.copy(out=yg_bf.rearrange("p c t -> p (c t)"), in_=ypf), reads=['ypre'], writes=['ygbf'])
                for co in range(3):
                    b = psum()
                    for ci in range(3):
                        P.add('pe', e_.matmul(banks[b][:, :], lhsT=wglu[:, ci, co * 128:(co + 1) * 128], rhs=yg_bf[:, ci, :],
                                                                          start=(ci == 0), stop=(ci == 2)),
                              reads=['wglu', 'ygbf'], writes=[PS(b)])
                    P.add('act', e_.activation(out=ytmp[:, co, :], in_=banks[b][:, :], func=AF.Sigmoid),
                          reads=[PS(b), 'ytmp'], writes=['ytmp'])
                    dv(e_.tensor_tensor(out=ymix[:, co, :], in0=ypre[:, co, :], in1=ytmp[:, co, :], op=ALU.mult),
                       r=['ypre', 'ytmp'], w=[('ym', co)])
                if STOP == 'F':
                    break

                def out_handler(o, b, tt=tt):
                    residual_add(b, o, tt, 1.0)
                proj(wout, KT, wp, lambda k: ymix[:, k, :], lambda k: ('ym', k), TT, out_handler, 'm')
            P.fence()

        for sq_i in range(n_seq):
            for j in range(S // 128):
                s_ = stage_slot()
                P.add('sp', e_.dma_start(out=stage[s_], in_=dr['x'][sq_i, j * 128:(j + 1) * 128, :]),
                      writes=[('st', s_)], chan='st%d' % s_)
                tt = j // 4
                for kh in range(2):
                    b = psum()
                    for kk in range(4):
                        k = kh * 4 + kk
                        P.add('pe', e_.transpose(out=banks[b][:, kk * 128:(kk + 1) * 128],
                                                                                in_=stage[s_][:, k * 128:(k + 1) * 128], identity=ident),
                              reads=[('st', s_), 'cst'], writes=[PS(b)])
                    eng = 'act' if kh == 0 else 'dve'
                    if eng == 'act':
                        fn = e_.copy(out=xres[:, kh * 4:(kh + 1) * 4, j * 128:(j + 1) * 128],
                                                               in_=banks[b][:, :].rearrange("p (k t) -> p k t", k=4))
                    else:
                        fn = e_.tensor_copy(out=xres[:, kh * 4:(kh + 1) * 4, j * 128:(j + 1) * 128],
                                                                      in_=banks[b][:, :].rearrange("p (k t) -> p k t", k=4))
                    P.add(eng, fn, reads=[PS(b)], writes=[('x', kh * 4 + kk, tt) for kk in range(4)])
            P.fence()
            full = all(st in stages for st in ('ffn1', 'mix', 'xattn', 'ffn2'))
            for l in range(n_layers):
                if 'ffn1' in stages:
                    ffn(l, "ffn1", prenormed=(full and l > 0))
                if 'mix' in stages:
                    mix(l, sq_i)
                if 'xattn' in stages:
                    xattn(l, sq_i)
                if 'ffn2' in stages:
                    ffn(l, "ffn2", prenormed=('xattn' in stages),
                        next_norm=((l + 1, "ffn1") if (full and l + 1 < n_layers) else None))
            cur[0] = D_off
            sqb = [carve(TT, BF16) for _ in range(4)]
            cur[0] = h_off
            yn = carve(KT * TT).rearrange("p (k t) -> p k t", k=KT)
            for tt in range(NTT):
                rmsnorm_tile(tt, "final_norm", 0, lambda k: yn[:, k, :], lambda k: ('yn', k), sqb)
                for jb in range(4):
                    s_ = stage_slot()
                    for kh in range(2):
                        b = psum()
                        for kk in range(4):
                            k = kh * 4 + kk
                            P.add('pe', e_.transpose(out=banks[b][:, kk * 128:(kk + 1) * 128],
                                                                                    in_=yn[:, k, jb * 128:(jb + 1) * 128], identity=ident),
                                  reads=[('yn', k), 'cst'], writes=[PS(b)])
                        if kh == 0:
                            P.add('act', e_.copy(out=stage[s_][:, 0:512], in_=banks[b][:, :]), reads=[PS(b)], writes=[('st', s_)])
                        else:
                            P.add('dve', e_.tensor_copy(out=stage[s_][:, 512:1024], in_=banks[b][:, :]), reads=[PS(b)], writes=[('st', s_)])
                    r0 = tt * TT + jb * 128
                    P.add('sp', e_.dma_start(out=out[sq_i, r0:r0 + 128, :], in_=stage[s_]),
                          reads=[('st', s_)], writes=[('outd', s_)], chan='out', kind='batch')
            P.fence()
        last_out = [o.id for o in P.ops if o.chan == 'out']
        P.add('sp', None, extra_deps=last_out)

        P.finalize()
        with nc.Block() as block:
            engines = {'pe': block.tensor, 'act': block.scalar, 'dve': block.vector, 'pool': block.gpsimd, 'sp': block.sync}
            P.emit(nc, engines, esem, csem)
    return nc


def make_consts():
    c = np.zeros((128, 512), np.float32)
    c[:, 0:128] = np.eye(128, dtype=np.float32)
    c[:, 128:256] = 1.0
    c[:, 256:384] = np.arange(1, 129, dtype=np.float32)[None, :]
    p = np.arange(128)
    c[:, 384] = ((p % 32) < 16).astype(np.float32)
    c[:, 385] = ((p % 32) >= 16).astype(np.float32)
    c[:, 386] = EPS
    for wi, w in enumerate((2, 4, 8, 16)):
        c[:, 400 + wi * 16:400 + (wi + 1) * 16] = 1.0 / np.minimum(np.arange(1, 17), w)[None, :]
    return c


_NC_CACHE = {}


def kernel(**inputs):
    n_cores = 8
    x = np.ascontiguousarray(inputs["x"], dtype=np.float32)
    mem = np.ascontiguousarray(inputs["mem"], dtype=np.float32)
    per = x.shape[0] // n_cores
    if 'nc' not in _NC_CACHE:
        _NC_CACHE['nc'] = build_nc(n_seq=per)
    nc = _NC_CACHE['nc']
    consts = make_consts()
    params = {n: np.ascontiguousarray(inputs[n], dtype=np.float32) for n in PARAM_NAMES}
    in_maps = []
    for c in range(n_cores):
        m = {"x": x[c * per:(c + 1) * per], "mem": mem[c * per:(c + 1) * per], "consts": consts}
        m.update(params)
        in_maps.append(m)
    res = run_bass_kernel_spmd(nc, in_maps, core_ids=list(range(n_cores)))
    return np.concatenate([r["out"] for r in res.results], axis=0)
```
